# Optimizing a Trainium2 kernel written in Bass

```python
import math
import jax, jax.numpy as jnp
from jax import lax
import numpy as np

D_MODEL = 1024
BATCH = 8
SEQ = 4096
DEPTH = 2

CHUNK = 64
Q_BLOCK = 128
RMS_EPS = 1e-6
ROPE_THETA = 500000.0

DIFF_HEADS = 4
DIFF_QK_DIM = 64
DIFF_V_DIM = 2 * DIFF_QK_DIM
DIFF_ROT_DIM = DIFF_QK_DIM // 4

MLA_HEADS = 4
MLA_Q_LORA = 256
MLA_KV_LORA = 128
MLA_NOPE_DIM = 64
MLA_ROPE_DIM = 32
MLA_V_DIM = 64

SB_HEADS = 4
SB_DIM = 64

D_FF = 2816

DIFF_COLS = DIFF_HEADS * (4 * DIFF_QK_DIM + DIFF_V_DIM)
MLA_COLS = MLA_Q_LORA + MLA_KV_LORA + MLA_ROPE_DIM
SB_COLS = 3 * SB_HEADS * SB_DIM
IN_COLS = DIFF_COLS + MLA_COLS + SB_COLS
D_MIX = DIFF_HEADS * DIFF_V_DIM + MLA_HEADS * MLA_V_DIM + SB_HEADS * SB_DIM
MLA_UQ_COLS = MLA_HEADS * (MLA_NOPE_DIM + MLA_ROPE_DIM)
MLA_UKV_COLS = MLA_HEADS * (MLA_NOPE_DIM + MLA_V_DIM)

kernel_name = 'hymba_diff_mla_stickbreaking_macaron'


def rms_norm(x, g):
    xf = x.astype(jnp.float32)
    y = xf * lax.rsqrt(jnp.mean(xf * xf, axis=-1, keepdims=True) + RMS_EPS)
    return (y * g.astype(jnp.float32)).astype(x.dtype)


def swiglu(h, w_gate, w_up, w_down):
    return (jax.nn.silu(h @ w_gate) * (h @ w_up)) @ w_down


def rope_tables(rot_dim, seq):
    inv = ROPE_THETA ** (-jnp.arange(0, rot_dim, 2, dtype=jnp.float32) / rot_dim)
    ang = jnp.arange(seq, dtype=jnp.float32)[:, None] * inv[None, :]
    return jnp.cos(ang), jnp.sin(ang)


def apply_rope(x, cos, sin):
    shape = (1, cos.shape[0]) + (1,) * (x.ndim - 3) + (cos.shape[1],)
    c = cos.reshape(shape)
    s = sin.reshape(shape)
    half = x.shape[-1] // 2
    x1, x2 = x[..., :half], x[..., half:]
    return jnp.concatenate([x1 * c - x2 * s, x2 * c + x1 * s], axis=-1).astype(x.dtype)


def partial_rope(x, cos, sin):
    r = 2 * cos.shape[1]
    return jnp.concatenate([apply_rope(x[..., :r], cos, sin), x[..., r:]], axis=-1)


def sweep_query_blocks(fn, *q_arrays):
    b, s = q_arrays[0].shape[:2]
    nb = s // Q_BLOCK
    def split(a):
        return jnp.moveaxis(a.reshape((b, nb, Q_BLOCK) + a.shape[2:]), 1, 0)
    out = lax.map(lambda args: fn(args[0], *args[1:]),
                  (jnp.arange(nb),) + tuple(split(a) for a in q_arrays))
    out = jnp.moveaxis(out, 0, 1)
    return out.reshape((b, s) + out.shape[3:])


def chunk_causal_mask(block_idx, seq):
    qpos = block_idx * Q_BLOCK + jnp.arange(Q_BLOCK)
    kpos = jnp.arange(seq)
    return (kpos[None, :] // CHUNK) <= (qpos[:, None] // CHUNK)


def diff_attention(q1, q2, k1, k2, v, lam, subln, lambda_init):
    seq = k1.shape[1]
    scale = DIFF_QK_DIM ** -0.5
    def block(bi, q1b, q2b):
        allowed = chunk_causal_mask(bi, seq)
        def probs(qb, k):
            sc = jnp.einsum('bqhd,bkhd->bhqk', qb, k).astype(jnp.float32) * scale
            return jax.nn.softmax(jnp.where(allowed, sc, -jnp.inf), axis=-1)
        p = probs(q1b, k1) - lam * probs(q2b, k2)
        return jnp.einsum('bhqk,bkhd->bqhd', p.astype(v.dtype), v)
    o = sweep_query_blocks(block, q1, q2)
    return rms_norm(o, subln) * (1.0 - lambda_init)


def mla_attention(q, k_nope, k_rope, v):
    seq = k_nope.shape[1]
    scale = (MLA_NOPE_DIM + MLA_ROPE_DIM) ** -0.5
    def block(bi, qb):
        allowed = chunk_causal_mask(bi, seq)
        sc = (jnp.einsum('bqhd,bkhd->bhqk', qb[..., :MLA_NOPE_DIM], k_nope)
              + jnp.einsum('bqhd,bkd->bhqk', qb[..., MLA_NOPE_DIM:], k_rope))
        sc = sc.astype(jnp.float32) * scale
        p = jax.nn.softmax(jnp.where(allowed, sc, -jnp.inf), axis=-1)
        return jnp.einsum('bhqk,bkhd->bqhd', p.astype(v.dtype), v)
    return sweep_query_blocks(block, q)


def stick_breaking_attention(q, k, v):
    seq = k.shape[1]
    scale = SB_DIM ** -0.5
    kpos = jnp.arange(seq)
    def block(bi, qb):
        qpos = bi * Q_BLOCK + jnp.arange(Q_BLOCK)
        causal = kpos[None, :] < qpos[:, None]
        z = jnp.einsum('bqhd,bkhd->bhqk', qb, k).astype(jnp.float32) * scale
        neg_log_fail = jnp.where(causal, jax.nn.softplus(z), 0.0)
        after = lax.cumsum(neg_log_fail, axis=3, reverse=True) - neg_log_fail
        a = jnp.where(causal, jnp.exp(jax.nn.log_sigmoid(z) - after), 0.0)
        return jnp.einsum('bhqk,bkhd->bqhd', a.astype(v.dtype), v)
    return sweep_query_blocks(block, q)


def hybrid_mixer(h, layer_idx, w_in, lam_q1, lam_k1, lam_q2, lam_k2, diff_subln,
                 mla_q_norm, mla_w_uq, mla_kv_norm, mla_w_ukv, w_out,
                 cos_d, sin_d, cos_m, sin_m):
    b, s, _ = h.shape
    proj = h @ w_in
    pa = proj[..., :DIFF_COLS]
    pb = proj[..., DIFF_COLS:DIFF_COLS + MLA_COLS]
    pc = proj[..., DIFF_COLS + MLA_COLS:]

    qk_w = DIFF_HEADS * 2 * DIFF_QK_DIM
    qa = partial_rope(pa[..., :qk_w].reshape(b, s, DIFF_HEADS, 2, DIFF_QK_DIM), cos_d, sin_d)
    ka = partial_rope(pa[..., qk_w:2 * qk_w].reshape(b, s, DIFF_HEADS, 2, DIFF_QK_DIM), cos_d, sin_d)
    va = pa[..., 2 * qk_w:].reshape(b, s, DIFF_HEADS, DIFF_V_DIM)
    lambda_init = 0.8 - 0.6 * math.exp(-0.3 * layer_idx)
    lam = (jnp.exp(jnp.sum(lam_q1.astype(jnp.float32) * lam_k1.astype(jnp.float32)))
           - jnp.exp(jnp.sum(lam_q2.astype(jnp.float32) * lam_k2.astype(jnp.float32)))
           + lambda_init)
    out_a = diff_attention(qa[..., 0, :], qa[..., 1, :], ka[..., 0, :], ka[..., 1, :],
                           va, lam, diff_subln, lambda_init)

    c_q = rms_norm(pb[..., :MLA_Q_LORA], mla_q_norm)
    c_kv = rms_norm(pb[..., MLA_Q_LORA:MLA_Q_LORA + MLA_KV_LORA], mla_kv_norm)
    k_rope = apply_rope(pb[..., MLA_Q_LORA + MLA_KV_LORA:], cos_m, sin_m)
    qb = (c_q @ mla_w_uq).reshape(b, s, MLA_HEADS, MLA_NOPE_DIM + MLA_ROPE_DIM)
    qb = jnp.concatenate([qb[..., :MLA_NOPE_DIM],
                          apply_rope(qb[..., MLA_NOPE_DIM:], cos_m, sin_m)], axis=-1)
    kv = (c_kv @ mla_w_ukv).reshape(b, s, MLA_HEADS, MLA_NOPE_DIM + MLA_V_DIM)
    out_b = mla_attention(qb, kv[..., :MLA_NOPE_DIM], k_rope, kv[..., MLA_NOPE_DIM:])

    sb_w = SB_HEADS * SB_DIM
    qc = pc[..., :sb_w].reshape(b, s, SB_HEADS, SB_DIM)
    kc = pc[..., sb_w:2 * sb_w].reshape(b, s, SB_HEADS, SB_DIM)
    vc = pc[..., 2 * sb_w:].reshape(b, s, SB_HEADS, SB_DIM)
    out_c = stick_breaking_attention(qc, kc, vc)

    merged = jnp.concatenate([out_a.reshape(b, s, -1), out_b.reshape(b, s, -1),
                              out_c.reshape(b, s, -1)], axis=-1)
    return merged @ w_out


def setup_inputs(seed: int = 0):
    key = jax.random.key(seed)
    ks = jax.random.split(key, 24)
    f32 = jnp.float32
    L = DEPTH
    def nrm(k, shape, scale):
        return jax.random.normal(k, shape, f32) * scale
    def gain(k, shape):
        return 1.0 + 0.02 * jax.random.normal(k, shape, f32)
    return {
        'x': nrm(ks[0], (BATCH, SEQ, D_MODEL), 1.0),
        'ffn1_norm': gain(ks[1], (L, D_MODEL)),
        'ffn1_w_gate': nrm(ks[2], (L, D_MODEL, D_FF), D_MODEL ** -0.5),
        'ffn1_w_up': nrm(ks[3], (L, D_MODEL, D_FF), D_MODEL ** -0.5),
        'ffn1_w_down': nrm(ks[4], (L, D_FF, D_MODEL), D_FF ** -0.5),
        'mix_norm': gain(ks[5], (L, D_MODEL)),
        'w_in': nrm(ks[6], (L, D_MODEL, IN_COLS), D_MODEL ** -0.5),
        'diff_lambda_q1': nrm(ks[7], (L, DIFF_QK_DIM), 0.1),
        'diff_lambda_k1': nrm(ks[8], (L, DIFF_QK_DIM), 0.1),
        'diff_lambda_q2': nrm(ks[9], (L, DIFF_QK_DIM), 0.1),
        'diff_lambda_k2': nrm(ks[10], (L, DIFF_QK_DIM), 0.1),
        'diff_subln': gain(ks[11], (L, DIFF_V_DIM)),
        'mla_q_norm': gain(ks[12], (L, MLA_Q_LORA)),
        'mla_w_uq': nrm(ks[13], (L, MLA_Q_LORA, MLA_UQ_COLS), MLA_Q_LORA ** -0.5),
        'mla_kv_norm': gain(ks[14], (L, MLA_KV_LORA)),
        'mla_w_ukv': nrm(ks[15], (L, MLA_KV_LORA, MLA_UKV_COLS), MLA_KV_LORA ** -0.5),
        'w_out': nrm(ks[16], (L, D_MIX, D_MODEL), D_MIX ** -0.5),
        'ffn2_norm': gain(ks[17], (L, D_MODEL)),
        'ffn2_w_gate': nrm(ks[18], (L, D_MODEL, D_FF), D_MODEL ** -0.5),
        'ffn2_w_up': nrm(ks[19], (L, D_MODEL, D_FF), D_MODEL ** -0.5),
        'ffn2_w_down': nrm(ks[20], (L, D_FF, D_MODEL), D_FF ** -0.5),
        'final_norm': gain(ks[21], (D_MODEL,)),
    }


def reference(x, ffn1_norm, ffn1_w_gate, ffn1_w_up, ffn1_w_down, mix_norm, w_in,
              diff_lambda_q1, diff_lambda_k1, diff_lambda_q2, diff_lambda_k2, diff_subln,
              mla_q_norm, mla_w_uq, mla_kv_norm, mla_w_ukv, w_out,
              ffn2_norm, ffn2_w_gate, ffn2_w_up, ffn2_w_down, final_norm):
    seq = x.shape[1]
    cos_d, sin_d = rope_tables(DIFF_ROT_DIM, seq)
    cos_m, sin_m = rope_tables(MLA_ROPE_DIM, seq)
    h = x
    for i in range(DEPTH):
        h = h + 0.5 * swiglu(rms_norm(h, ffn1_norm[i]), ffn1_w_gate[i], ffn1_w_up[i], ffn1_w_down[i])
        h = h + hybrid_mixer(rms_norm(h, mix_norm[i]), i, w_in[i],
                             diff_lambda_q1[i], diff_lambda_k1[i], diff_lambda_q2[i], diff_lambda_k2[i],
                             diff_subln[i], mla_q_norm[i], mla_w_uq[i], mla_kv_norm[i], mla_w_ukv[i],
                             w_out[i], cos_d, sin_d, cos_m, sin_m)
        h = h + 0.5 * swiglu(rms_norm(h, ffn2_norm[i]), ffn2_w_gate[i], ffn2_w_up[i], ffn2_w_down[i])
    return rms_norm(h, final_norm)
```

```python
import math
import numpy as np
import ml_dtypes
import concourse.bass as bass
import concourse.mybir as mybir
from concourse.bass_utils import run_bass_kernel_spmd

F32 = mybir.dt.float32
BF16 = mybir.dt.bfloat16
ALU = mybir.AluOpType
AF = mybir.ActivationFunctionType

S = 4096
D = 1024
T = 512
NT = S // T
DFF = 2816
NFC = DFF // 128
INC = 2720
EPS = 1e-6
NEG = -30000.0
ENG = ('pe', 'act', 'dve', 'pool', 'sp')


class Buf:
    __slots__ = ('ap', 'w', 'r', 'name', 'ex')

    def __init__(self, ap, name=''):
        self.ap = ap
        self.w = None
        self.r = {}
        self.name = name
        self.ex = False


class Prog:
    def __init__(self, nc):
        self.nc = nc
        self.q = {e: [] for e in ENG}
        self.seen = {e: {} for e in ENG}
        self.marked = {e: set() for e in ENG}
        self.dsem = {}
        self.sb_off = 16640
        self.sb_base = 16640
        self.uid = 0
        self.allbufs = []

    def sbt(self, shape, dt):
        per = 1
        for s_ in shape[1:]:
            per *= s_
        nbytes = per * (4 if dt == F32 else 2)
        nbytes = (nbytes + 63) // 64 * 64
        off = self.sb_off
        self.sb_off += nbytes
        assert self.sb_off <= 229376, f"SBUF overflow {self.sb_off}"
        self.uid += 1
        return self.nc.alloc_sbuf_tensor_at(f"sb{self.uid}", list(shape), dt, offset=off)

    def buf(self, ap, name=''):
        b = Buf(ap, name)
        self.allbufs.append(b)
        return b

    def sbuf(self, shape, dt, name=''):
        t = self.sbt(shape, dt)
        return self.buf(t[:], name), t

    def op(self, eng, fn, reads=(), writes=(), dma=None, extra=()):
        deps = set(extra)
        for b in reads:
            if b.w is not None:
                deps.add(b.w)
            if b.ex:
                for k, v in b.r.items():
                    if not (k[0] == 'e' and k[1] == eng):
                        deps.add((k[0], k[1], v))
        for b in writes:
            if b.w is not None:
                deps.add(b.w)
            for k, v in b.r.items():
                deps.add((k[0], k[1], v))
        waits = []
        seen = self.seen[eng]
        best = {}
        for (kind, key, v) in deps:
            if kind == 'e' and key == eng and eng in ('pe', 'sp'):
                continue
            if seen.get((kind, key), -1) >= v:
                continue
            if best.get((kind, key), -1) < v:
                best[(kind, key)] = v
        for (kind, key), v in best.items():
            seen[(kind, key)] = v
            waits.append((kind, key, v))
            if kind == 'e':
                self.marked[key].add(v)
        idx = len(self.q[eng])
        if dma is not None:
            cnt = self.dsem.get(dma, 0) + 16
            self.dsem[dma] = cnt
            tok = ('d', dma, cnt)
        else:
            tok = ('e', eng, idx)
        self.q[eng].append((fn, waits, dma))
        for b in reads:
            k = (tok[0], tok[1])
            if b.r.get(k, -1) < tok[2]:
                b.r[k] = tok[2]
        for b in writes:
            b.w = tok
            b.r = {}
        return tok

    def fence(self):
        toks = []
        for e in ('pe', 'act', 'dve', 'pool'):
            if self.q[e]:
                toks.append(('e', e, len(self.q[e]) - 1))
        for k, v in self.dsem.items():
            toks.append(('d', k, v))
        for e in ENG:
            self.op(e, lambda g: g.nop(), extra=toks)
        for b in self.allbufs:
            b.w = None
            b.r = {}

    def emit(self):
        nc = self.nc
        esem = {e: nc.alloc_semaphore(f"es_{e}") for e in ('pe', 'act', 'dve', 'pool')}
        dsem = {k: nc.alloc_semaphore(f"ds_{k}") for k in self.dsem}
        rank = {}
        for e in ('pe', 'act', 'dve', 'pool'):
            m = sorted(self.marked[e])
            rank[e] = {idx: i + 1 for i, idx in enumerate(m)}
        P = self

        def run(eng_name, g):
            for idx, (fn, waits, dma) in enumerate(P.q[eng_name]):
                for (kind, key, v) in waits:
                    if kind == 'e':
                        g.wait_ge(esem[key], rank[key][v])
                    else:
                        g.wait_ge(dsem[key], v)
                ins = fn(g)
                if dma is not None:
                    ins.then_inc(dsem[dma], 16)
                elif eng_name in rank and idx in rank[eng_name]:
                    ins.then_inc(esem[eng_name], 1)

        with nc.Block() as block:
            @block.tensor
            def _(g):
                run('pe', g)

            @block.scalar
            def _(g):
                run('act', g)

            @block.vector
            def _(g):
                run('dve', g)

            @block.gpsimd
            def _(g):
                run('pool', g)

            @block.sync
            def _(g):
                run('sp', g)


def MM(P, out, lhsT, rhs, start, stop, reads, writes):
    P.op('pe', lambda g, o=out, l=lhsT, r=rhs, s=start, t=stop: g.matmul(o, l, r, start=s, stop=t),
         reads, writes)


def TR(P, out, in_, ident, reads, writes):
    P.op('pe', lambda g, o=out, i=in_, d=ident: g.transpose(o, i, d), reads, writes)


def ACT(P, out, in_, func, reads, writes, bias=None, scale=1.0):
    if bias is None:
        P.op('act', lambda g, o=out, i=in_, f=func, s=scale: g.activation(out=o, in_=i, func=f, scale=s),
             reads, writes)
    else:
        P.op('act', lambda g, o=out, i=in_, f=func, s=scale, b=bias: g.activation(out=o, in_=i, func=f, bias=b, scale=s),
             reads, writes)


def TT(P, eng, out, in0, in1, op, reads, writes):
    P.op(eng, lambda g, o=out, a=in0, b=in1, p=op: g.tensor_tensor(out=o, in0=a, in1=b, op=p), reads, writes)


def STT(P, eng, out, in0, scalar, in1, op0, op1, reads, writes):
    P.op(eng, lambda g, o=out, a=in0, s=scalar, b=in1, p0=op0, p1=op1:
         g.scalar_tensor_tensor(out=o, in0=a, scalar=s, in1=b, op0=p0, op1=p1), reads, writes)


def TS(P, eng, out, in0, s1, op0, reads, writes, s2=None, op1=None):
    if op1 is None:
        P.op(eng, lambda g, o=out, a=in0, s=s1, p=op0: g.tensor_scalar(out=o, in0=a, scalar1=s, scalar2=None, op0=p),
             reads, writes)
    else:
        P.op(eng, lambda g, o=out, a=in0, s=s1, p=op0, t=s2, q=op1:
             g.tensor_scalar(out=o, in0=a, scalar1=s, scalar2=t, op0=p, op1=q), reads, writes)


def CP(P, eng, out, in_, reads, writes):
    if eng == 'act':
        P.op('act', lambda g, o=out, i=in_: g.copy(out=o, in_=i), reads, writes)
    else:
        P.op(eng, lambda g, o=out, i=in_: g.tensor_copy(out=o, in_=i), reads, writes)


def DMA(P, eng, out, in_, sem, reads, writes):
    P.op(eng, lambda g, o=out, i=in_: g.dma_start(out=o, in_=i), reads, writes, dma=sem)


def _rope_np(rot_dim):
    inv = (np.float32(500000.0) ** (-np.arange(0, rot_dim, 2, dtype=np.float32) / np.float32(rot_dim))).astype(np.float32)
    ang = (np.arange(S, dtype=np.float32)[:, None] * inv[None, :]).astype(np.float32)
    return np.cos(ang).astype(np.float32), np.sin(ang).astype(np.float32)


def make_consts():
    bf = ml_dtypes.bfloat16
    c = {}
    c['ident_f'] = np.eye(128, dtype=np.float32)
    cb = np.zeros((16, 128, 128), np.float32)
    cb[0] = np.eye(128)
    cb[1] = 1.0 / 1024
    cb[2] = 1.0 / 256
    cb[3] = 1.0 / 128
    cb[4] = 1.0
    jj = np.arange(128)
    cb[5] = np.where(jj[:, None] >= jj[None, :], -8.0, 0.0)
    cb[6] = -8.0
    pm = np.zeros((128, 128), np.float32)
    for m in range(128):
        d = m % 64
        if d < 8:
            pm[m + 8, m] = 1
        elif d < 16:
            pm[m - 8, m] = 1
    cb[7] = pm
    pm = np.zeros((128, 128), np.float32)
    for m in range(64, 96):
        d = m - 64
        pm[(m + 16) if d < 16 else (m - 16), m] = 1
    cb[8] = pm
    pm = np.zeros((128, 128), np.float32)
    for m in range(32):
        pm[(m + 16) if m < 16 else (m - 16), m] = 1
    cb[9] = pm
    es = np.zeros((128, 128), np.float32)
    for i in range(32):
        es[i, 64 + i] = 1
    cb[10] = es
    oh = np.zeros((128, 128), np.float32)
    oh[:, :64] = 1
    cb[11] = oh
    oh = np.zeros((128, 128), np.float32)
    oh[:, 64:] = 1
    cb[12] = oh
    c['cbf'] = np.ascontiguousarray(cb.transpose(1, 0, 2)).astype(bf)
    mk = np.zeros((128, 8, 512), np.float32)
    j = np.arange(128)[:, None]
    t = np.arange(512)[None, :]
    for o in range(4):
        kp = o * 128 + j
        mk[:, o, :] = np.where((kp // 64) <= (t // 64), 0.0, NEG)
        mk[:, 4 + o, :] = np.where(kp < t, 0.0, NEG)
    c['masks'] = mk.astype(bf)
    cd, sd = _rope_np(16)
    cm, sm = _rope_np(32)
    rt = np.zeros((6, 128, S), np.float32)
    rt[0] = 1.0
    rt[2] = 1.0
    rt[4] = 1.0
    for m in range(128):
        d = m % 64
        if d < 16:
            rt[0, m] = cd[:, d % 8]
            rt[1, m] = -sd[:, d % 8] if d < 8 else sd[:, d % 8]
    for m in range(64, 96):
        d = m - 64
        rt[2, m] = cm[:, d % 16]
        rt[3, m] = -sm[:, d % 16] if d < 16 else sm[:, d % 16]
    for m in range(32):
        rt[4, m] = cm[:, m % 16]
        rt[5, m] = -sm[:, m % 16] if m < 16 else sm[:, m % 16]
    c['rope'] = rt
    return c


GAIN_COLS = 64


def pack_gains(inp):
    g = np.zeros((128, GAIN_COLS), np.float32)

    def put(col, vec):
        n = vec.shape[0] // 128
        g[:, col:col + n] = vec.reshape(n, 128).T
    for l in range(2):
        put(0 + 8 * l, inp['ffn1_norm'][l])
        put(16 + 8 * l, inp['mix_norm'][l])
        put(32 + 8 * l, inp['ffn2_norm'][l])
        put(56 + l, inp['diff_subln'][l])
        put(58 + 2 * l, inp['mla_q_norm'][l])
        put(62 + l, inp['mla_kv_norm'][l])
    put(48, inp['final_norm'])
    return g


def pack_lam(inp):
    a = np.stack([np.stack([inp['diff_lambda_q1'][l], inp['diff_lambda_k1'][l],
                            inp['diff_lambda_q2'][l], inp['diff_lambda_k2'][l]]) for l in range(2)])
    return np.ascontiguousarray(np.broadcast_to(a[None], (128, 2, 4, 64))).astype(np.float32)


def build_program(phases=None, dump=None, dbg=None):
    dbg = dbg or {}
    nc = bass.Bass("TRN2", target_bir_lowering=False)
    _lp = nc.allow_low_precision("bf16 matmul operands, fp32 accumulation")
    _lp.__enter__()
    P = Prog(nc)
    dt = nc.dram_tensor

    def ein(name, shape, dtype=F32):
        return dt(name, list(shape), dtype, kind="ExternalInput").ap()

    x_in = ein("x", [S, D])
    w_gate = [ein("ffn1_w_gate", [2, D, DFF]), ein("ffn2_w_gate", [2, D, DFF])]
    w_up = [ein("ffn1_w_up", [2, D, DFF]), ein("ffn2_w_up", [2, D, DFF])]
    w_down = [ein("ffn1_w_down", [2, DFF, D]), ein("ffn2_w_down", [2, DFF, D])]
    w_in = ein("w_in", [2, D, INC])
    w_uq = ein("mla_w_uq", [2, 256, 384])
    w_ukv = ein("mla_w_ukv", [2, 128, 512])
    w_out = ein("w_out", [2, D, D])
    gains_d = ein("gains", [128, GAIN_COLS])
    lam_d = ein("lam", [128, 2, 4, 64])
    identf_d = ein("ident_f", [128, 128])
    cbf_d = ein("cbf", [128, 16, 128], BF16)
    masks_d = ein("masks", [128, 8, 512], BF16)
    rope_d = ein("rope", [6, 128, S])
    out_d = dt("out", [S, D], F32, kind="ExternalOutput").ap()

    xT_d = dt("xT_s", [8, 128, S], F32).ap()
    mg_d = dt("mg_s", [8, 128, S], BF16).ap()
    WGU = [[dt(f"wgu_{l}_{j}", [11, 128, 2, 8, 256], BF16).ap() for j in range(2)] for l in range(2)]
    WD = [[dt(f"wd_{l}_{j}", [11, 128, 2, D], BF16).ap() for j in range(2)] for l in range(2)]
    WIN = [dt(f"win_{l}", [128, 8, INC], BF16).ap() for l in range(2)]
    WUQ = [dt(f"wuq_{l}", [128, 2, 384], BF16).ap() for l in range(2)]
    WUKV = [dt(f"wukv_{l}", [128, 512], BF16).ap() for l in range(2)]
    WOUT = [dt(f"wout_{l}", [128, 8, D], BF16).ap() for l in range(2)]
    dbuf = {}

    def DB(name):
        if name not in dbuf:
            dbuf[name] = P.buf(None, name)
        return dbuf[name]

    if phases is None:
        phases = ['conv', 'F1_0', 'A_0', 'B_0', 'C_0', 'O_0', 'F2_0', 'F1_1', 'A_1', 'B_1', 'C_1', 'O_1', 'F2_1']

    gains_b, gains_t = P.sbuf([128, GAIN_COLS], F32, 'gains')
    identf_b, identf_t = P.sbuf([128, 128], F32, 'identf')
    cbf_b, cbf_t = P.sbuf([128, 16, 128], BF16, 'cbf')
    bias_b, bias_t = P.sbuf([128, 4], F32, 'bias')
    lamv_b, lamv_t = P.sbuf([128, 8], F32, 'lamv')
    P.sb_base = P.sb_off
    ps = []
    ps_all = nc.alloc_psum_tensor("ps_all", [128, 8, 512], F32)
    for i in range(8):
        ps.append((P.buf(None, f"ps{i}"), ps_all[:, i, :]))
        ps[-1][0].ex = True

    DMA(P, 'sp', gains_t[:], gains_d, 'c_g', [], [gains_b])
    DMA(P, 'sp', identf_t[:], identf_d, 'c_i', [], [identf_b])
    DMA(P, 'sp', cbf_t[:], cbf_d, 'c_c', [], [cbf_b])
    P.op('dve', lambda g: g.memset(bias_t[:, 0:1], EPS), [], [bias_b])
    P.op('dve', lambda g: g.memset(bias_t[:, 1:2], 1.0), [], [bias_b])
    eps_ap = bias_t[:, 0:1]
    one_ap = bias_t[:, 1:2]

    def cb(i, k=128, m=128):
        return cbf_t[0:k, i, 0:m]

    conv_items = []
    conv_pos = [0]
    conv_mark = {}

    def gen_conv_items():
        def item(src, dst, shape, dbuf_name):
            conv_items.append((src, dst, shape, dbuf_name))

        def conv_ffn(l, j):
            for g_ in range(11):
                for m, w in enumerate((w_gate[j], w_up[j])):
                    for kh in range(2):
                        src = w[l].rearrange("(kc p) f -> p kc f", p=128)[:, kh * 4:(kh + 1) * 4, g_ * 256:(g_ + 1) * 256]
                        item(src, WGU[l][j][g_, :, m, kh * 4:(kh + 1) * 4, :], [4, 256], f"wgu{l}{j}_{g_}")
            for g_ in range(11):
                src = w_down[j][l].rearrange("(fc p) d -> p fc d", p=128)[:, 2 * g_:2 * g_ + 2, :]
                item(src, WD[l][j][g_], [2, D], f"wd{l}{j}_{g_}")

        def conv_mixer(l):
            for kc in range(8):
                item(w_in[l][kc * 128:(kc + 1) * 128, :], WIN[l][:, kc, :], [INC], f"win{l}")
            item(w_uq[l].rearrange("(kc p) f -> p kc f", p=128), WUQ[l], [2, 384], f"wm{l}")
            item(w_ukv[l], WUKV[l], [512], f"wm{l}")
            for i in range(4):
                item(w_out[l].rearrange("(kc p) f -> p kc f", p=128)[:, 2 * i:2 * i + 2, :], WOUT[l][:, 2 * i:2 * i + 2, :],
                     [2, D], f"wo{l}")

        conv_ffn(0, 0)
        conv_mark['F1_0'] = len(conv_items)
        conv_mixer(0)
        conv_mark['A_0'] = conv_mark['B_0'] = conv_mark['C_0'] = conv_mark['O_0'] = len(conv_items)
        conv_ffn(0, 1)
        conv_mark['F2_0'] = len(conv_items)
        conv_ffn(1, 0)
        conv_mark['F1_1'] = len(conv_items)
        conv_mixer(1)
        conv_mark['A_1'] = conv_mark['B_1'] = conv_mark['C_1'] = conv_mark['O_1'] = len(conv_items)
        conv_ffn(1, 1)
        conv_mark['F2_1'] = len(conv_items)

    gen_conv_items()

    def conv_setup(nslots, engs, width=2816):
        st32 = [P.sbuf([128, width], F32, f"st32_{s}") for s in range(nslots)]
        st16 = [P.sbuf([128, width], BF16, f"st16_{s}") for s in range(nslots)]
        cn = [0]

        def pull(n):
            for _ in range(n):
                if conv_pos[0] >= len(conv_items):
                    return
                src, dst, shape, dbuf_name = conv_items[conv_pos[0]]
                ne_ = 1
                for d_ in shape:
                    ne_ *= d_
                if ne_ > width:
                    return
                conv_pos[0] += 1
                k = cn[0] % nslots
                e = engs[cn[0] % len(engs)]
                cn[0] += 1
                ne = 1
                for d_ in shape:
                    ne *= d_
                b32, t32 = st32[k]
                b16, t16 = st16[k]
                if len(shape) == 1:
                    v32 = t32[:, 0:ne]
                    v16 = t16[:, 0:ne]
                else:
                    v32 = t32[:, 0:ne].rearrange("p (a b) -> p a b", a=shape[0])
                    v16 = t16[:, 0:ne].rearrange("p (a b) -> p a b", a=shape[0])
                DMA(P, 'sp', v32, src, f"st32_{k}", [], [b32])
                CP(P, e, t16[:, 0:ne], t32[:, 0:ne], [b32], [b16])
                DMA(P, 'sp', dst, v16, f"st16_{k}", [b16], [DB(dbuf_name)])
        return pull

    def conv_ensure(upto):
        if conv_pos[0] >= upto:
            return
        P.fence()
        P.sb_off = P.sb_base
        pull = conv_setup(3, ('dve', 'act', 'dve', 'pool'))
        pull(upto - conv_pos[0])

    def tile_share(total, tt):
        a = total * (tt * (tt + 1) // 2) // 36
        b = total * ((tt + 1) * (tt + 2) // 2) // 36
        return b - a

    def xT_tile_ap(tt):
        return xT_d[:, :, tt * T:(tt + 1) * T].rearrange("c p t -> p c t")

    def alloc_chunks(n, dtype, name):
        t_ = P.sbt([128, n, T], dtype)
        return [P.buf(t_[:, c, :], f"{name}{c}") for c in range(n)], t_

    def norm_tile(src_aps, src_bufs, nch, gcol, out_aps, out_bufs, ones_idx, bank, sqs, rstd, eng='dve'):
        pb, pt = bank
        for c in range(nch):
            sb_, st_ = sqs[c % 2]
            ACT(P, st_[:], src_aps[c], AF.Square, [src_bufs[c]], [sb_])
            MM(P, pt[:], cb(ones_idx), st_[:], c == 0, c == nch - 1, [cbf_b, sb_], [pb])
        rb, rt_ = rstd
        ACT(P, rt_[:], pt[:], AF.Ln, [pb, bias_b], [rb], bias=eps_ap)
        ACT(P, rt_[:], rt_[:], AF.Exp, [rb], [rb], scale=-0.5)
        for c in range(nch):
            STT(P, eng, out_aps[c], src_aps[c], gains_t[:, gcol + c:gcol + c + 1], rt_[:], ALU.mult, ALU.mult,
                [src_bufs[c], gains_b, rb], [out_bufs[c]])

    def phase_ffn(l, j, first, last):
        P.fence()
        P.sb_off = P.sb_base
        gcol = (0 if j == 0 else 32) + 8 * l
        xts = [alloc_chunks(8, F32, f"xt{s}") for s in range(2)]
        hts = [alloc_chunks(8, BF16, f"ht{s}") for s in range(2)]
        HT, HT_t = alloc_chunks(NFC, BF16, "HT")
        sqs = [P.sbuf([128, T], BF16, f"sq{s}") for s in range(2)]
        rstds = [P.sbuf([128, T], F32, f"rstd{s}") for s in range(2)]
        sgs = [P.sbuf([128, T], F32, f"sg{s}") for s in range(2)]
        wgu = [P.sbuf([128, 2, 8, 256], BF16, f"wgu{s}") for s in range(3)]
        wd = [P.sbuf([128, 2, D], BF16, f"wd{s}") for s in range(3)]
        if first:
            xtok = [P.sbuf([128, D], F32, f"xtok{s}") for s in range(2)]
        if last:
            yT, yT_t = alloc_chunks(8, F32, "yT")
            otok = [P.sbuf([128, D], F32, f"otok{s}") for s in range(2)]
        if (l, j) == (0, 0):
            cpull, cper = conv_setup(2, ('pool',)), 2
            ctgt = conv_mark['A_0']
        elif (l, j) == (0, 1):
            cpull, cper = conv_setup(2, ('act',)), 5
            ctgt = conv_mark['A_1']
        elif (l, j) == (1, 0):
            cpull, cper = conv_setup(2, ('act',)), 4
            ctgt = conv_mark['A_1'] + 25
        else:
            cpull, cper, ctgt = None, 0, 0
        psG = [ps[0], ps[1]]
        psU = [ps[2], ps[3]]
        psY = [ps[4], ps[5]]
        psN = ps[6]
        psX = [ps[6], ps[7]]
        gu_n = [0]
        d_n = [0]

        def load_gu(n):
            if n >= NT * 11:
                return
            g_ = n % 11
            b_, t_ = wgu[n % 3]
            DMA(P, 'sp', t_[:], WGU[l][j][g_], f"wgu{n % 3}", [DB(f"wgu{l}{j}_{g_}")], [b_])

        def load_d(n):
            if n >= NT * 11:
                return
            g_ = n % 11
            b_, t_ = wd[n % 3]
            DMA(P, 'sp', t_[:], WD[l][j][g_], f"wd{n % 3}", [DB(f"wd{l}{j}_{g_}")], [b_])

        def load_x(tt):
            bufs, t_ = xts[tt % 2]
            if not first:
                for hf in range(2):
                    DMA(P, 'sp', t_[:, hf * 4:(hf + 1) * 4, :], xT_tile_ap(tt)[:, hf * 4:(hf + 1) * 4, :], f"xt{tt % 2}_{hf}",
                        [DB('xT')], bufs[hf * 4:(hf + 1) * 4])
            else:
                for s in range(4):
                    xb, xt_ = xtok[s % 2]
                    r0 = tt * T + s * 128
                    DMA(P, 'sp', xt_[:], x_in[r0:r0 + 128, :], f"xtok{s % 2}", [], [xb])
                    for hb in range(2):
                        pb, pt = psX[hb]
                        for c4 in range(4):
                            c = hb * 4 + c4
                            TR(P, pt[:, c4 * 128:(c4 + 1) * 128], xt_[:, c * 128:(c + 1) * 128], identf_t[:],
                               [xb, identf_b], [pb])
                        CP(P, 'dve', t_[:, hb * 4:(hb + 1) * 4, s * 128:(s + 1) * 128],
                           pt[:].rearrange("p (c t) -> p c t", c=4), [pb], bufs[hb * 4:(hb + 1) * 4])

        def do_norm(tt):
            bufs, t_ = xts[tt % 2]
            hb, ht_ = hts[tt % 2]
            norm_tile([t_[:, c, :] for c in range(8)], bufs, 8, gcol, [ht_[:, c, :] for c in range(8)], hb,
                      1, psN, sqs, rstds[tt % 2])

        def gateup(tt):
            hb, ht_ = hts[tt % 2]
            for g_ in range(11):
                n = tt * 11 + g_
                load_gu(n + 2)
                wb, wt = wgu[n % 3]
                for fi in range(2):
                    fc = 2 * g_ + fi
                    for m, bank in enumerate((psG[fc % 2], psU[fc % 2])):
                        pb, pt = bank
                        for kc in range(8):
                            MM(P, pt[:], wt[:, m, kc, fi * 128:(fi + 1) * 128], ht_[:, kc, :], kc == 0, kc == 7,
                               [wb, hb[kc]], [pb])
                    sb_, st_ = sgs[fc % 2]
                    ACT(P, st_[:], psG[fc % 2][1][:], AF.Silu, [psG[fc % 2][0]], [sb_])
                    TT(P, 'dve', HT_t[:, fc, :], st_[:], psU[fc % 2][1][:], ALU.mult, [sb_, psU[fc % 2][0]], [HT[fc]])

        def down(tt):
            bufs, t_ = xts[tt % 2]
            for g_ in range(11):
                n = tt * 11 + g_
                load_d(n + 2)
                wb, wt = wd[n % 3]
                for fi in range(2):
                    fc = 2 * g_ + fi
                    for dmc in range(8):
                        pb, pt = ps[dmc]
                        MM(P, pt[:], wt[:, fi, dmc * 128:(dmc + 1) * 128], HT_t[:, fc, :], fc == 0, fc == NFC - 1,
                           [wb, HT[fc]], [pb])
            for dmc in range(8):
                pb, pt = ps[dmc]
                STT(P, 'dve', t_[:, dmc, :], pt[:], 0.5, t_[:, dmc, :], ALU.mult, ALU.add, [pb, bufs[dmc]], [bufs[dmc]])

        def store(tt):
            bufs, t_ = xts[tt % 2]
            if not last:
                for hf in range(2):
                    DMA(P, 'sp', xT_tile_ap(tt)[:, hf * 4:(hf + 1) * 4, :], t_[:, hf * 4:(hf + 1) * 4, :], f"xt{tt % 2}_{hf}",
                        bufs[hf * 4:(hf + 1) * 4], [DB('xT')])
            else:
                norm_tile([t_[:, c, :] for c in range(8)], bufs, 8, 48, [yT_t[:, c, :] for c in range(8)], yT,
                          1, psN, sqs, rstds[tt % 2])
                for s in range(4):
                    ob, ot = otok[s % 2]
                    for hb_ in range(2):
                        pb, pt = psX[hb_]
                        for c4 in range(4):
                            c = hb_ * 4 + c4
                            TR(P, pt[:, c4 * 128:(c4 + 1) * 128], yT_t[:, c, s * 128:(s + 1) * 128], identf_t[:],
                               [yT[c], identf_b], [pb])
                        CP(P, 'dve', ot[:, hb_ * 512:(hb_ + 1) * 512], pt[:], [pb], [ob])
                    r0 = tt * T + s * 128
                    DMA(P, 'sp', out_d[r0:r0 + 128, :], ot[:], f"otok{s % 2}", [ob], [DB('out')])

        load_gu(0)
        load_gu(1)
        load_d(0)
        load_d(1)
        load_x(0)
        do_norm(0)
        for tt in range(NT):
            if tt + 1 < NT:
                load_x(tt + 1)
            if cpull is not None:
                cpull(max(0, min(cper, ctgt - conv_pos[0])))
            gateup(tt)
            if tt + 1 < NT:
                do_norm(tt + 1)
            down(tt)
            store(tt)

    def lam_setup(l):
        li = 0.8 - 0.6 * math.exp(-0.3 * l)
        lb, lt = P.sbuf([128, 4, 64], F32, 'laml')
        pr_b, pr_t = P.sbuf([128, 2, 64], F32, 'lamp')
        sm_b, sm_t = P.sbuf([128, 2], F32, 'lams')
        DMA(P, 'sp', lt[:], lam_d[:, l, :, :], 'laml', [], [lb])
        TT(P, 'dve', pr_t[:, 0, :], lt[:, 0, :], lt[:, 1, :], ALU.mult, [lb], [pr_b])
        TT(P, 'dve', pr_t[:, 1, :], lt[:, 2, :], lt[:, 3, :], ALU.mult, [lb], [pr_b])
        P.op('dve', lambda g: g.reduce_sum(out=sm_t[:], in_=pr_t[:], axis=mybir.AxisListType.X), [pr_b], [sm_b])
        ACT(P, sm_t[:], sm_t[:], AF.Exp, [sm_b], [sm_b])
        TT(P, 'dve', lamv_t[:, 2 * l:2 * l + 1], sm_t[:, 1:2], sm_t[:, 0:1], ALU.subtract, [sm_b], [lamv_b])
        TS(P, 'dve', lamv_t[:, 2 * l:2 * l + 1], lamv_t[:, 2 * l:2 * l + 1], -li, ALU.add, [lamv_b], [lamv_b])
        TS(P, 'dve', lamv_t[:, 4 + l:5 + l], gains_t[:, 56 + l:57 + l], 1.0 - li, ALU.mult, [gains_b, lamv_b], [lamv_b])

    def rope_chunk(src_bank, rows, pm_idx, ppbank, qb, cs_t, cs_b, t1, t2, out_ap, out_bufs):
        pb, pt = src_bank
        qbb, qbt = qb
        sub = dbg.get('sub', 9)
        CP(P, 'act', qbt[0:rows, :], pt[0:rows, :], [pb], [qbb])
        ppb, ppt = ppbank
        if sub < 2:
            return
        MM(P, ppt[0:rows, :], cb(pm_idx, rows, rows), qbt[0:rows, :], True, True, [cbf_b, qbb], [ppb])
        t1b, t1t = t1
        t2b, t2t = t2
        if sub < 3:
            return
        TT(P, 'dve', t1t[0:rows, :], pt[0:rows, :], cs_t[0:rows, 0, :], ALU.mult, [pb, cs_b], [t1b])
        TT(P, 'dve', t2t[0:rows, :], ppt[0:rows, :], cs_t[0:rows, 1, :], ALU.mult, [ppb, cs_b], [t2b])
        TT(P, dbg.get('addeng', 'dve'), out_ap, t1t[0:rows, :], t2t[0:rows, :], ALU.add, [t1b, t2b], out_bufs)

    def phase_A(l):
        P.fence()
        P.sb_off = P.sb_base
        lam_setup(l)
        xt, xt_t = alloc_chunks(8, F32, "xt")
        ht, ht_t = alloc_chunks(8, BF16, "ht")
        sqs = [P.sbuf([128, T], BF16, f"sq{s}") for s in range(2)]
        rstd = P.sbuf([128, T], F32, "rstd")
        _wb, win_t = P.sbuf([128, 8, 1536], BF16, "winA")
        win_p = [P.buf(None, f"winA{i}") for i in range(4)]
        KT = [P.sbuf([128, S], BF16, f"KT{h}") for h in range(4)]
        V_b, V_t = P.sbuf([128, 32, 512], BF16, "V")
        QT = [[P.sbuf([128, T], BF16, f"QT{s}{h}") for h in range(4)] for s in range(2)]
        qbs = [P.sbuf([128, T], BF16, f"qb{s}") for s in range(2)]
        cs = [P.sbuf([128, 2, T], F32, f"cs{s}") for s in range(2)]
        t1s = [P.sbuf([128, T], F32, f"t1{s}") for s in range(2)]
        t2s = [P.sbuf([128, T], F32, f"t2{s}") for s in range(2)]
        PT = [P.sbuf([128, 2, T], BF16, f"PT{s}") for s in range(3)]
        mk_b, mk_t = P.sbuf([128, 4, T], BF16, "mask")
        cpull = conv_setup(2, ('pool',), 2048)
        ctarget = conv_mark['F2_0'] if l == 0 else conv_mark['F2_1']
        ctotal = max(0, ctarget - conv_pos[0])
        o1s = [P.sbuf([128, T], F32, f"o1{s}") for s in range(2)]
        rsb = [P.sbuf([128, T], F32, f"rs{s}") for s in range(4)]
        orstd = P.sbuf([128, T], F32, "orstd")
        mgs = [P.sbuf([128, T], BF16, f"mg{s}") for s in range(2)]
        sacc = [P.sbuf([128, T], F32, f"sacc{s}") for s in range(2)]
        saccb = [P.sbuf([128, T], BF16, f"saccb{s}") for s in range(2)]
        print("phase A sbuf bytes", P.sb_off)
        for hf in range(4):
            DMA(P, 'sp', win_t[:, hf * 2:(hf + 1) * 2, :], WIN[l][:, hf * 2:(hf + 1) * 2, 0:1536], f"winA{hf}", [DB(f"win{l}")], [win_p[hf]])
        DMA(P, 'sp', mk_t[:], masks_d[:, 0:4, :], "mask", [], [mk_b])
        gcol = 16 + 8 * l
        scale = 0.125
        cnt = [0]
        lvl = dbg.get('lvl', 9)
        for tt in range(dbg.get('tiles', NT)):
            c0 = tt * T
            for hf in range(2):
                DMA(P, 'sp', xt_t[:, hf * 4:(hf + 1) * 4, :], xT_tile_ap(tt)[:, hf * 4:(hf + 1) * 4, :], f"xtA{hf}",
                    [DB('xT')], xt[hf * 4:(hf + 1) * 4])
            csb, cst = cs[tt % 2]
            DMA(P, 'sp', cst[:], rope_d[0:2, :, c0:c0 + T].rearrange("a p t -> p a t"), f"cs{tt % 2}", [], [csb])
            norm_tile([xt_t[:, c, :] for c in range(8)], xt, 8, gcol, [ht_t[:, c, :] for c in range(8)], ht,
                      1, ps[6], sqs, rstd)
            if lvl < 1:
                continue
            pend = []
            for kind in range(2):
                for h in range(4):
                    n = cnt[0]
                    cnt[0] += 1
                    bank = ps[6 + n % 2]
                    col = kind * 512 + h * 128
                    for kc in range(8):
                        MM(P, bank[1][:], win_t[:, kc, col:col + 128], ht_t[:, kc, :], kc == 0, kc == 7,
                           [win_p[kc // 2], ht[kc]], [bank[0]])
                    if kind == 0:
                        ob, ot = QT[tt % 2][h]
                        oap = ot[:]
                    else:
                        ob, ot = KT[h]
                        oap = ot[:, c0:c0 + T]
                    if pend:
                        rope_chunk(*pend.pop())
                    pend.append((bank, 128, 7, ps[n % 2], qbs[n % 2], cst, csb, t1s[n % 2], t2s[n % 2], oap, [ob]))
            rope_chunk(*pend.pop())
            if lvl < 2:
                continue
            for s in range(4):
                n = cnt[0]
                cnt[0] += 1
                bank = ps[6 + n % 2]
                for kc in range(8):
                    MM(P, bank[1][:], ht_t[:, kc, s * 128:(s + 1) * 128], win_t[:, kc, 1024:1536], kc == 0, kc == 7,
                       [win_p[kc // 2], ht[kc]], [bank[0]])
                CP(P, 'act' if s % 2 == 0 else 'dve', V_t[:, tt * 4 + s, :], bank[1][:], [bank[0]], [V_b])
            if lvl < 3:
                continue
            cpull(tile_share(ctotal, tt))
            nkb = 4 * (tt + 1)
            pend = [[], [], [], []]
            for h in range(4):
                qb_, qt_ = QT[tt % 2][h]
                kb_, kt_ = KT[h]
                Ob = [ps[4], ps[5]]
                Sm0 = ps[6]
                sab, sat = sacc[h % 2]

                def qk(kb):
                    slot = kb % 2
                    diag = kb >= 4 * tt
                    q0 = 128 * (kb - 4 * tt) if diag else 0
                    for v in range(2):
                        r0 = v * 64
                        bank = ps[2 * slot + v]
                        MM(P, bank[1][:, q0:], kt_[r0:r0 + 64, kb * 128:(kb + 1) * 128], qt_[r0:r0 + 64, q0:], True, not diag,
                           [kb_, qb_], [bank[0]])
                    if diag:
                        for v in range(2):
                            bank = ps[2 * slot + v]
                            MM(P, bank[1][:, q0:], cb(0), mk_t[:, kb - 4 * tt, q0:], False, True, [cbf_b, mk_b], [bank[0]])

                qk(0)
                if nkb > 1:
                    qk(1)
                for kb in range(nkb):
                    slot = kb % 2
                    q0 = 128 * (kb - 4 * tt) if kb >= 4 * tt else 0
                    pb_, pt_ = PT[kb % 3]
                    ACT(P, pt_[:, :, q0:], ps_all[:, 2 * slot:2 * slot + 2, q0:], AF.Exp, [ps[2 * slot][0], ps[2 * slot + 1][0]], [pb_],
                        scale=scale)
                    if kb + 2 < nkb:
                        qk(kb + 2)
                    for v in range(2):
                        MM(P, Ob[v][1][:, q0:], V_t[:, kb, h * 128:(h + 1) * 128], pt_[:, v, q0:], kb == 0, kb == nkb - 1,
                           [V_b, pb_], [Ob[v][0]])
                    MM(P, Sm0[1][:, q0:], cb(4), pt_[:, 0, q0:], kb == 0, kb == nkb - 1, [cbf_b, pb_], [Sm0[0]])
                    if kb == 0:
                        CP(P, 'dve', sat[:], pt_[:, 1, :], [pb_], [sab])
                    else:
                        TT(P, 'dve', sat[:, q0:], sat[:, q0:], pt_[:, 1, q0:], ALU.add, [sab, pb_], [sab])
                    for st_i, kq in enumerate((1, 2, 3, 4)):
                        if kb == min(kq, nkb - 1) and pend[st_i]:
                            pend[st_i].pop(0)()
                if lvl < 4:
                    continue
                o1b, o1t = o1s[h % 2]
                t2b, t2t = t2s[h % 2]
                r0b, r0t = rsb[h % 2]
                r1b, r1t = rsb[2 + h % 2]
                sbb, sbt = saccb[h % 2]
                ACT(P, r0t[:], Sm0[1][:], AF.Ln, [Sm0[0]], [r0b])
                ACT(P, r0t[:], r0t[:], AF.Exp, [r0b], [r0b], scale=-1.0)
                CP(P, 'act', t2t[:], Ob[1][1][:], [Ob[1][0]], [t2b])
                TT(P, 'dve', o1t[:], Ob[0][1][:], r0t[:], ALU.mult, [Ob[0][0], r0b], [o1b])
                CP(P, 'dve', sbt[:], sat[:], [sab], [sbb])

                def S1(h=h, sbb=sbb, sbt=sbt):
                    nb_ = ps[7]
                    MM(P, nb_[1][:], cb(4), sbt[:], True, True, [cbf_b, sbb], [nb_[0]])

                def S2(h=h, o1b=o1b, o1t=o1t, t2b=t2b, t2t=t2t, r1b=r1b, r1t=r1t):
                    nb_ = ps[7]
                    ACT(P, r1t[:], nb_[1][:], AF.Ln, [nb_[0]], [r1b])
                    ACT(P, r1t[:], r1t[:], AF.Exp, [r1b], [r1b], scale=-1.0)
                    TT(P, 'dve', t2t[:], t2t[:], r1t[:], ALU.mult, [t2b, r1b], [t2b])
                    STT(P, 'dve', o1t[:], t2t[:], lamv_t[:, 2 * l:2 * l + 1], o1t[:], ALU.mult, ALU.add,
                        [t2b, lamv_b, o1b], [o1b])

                def S3(h=h, o1b=o1b, o1t=o1t):
                    nb_ = ps[7]
                    sb_, st_ = sqs[h % 2]
                    ACT(P, st_[:], o1t[:], AF.Square, [o1b], [sb_])
                    MM(P, nb_[1][:], cb(3), st_[:], True, True, [cbf_b, sb_], [nb_[0]])

                def S4(h=h, o1b=o1b, o1t=o1t):
                    nb_ = ps[7]
                    mb, mt = mgs[h % 2]
                    orb, ort = orstd
                    ACT(P, ort[:], nb_[1][:], AF.Ln, [nb_[0], bias_b], [orb], bias=eps_ap)
                    ACT(P, ort[:], ort[:], AF.Exp, [orb], [orb], scale=-0.5)
                    STT(P, 'dve', mt[:], o1t[:], lamv_t[:, 4 + l:5 + l], ort[:], ALU.mult, ALU.mult,
                        [o1b, lamv_b, orb], [mb])
                    DMA(P, 'sp', mg_d[h, :, c0:c0 + T], mt[:], f"mg{h % 2}", [mb], [DB('mg')])
                for st_i, f in enumerate((S1, S2, S3, S4)):
                    pend[st_i].append(f)
            for st_i in range(4):
                while pend[st_i]:
                    pend[st_i].pop(0)()
        cpull(max(0, ctarget - conv_pos[0]))

    def phase_B(l):
        P.fence()
        P.sb_off = P.sb_base
        xt, xt_t = alloc_chunks(8, F32, "xt")
        ht, ht_t = alloc_chunks(8, BF16, "ht")
        sqs = [P.sbuf([128, T], BF16, f"sq{s}") for s in range(2)]
        rstd = P.sbuf([128, T], F32, "rstd")
        rstd2 = P.sbuf([128, T], F32, "rstd2")
        rstd3 = P.sbuf([128, T], F32, "rstd3")
        _wb, win_t = P.sbuf([128, 8, 416], BF16, "winB")
        win_p = [P.buf(None, f"winB{i}") for i in range(2)]
        wuq_b, wuq_t = P.sbuf([128, 2, 384], BF16, "wuq")
        wukv_b, wukv_t = P.sbuf([128, 512], BF16, "wukv")
        wkp_b, wkp_t = P.sbuf([128, 4, 96], BF16, "wkp")
        cqn, cqn_t = alloc_chunks(2, BF16, "cqn")
        ckvn_b, ckvn_t = P.sbuf([128, T], BF16, "ckvn")
        krr_b, krr_t = P.sbuf([128, T], BF16, "krr")
        KT = [P.sbuf([128, S], BF16, f"KT{h}") for h in range(4)]
        Vp_b, Vp_t = P.sbuf([128, 32, 4, 128], BF16, "Vp")
        QT = [[P.sbuf([128, T], BF16, f"QT{s}{h}") for h in range(4)] for s in range(2)]
        qbs = [P.sbuf([128, T], BF16, f"qb{s}") for s in range(2)]
        cs = [P.sbuf([128, 4, T], F32, f"cs{s}") for s in range(2)]
        t1s = [P.sbuf([128, T], F32, f"t1{s}") for s in range(2)]
        t2s = [P.sbuf([128, T], F32, f"t2{s}") for s in range(2)]
        PT = [P.sbuf([128, T], BF16, f"PT{s}") for s in range(4)]
        mk_b, mk_t = P.sbuf([128, 4, T], BF16, "mask")
        rsb = [P.sbuf([128, T], F32, f"rs{s}") for s in range(2)]
        mgs = [P.sbuf([128, T], BF16, f"mg{s}") for s in range(2)]
        for hf in range(2):
            DMA(P, 'sp', win_t[:, hf * 4:(hf + 1) * 4, :], WIN[l][:, hf * 4:(hf + 1) * 4, 1536:1952], f"winB{hf}", [DB(f"win{l}")], [win_p[hf]])
        DMA(P, 'sp', wuq_t[:], WUQ[l], "wuq", [DB(f"wm{l}")], [wuq_b])
        DMA(P, 'sp', wukv_t[:], WUKV[l], "wukv", [DB(f"wm{l}")], [wukv_b])
        DMA(P, 'sp', mk_t[:], masks_d[:, 0:4, :], "mask", [], [mk_b])
        P.op('pool', lambda g: g.memset(Vp_t[:], 0.0), [], [Vp_b])
        P.op('pool', lambda g: g.memset(wkp_t[:], 0.0), [], [wkp_b])
        for h in range(4):
            CP(P, 'dve', wkp_t[:, h, 0:64], wukv_t[:, h * 128:h * 128 + 64], [wukv_b], [wkp_b])
        gcol = 16 + 8 * l
        scale = 96 ** -0.5
        cnt = [0]
        for tt in range(NT):
            c0 = tt * T
            for hf in range(2):
                DMA(P, 'sp', xt_t[:, hf * 4:(hf + 1) * 4, :], xT_tile_ap(tt)[:, hf * 4:(hf + 1) * 4, :], f"xtA{hf}",
                    [DB('xT')], xt[hf * 4:(hf + 1) * 4])
            csb, cst = cs[tt % 2]
            DMA(P, 'sp', cst[:], rope_d[2:6, :, c0:c0 + T].rearrange("a p t -> p a t"), f"cs{tt % 2}", [], [csb])
            norm_tile([xt_t[:, c, :] for c in range(8)], xt, 8, gcol, [ht_t[:, c, :] for c in range(8)], ht,
                      1, ps[6], sqs, rstd)
            for c in range(2):
                for kc in range(8):
                    MM(P, ps[4 + c][1][:], win_t[:, kc, c * 128:(c + 1) * 128], ht_t[:, kc, :], kc == 0, kc == 7,
                       [win_p[kc // 4], ht[kc]], [ps[4 + c][0]])
            norm_tile([ps[4][1][:], ps[5][1][:]], [ps[4][0], ps[5][0]], 2, 58 + 2 * l,
                      [cqn_t[:, 0, :], cqn_t[:, 1, :]], cqn, 2, ps[6], sqs, rstd2)
            for kc in range(8):
                MM(P, ps[7][1][:], win_t[:, kc, 256:384], ht_t[:, kc, :], kc == 0, kc == 7, [win_p[kc // 4], ht[kc]], [ps[7][0]])
            norm_tile([ps[7][1][:]], [ps[7][0]], 1, 62 + l, [ckvn_t[:]], [ckvn_b], 3, ps[6], sqs, rstd3)
            for kc in range(8):
                MM(P, ps[4][1][0:32, :], win_t[:, kc, 384:416], ht_t[:, kc, :], kc == 0, kc == 7, [win_p[kc // 4], ht[kc]], [ps[4][0]])
            cs_kr = cst[:, 2:4, :]
            rope_chunk(ps[4], 32, 9, ps[5], qbs[0], cs_kr, csb, t1s[0], t2s[0], krr_t[0:32, :], [krr_b])
            for h in range(4):
                n = cnt[0]
                cnt[0] += 1
                bank = ps[6 + n % 2]
                for kc in range(2):
                    MM(P, bank[1][0:96, :], wuq_t[:, kc, h * 96:(h + 1) * 96], cqn_t[:, kc, :], kc == 0, kc == 1,
                       [wuq_b, cqn[kc]], [bank[0]])
                qb_, qt_ = QT[tt % 2][h]
                rope_chunk(bank, 96, 8, ps[4 + n % 2], qbs[n % 2], cst[:, 0:2, :], csb, t1s[n % 2], t2s[n % 2],
                           qt_[0:96, :], [qb_])
                kbank = ps[n % 2]
                MM(P, kbank[1][0:96, :], wkp_t[:, h, :], ckvn_t[:], True, False, [wkp_b, ckvn_b], [kbank[0]])
                MM(P, kbank[1][0:96, :], cb(10, 32, 96), krr_t[0:32, :], False, True, [cbf_b, krr_b], [kbank[0]])
                kb_, kt_ = KT[h]
                CP(P, 'act', kt_[0:96, c0:c0 + T], kbank[1][0:96, :], [kbank[0]], [kb_])
            for s in range(4):
                n = cnt[0]
                cnt[0] += 1
                bank = ps[6 + n % 2]
                rhs = wukv_t[:].rearrange("p (h c) -> p h c", h=4)[:, :, 64:128]
                MM(P, bank[1][:, 0:256].rearrange("p (h c) -> p h c", h=4), ckvn_t[:, s * 128:(s + 1) * 128], rhs, True, True,
                   [wukv_b, ckvn_b], [bank[0]])
                for h in range(4):
                    CP(P, 'dve' if h % 2 == 0 else 'act', Vp_t[:, tt * 4 + s, h, (h % 2) * 64:(h % 2) * 64 + 64],
                       bank[1][:, h * 64:(h + 1) * 64], [bank[0]], [Vp_b])
            nkb = 4 * (tt + 1)
            for pr in range(2):
                Ob, Ot = ps[4 + pr]
                Sb_, St_ = ps[6 + pr]
                steps = [(kb, hh) for kb in range(nkb) for hh in range(2)]

                def qk(i):
                    kb, hh = steps[i]
                    h = 2 * pr + hh
                    sbk = ps[i % 4]
                    diag = kb >= 4 * tt
                    MM(P, sbk[1][:], KT[h][1][0:96, kb * 128:(kb + 1) * 128], QT[tt % 2][h][1][0:96, :], True, not diag,
                       [KT[h][0], QT[tt % 2][h][0]], [sbk[0]])
                    if diag:
                        MM(P, sbk[1][:], cb(0), mk_t[:, kb - 4 * tt, :], False, True, [cbf_b, mk_b], [sbk[0]])

                qk(0)
                qk(1)
                for i in range(len(steps)):
                    if i + 2 < len(steps):
                        qk(i + 2)
                    kb, hh = steps[i]
                    h = 2 * pr + hh
                    sbk = ps[i % 4]
                    pb_, pt_ = PT[i % 4]
                    ACT(P, pt_[:], sbk[1][:], AF.Exp, [sbk[0]], [pb_], scale=scale)
                    MM(P, Ot[:], Vp_t[:, kb, h, :], pt_[:], i == 0, i == len(steps) - 1, [Vp_b, pb_], [Ob])
                    MM(P, St_[:], cb(11 + hh), pt_[:], i == 0, i == len(steps) - 1, [cbf_b, pb_], [Sb_])
                rb_, rt_ = rsb[pr]
                ACT(P, rt_[:], St_[:], AF.Ln, [Sb_], [rb_])
                ACT(P, rt_[:], rt_[:], AF.Exp, [rb_], [rb_], scale=-1.0)
                mb, mt = mgs[pr]
                TT(P, 'dve', mt[:], Ot[:], rt_[:], ALU.mult, [Ob, rb_], [mb])
                DMA(P, 'sp', mg_d[4 + pr, :, c0:c0 + T], mt[:], f"mg{pr}", [mb], [DB('mg')])

    def phase_C(l):
        P.fence()
        P.sb_off = P.sb_base
        xt, xt_t = alloc_chunks(8, F32, "xt")
        ht, ht_t = alloc_chunks(8, BF16, "ht")
        sqs = [P.sbuf([128, T], BF16, f"sq{s}") for s in range(2)]
        rstd = P.sbuf([128, T], F32, "rstd")
        _wb, win_t = P.sbuf([128, 8, 768], BF16, "winC")
        win_p = [P.buf(None, f"winC{i}") for i in range(2)]
        KT = [P.sbuf([128, S], BF16, f"KT{p}") for p in range(2)]
        Vp_b, Vp_t = P.sbuf([128, 32, 4, 128], BF16, "Vp")
        QT = [[P.sbuf([128, T], BF16, f"QT{s}{p}") for p in range(2)] for s in range(2)]
        mk_b, mk_t = P.sbuf([128, 4, T], BF16, "mask")
        e32 = [P.sbuf([128, 2, T], F32, f"e32{s}") for s in range(2)]
        spb = [P.sbuf([128, 2, T], BF16, f"sp{s}") for s in range(3)]
        raccs = [P.sbuf([128, 2, T], BF16, f"racc{s}") for s in range(2)]
        PT = [P.sbuf([128, 2, T], BF16, f"PT{s}") for s in range(3)]
        mgs = [P.sbuf([128, T], BF16, f"mg{s}") for s in range(2)]
        for hf in range(2):
            DMA(P, 'sp', win_t[:, hf * 4:(hf + 1) * 4, :], WIN[l][:, hf * 4:(hf + 1) * 4, 1952:2720], f"winC{hf}", [DB(f"win{l}")], [win_p[hf]])
        DMA(P, 'sp', mk_t[:], masks_d[:, 4:8, :], "mask", [], [mk_b])
        P.op('pool', lambda g: g.memset(Vp_t[:], 0.0), [], [Vp_b])
        cpull = conv_setup(2, ('pool',))
        ctarget = (conv_mark['F2_0'] + 30) if l == 0 else 0
        ctotal = max(0, ctarget - conv_pos[0])
        gcol = 16 + 8 * l
        scale = 0.125
        cnt = [0]
        for tt in range(NT):
            c0 = tt * T
            for hf in range(2):
                DMA(P, 'sp', xt_t[:, hf * 4:(hf + 1) * 4, :], xT_tile_ap(tt)[:, hf * 4:(hf + 1) * 4, :], f"xtA{hf}",
                    [DB('xT')], xt[hf * 4:(hf + 1) * 4])
            norm_tile([xt_t[:, c, :] for c in range(8)], xt, 8, gcol, [ht_t[:, c, :] for c in range(8)], ht,
                      1, ps[6], sqs, rstd)
            for kind in range(2):
                for pr in range(2):
                    n = cnt[0]
                    cnt[0] += 1
                    bank = ps[6 + n % 2]
                    col = kind * 256 + pr * 128
                    for kc in range(8):
                        MM(P, bank[1][:], win_t[:, kc, col:col + 128], ht_t[:, kc, :], kc == 0, kc == 7,
                           [win_p[kc // 4], ht[kc]], [bank[0]])
                    if kind == 0:
                        CP(P, 'act', QT[tt % 2][pr][1][:], bank[1][:], [bank[0]], [QT[tt % 2][pr][0]])
                    else:
                        CP(P, 'dve', KT[pr][1][:, c0:c0 + T], bank[1][:], [bank[0]], [KT[pr][0]])
            for s in range(4):
                n = cnt[0]
                cnt[0] += 1
                bank = ps[6 + n % 2]
                for kc in range(8):
                    MM(P, bank[1][:, 0:256], ht_t[:, kc, s * 128:(s + 1) * 128], win_t[:, kc, 512:768], kc == 0, kc == 7,
                       [win_p[kc // 4], ht[kc]], [bank[0]])
                for h in range(4):
                    CP(P, 'dve' if h % 2 == 0 else 'act', Vp_t[:, tt * 4 + s, h, (h % 2) * 64:(h % 2) * 64 + 64],
                       bank[1][:, h * 64:(h + 1) * 64], [bank[0]], [Vp_b])
            nkb = 4 * (tt + 1)
            cpull(tile_share(ctotal, tt))
            for pr in range(2):
                Ob, Ot = ps[6 + pr]
                P.op('dve', lambda g, o=raccs[0][1][:]: g.memset(o, 0.0), [], [raccs[0][0]])
                kbs = list(range(nkb - 1, -1, -1))
                ns = len(kbs)

                def q0_of(i):
                    kb = kbs[i]
                    return 128 * (kb - 4 * tt) if (kb >= 4 * tt and i > 0) else 0

                def st1(i):
                    kb = kbs[i]
                    slot = i % 3
                    q0 = q0_of(i)
                    diag = kb >= 4 * tt
                    for hh in range(2):
                        r0 = hh * 64
                        zb = ps[2 * slot + hh]
                        MM(P, zb[1][:, q0:], KT[pr][1][r0:r0 + 64, kb * 128:(kb + 1) * 128], QT[tt % 2][pr][1][r0:r0 + 64, q0:],
                           True, not diag, [KT[pr][0], QT[tt % 2][pr][0]], [zb[0]])
                    if diag:
                        for hh in range(2):
                            zb = ps[2 * slot + hh]
                            MM(P, zb[1][:, q0:], cb(0), mk_t[:, kb - 4 * tt, q0:], False, True, [cbf_b, mk_b], [zb[0]])
                    eb, et = e32[i % 2]
                    zbufs = [ps[2 * slot][0], ps[2 * slot + 1][0]]
                    ACT(P, et[:, :, q0:], ps_all[:, 2 * slot:2 * slot + 2, q0:], AF.Exp, zbufs, [eb], scale=scale)
                    sb_, st_ = spb[i % 3]
                    ACT(P, st_[:, :, q0:], et[:, :, q0:], AF.Ln, [eb, bias_b], [sb_], bias=one_ap)

                def st2(i):
                    kb = kbs[i]
                    slot = i % 3
                    q0 = q0_of(i)
                    diag = kb >= 4 * tt
                    sb_, st_ = spb[i % 3]
                    for hh in range(2):
                        r0 = hh * 64
                        zb = ps[2 * slot + hh]
                        MM(P, zb[1][:, q0:], KT[pr][1][r0:r0 + 64, kb * 128:(kb + 1) * 128], QT[tt % 2][pr][1][r0:r0 + 64, q0:],
                           True, False, [KT[pr][0], QT[tt % 2][pr][0]], [zb[0]])
                    for hh in range(2):
                        zb = ps[2 * slot + hh]
                        if diag:
                            MM(P, zb[1][:, q0:], cb(0), mk_t[:, kb - 4 * tt, q0:], False, False, [cbf_b, mk_b], [zb[0]])
                        MM(P, zb[1][:, q0:], cb(5), st_[:, hh, q0:], False, False, [cbf_b, sb_], [zb[0]])
                        MM(P, zb[1][:, q0:], cb(6), raccs[i % 2][1][:, hh, q0:], False, True, [cbf_b, raccs[i % 2][0]], [zb[0]])
                    pb_, pt_ = PT[i % 3]
                    zbufs = [ps[2 * slot][0], ps[2 * slot + 1][0]]
                    ACT(P, pt_[:, :, q0:], ps_all[:, 2 * slot:2 * slot + 2, q0:], AF.Exp, zbufs, [pb_], scale=scale)
                    TT(P, 'dve', raccs[(i + 1) % 2][1][:, :, q0:], raccs[i % 2][1][:, :, q0:], st_[:, :, q0:], ALU.add,
                       [raccs[i % 2][0], sb_], [raccs[(i + 1) % 2][0]])

                def st3(i):
                    kb = kbs[i]
                    q0 = q0_of(i)
                    pb_, pt_ = PT[i % 3]
                    for hh in range(2):
                        h = 2 * pr + hh
                        MM(P, Ot[:, q0:], Vp_t[:, kb, h, :], pt_[:, hh, q0:], i == 0 and hh == 0, i == ns - 1 and hh == 1,
                           [Vp_b, pb_], [Ob])

                st1(0)
                for i in range(ns):
                    if i + 1 < ns:
                        st1(i + 1)
                    st2(i)
                    if i >= 1:
                        st3(i - 1)
                st3(ns - 1)
                mb, mt = mgs[pr]
                CP(P, 'dve', mt[:], Ot[:], [Ob], [mb])
                DMA(P, 'sp', mg_d[6 + pr, :, c0:c0 + T], mt[:], f"mg{pr}", [mb], [DB('mg')])
        cpull(max(0, ctarget - conv_pos[0]))

    def phase_O(l):
        P.fence()
        P.sb_off = P.sb_base
        xts = [alloc_chunks(8, F32, f"xt{s}") for s in range(2)]
        mgt = [alloc_chunks(8, BF16, f"mgt{s}") for s in range(2)]
        _wb, wo_t = P.sbuf([128, 8, D], BF16, "wout")
        wo_p = [P.buf(None, f"wout{i}") for i in range(2)]
        for hf in range(2):
            DMA(P, 'sp', wo_t[:, hf * 4:(hf + 1) * 4, :], WOUT[l][:, hf * 4:(hf + 1) * 4, :], f"wout{hf}", [DB(f"wo{l}")], [wo_p[hf]])
        for tt in range(NT):
            c0 = tt * T
            xb, xt_t = xts[tt % 2]
            mb, mt_t = mgt[tt % 2]
            for hf in range(2):
                DMA(P, 'sp', xt_t[:, hf * 4:(hf + 1) * 4, :], xT_tile_ap(tt)[:, hf * 4:(hf + 1) * 4, :], f"xt{tt % 2}_{hf}",
                    [DB('xT')], xb[hf * 4:(hf + 1) * 4])
                DMA(P, 'sp', mt_t[:, hf * 4:(hf + 1) * 4, :], mg_d[hf * 4:(hf + 1) * 4, :, c0:c0 + T].rearrange("c p t -> p c t"),
                    f"mgt{tt % 2}_{hf}", [DB('mg')], mb[hf * 4:(hf + 1) * 4])
            for dmc in range(8):
                pb, pt = ps[dmc % 4]
                for c in range(8):
                    MM(P, pt[:], wo_t[:, c, dmc * 128:(dmc + 1) * 128], mt_t[:, c, :], c == 0, c == 7, [wo_p[c // 4], mb[c]], [pb])
                TT(P, 'dve', xt_t[:, dmc, :], pt[:], xt_t[:, dmc, :], ALU.add, [pb, xb[dmc]], [xb[dmc]])
            for hf in range(2):
                DMA(P, 'sp', xT_tile_ap(tt)[:, hf * 4:(hf + 1) * 4, :], xt_t[:, hf * 4:(hf + 1) * 4, :], f"xt{tt % 2}_{hf}",
                    xb[hf * 4:(hf + 1) * 4], [DB('xT')])

    nphase = len([p for p in phases if p != 'conv'])
    seen_ffn = 0
    for ph in phases:
        if ph == 'conv':
            continue
        conv_ensure(conv_mark[ph])
        if ph.startswith('F'):
            j = int(ph[1]) - 1
            l = int(ph[3])
            phase_ffn(l, j, first=(ph == 'F1_0'), last=(ph == 'F2_1'))
        elif ph.startswith('A'):
            phase_A(int(ph[2]))
        elif ph.startswith('B'):
            phase_B(int(ph[2]))
        elif ph.startswith('C'):
            phase_C(int(ph[2]))
        elif ph.startswith('O'):
            phase_O(int(ph[2]))
    P.fence()
    if dump is not None:
        dd = dt("dump", [8, 128, S], F32 if dump == 'xT' else BF16, kind="ExternalOutput").ap()
        src = xT_d if dump == 'xT' else mg_d
        for c in range(8):
            DMA(P, 'sp', dd[c], src[c], "dump", [], [DB('dump')])
        P.fence()
    P.emit()
    return nc


_CONSTS = None


def make_in_maps(inputs):
    global _CONSTS
    if _CONSTS is None:
        _CONSTS = make_consts()
    c = _CONSTS
    inp = {k: np.asarray(v) for k, v in inputs.items()}
    shared = {
        "ffn1_w_gate": inp['ffn1_w_gate'], "ffn2_w_gate": inp['ffn2_w_gate'],
        "ffn1_w_up": inp['ffn1_w_up'], "ffn2_w_up": inp['ffn2_w_up'],
        "ffn1_w_down": inp['ffn1_w_down'], "ffn2_w_down": inp['ffn2_w_down'],
        "w_in": inp['w_in'], "mla_w_uq": inp['mla_w_uq'], "mla_w_ukv": inp['mla_w_ukv'], "w_out": inp['w_out'],
        "gains": pack_gains(inp), "lam": pack_lam(inp),
        "ident_f": c['ident_f'], "cbf": c['cbf'], "masks": c['masks'], "rope": c['rope'],
    }
    maps = []
    for b in range(8):
        m = dict(shared)
        m["x"] = np.ascontiguousarray(inp['x'][b])
        maps.append(m)
    return maps


def kernel(**inputs):
    nc = build_program()
    maps = make_in_maps(inputs)
    res = run_bass_kernel_spmd(nc, maps, core_ids=list(range(8)))
    out = np.stack([np.asarray(r["out"]) for r in res.results], axis=0)
    return out.astype(np.float32)
```

```python
import math
import numpy as np
import ml_dtypes
import concourse.bass as bass
import concourse.mybir as mybir
from concourse.bass_utils import run_bass_kernel_spmd

F32 = mybir.dt.float32
BF16 = mybir.dt.bfloat16
ALU = mybir.AluOpType
AF = mybir.ActivationFunctionType

S = 4096
D = 1024
T = 512
NT = S // T
DFF = 2816
NFC = DFF // 128
INC = 2720
EPS = 1e-6
NEG = -30000.0
ENG = ('pe', 'act', 'dve', 'pool', 'sp')


class Buf:
    __slots__ = ('ap', 'w', 'r', 'name', 'ex')

    def __init__(self, ap, name=''):
        self.ap = ap
        self.w = None
        self.r = {}
        self.name = name
        self.ex = False


class Prog:
    def __init__(self, nc):
        self.nc = nc
        self.q = {e: [] for e in ENG}
        self.seen = {e: {} for e in ENG}
        self.marked = {e: set() for e in ENG}
        self.dsem = {}
        self.sb_off = 16640
        self.sb_base = 16640
        self.uid = 0
        self.allbufs = []

    def sbt(self, shape, dt):
        per = 1
        for s_ in shape[1:]:
            per *= s_
        nbytes = per * (4 if dt == F32 else 2)
        nbytes = (nbytes + 63) // 64 * 64
        off = self.sb_off
        self.sb_off += nbytes
        assert self.sb_off <= 229376, f"SBUF overflow {self.sb_off}"
        self.uid += 1
        return self.nc.alloc_sbuf_tensor_at(f"sb{self.uid}", list(shape), dt, offset=off)

    def buf(self, ap, name=''):
        b = Buf(ap, name)
        self.allbufs.append(b)
        return b

    def sbuf(self, shape, dt, name=''):
        t = self.sbt(shape, dt)
        return self.buf(t[:], name), t

    def op(self, eng, fn, reads=(), writes=(), dma=None, extra=()):
        deps = set(extra)
        for b in reads:
            if b.w is not None:
                deps.add(b.w)
            if b.ex:
                for k, v in b.r.items():
                    if not (k[0] == 'e' and k[1] == eng):
                        deps.add((k[0], k[1], v))
        for b in writes:
            if b.w is not None:
                deps.add(b.w)
            for k, v in b.r.items():
                deps.add((k[0], k[1], v))
        waits = []
        seen = self.seen[eng]
        best = {}
        for (kind, key, v) in deps:
            if kind == 'e' and key == eng and eng in ('pe', 'sp'):
                continue
            if seen.get((kind, key), -1) >= v:
                continue
            if best.get((kind, key), -1) < v:
                best[(kind, key)] = v
        for (kind, key), v in best.items():
            seen[(kind, key)] = v
            waits.append((kind, key, v))
            if kind == 'e':
                self.marked[key].add(v)
        idx = len(self.q[eng])
        if dma is not None:
            cnt = self.dsem.get(dma, 0) + 16
            self.dsem[dma] = cnt
            tok = ('d', dma, cnt)
        else:
            tok = ('e', eng, idx)
        self.q[eng].append((fn, waits, dma))
        for b in reads:
            k = (tok[0], tok[1])
            if b.r.get(k, -1) < tok[2]:
                b.r[k] = tok[2]
        for b in writes:
            b.w = tok
            b.r = {}
        return tok

    def fence(self):
        toks = []
        for e in ('pe', 'act', 'dve', 'pool'):
            if self.q[e]:
                toks.append(('e', e, len(self.q[e]) - 1))
        for k, v in self.dsem.items():
            toks.append(('d', k, v))
        for e in ENG:
            self.op(e, lambda g: g.nop(), extra=toks)
        for b in self.allbufs:
            b.w = None
            b.r = {}

    def emit(self):
        nc = self.nc
        esem = {e: nc.alloc_semaphore(f"es_{e}") for e in ('pe', 'act', 'dve', 'pool')}
        dsem = {k: nc.alloc_semaphore(f"ds_{k}") for k in self.dsem}
        rank = {}
        for e in ('pe', 'act', 'dve', 'pool'):
            m = sorted(self.marked[e])
            rank[e] = {idx: i + 1 for i, idx in enumerate(m)}
        P = self

        def run(eng_name, g):
            for idx, (fn, waits, dma) in enumerate(P.q[eng_name]):
                for (kind, key, v) in waits:
                    if kind == 'e':
                        g.wait_ge(esem[key], rank[key][v])
                    else:
                        g.wait_ge(dsem[key], v)
                ins = fn(g)
                if dma is not None:
                    ins.then_inc(dsem[dma], 16)
                elif eng_name in rank and idx in rank[eng_name]:
                    ins.then_inc(esem[eng_name], 1)

        with nc.Block() as block:
            @block.tensor
            def _(g):
                run('pe', g)

            @block.scalar
            def _(g):
                run('act', g)

            @block.vector
            def _(g):
                run('dve', g)

            @block.gpsimd
            def _(g):
                run('pool', g)

            @block.sync
            def _(g):
                run('sp', g)


def MM(P, out, lhsT, rhs, start, stop, reads, writes):
    P.op('pe', lambda g, o=out, l=lhsT, r=rhs, s=start, t=stop: g.matmul(o, l, r, start=s, stop=t),
         reads, writes)


def TR(P, out, in_, ident, reads, writes):
    P.op('pe', lambda g, o=out, i=in_, d=ident: g.transpose(o, i, d), reads, writes)


def ACT(P, out, in_, func, reads, writes, bias=None, scale=1.0):
    if bias is None:
        P.op('act', lambda g, o=out, i=in_, f=func, s=scale: g.activation(out=o, in_=i, func=f, scale=s),
             reads, writes)
    else:
        P.op('act', lambda g, o=out, i=in_, f=func, s=scale, b=bias: g.activation(out=o, in_=i, func=f, bias=b, scale=s),
             reads, writes)


def TT(P, eng, out, in0, in1, op, reads, writes):
    P.op(eng, lambda g, o=out, a=in0, b=in1, p=op: g.tensor_tensor(out=o, in0=a, in1=b, op=p), reads, writes)


def STT(P, eng, out, in0, scalar, in1, op0, op1, reads, writes):
    P.op(eng, lambda g, o=out, a=in0, s=scalar, b=in1, p0=op0, p1=op1:
         g.scalar_tensor_tensor(out=o, in0=a, scalar=s, in1=b, op0=p0, op1=p1), reads, writes)


def TS(P, eng, out, in0, s1, op0, reads, writes, s2=None, op1=None):
    if op1 is None:
        P.op(eng, lambda g, o=out, a=in0, s=s1, p=op0: g.tensor_scalar(out=o, in0=a, scalar1=s, scalar2=None, op0=p),
             reads, writes)
    else:
        P.op(eng, lambda g, o=out, a=in0, s=s1, p=op0, t=s2, q=op1:
             g.tensor_scalar(out=o, in0=a, scalar1=s, scalar2=t, op0=p, op1=q), reads, writes)


def CP(P, eng, out, in_, reads, writes):
    if eng == 'act':
        P.op('act', lambda g, o=out, i=in_: g.copy(out=o, in_=i), reads, writes)
    else:
        P.op(eng, lambda g, o=out, i=in_: g.tensor_copy(out=o, in_=i), reads, writes)


def DMA(P, eng, out, in_, sem, reads, writes):
    P.op(eng, lambda g, o=out, i=in_: g.dma_start(out=o, in_=i), reads, writes, dma=sem)


def _rope_np(rot_dim):
    inv = (np.float32(500000.0) ** (-np.arange(0, rot_dim, 2, dtype=np.float32) / np.float32(rot_dim))).astype(np.float32)
    ang = (np.arange(S, dtype=np.float32)[:, None] * inv[None, :]).astype(np.float32)
    return np.cos(ang).astype(np.float32), np.sin(ang).astype(np.float32)


def make_consts():
    bf = ml_dtypes.bfloat16
    c = {}
    c['ident_f'] = np.eye(128, dtype=np.float32)
    cb = np.zeros((16, 128, 128), np.float32)
    cb[0] = np.eye(128)
    cb[1] = 1.0 / 1024
    cb[2] = 1.0 / 256
    cb[3] = 1.0 / 128
    cb[4] = 1.0
    jj = np.arange(128)
    cb[5] = np.where(jj[:, None] >= jj[None, :], -8.0, 0.0)
    cb[6] = -8.0
    pm = np.zeros((128, 128), np.float32)
    for m in range(128):
        d = m % 64
        if d < 8:
            pm[m + 8, m] = 1
        elif d < 16:
            pm[m - 8, m] = 1
    cb[7] = pm
    pm = np.zeros((128, 128), np.float32)
    for m in range(64, 96):
        d = m - 64
        pm[(m + 16) if d < 16 else (m - 16), m] = 1
    cb[8] = pm
    pm = np.zeros((128, 128), np.float32)
    for m in range(32):
        pm[(m + 16) if m < 16 else (m - 16), m] = 1
    cb[9] = pm
    es = np.zeros((128, 128), np.float32)
    for i in range(32):
        es[i, 64 + i] = 1
    cb[10] = es
    oh = np.zeros((128, 128), np.float32)
    oh[:, :64] = 1
    cb[11] = oh
    oh = np.zeros((128, 128), np.float32)
    oh[:, 64:] = 1
    cb[12] = oh
    c['cbf'] = np.ascontiguousarray(cb.transpose(1, 0, 2)).astype(bf)
    mk = np.zeros((128, 8, 512), np.float32)
    j = np.arange(128)[:, None]
    t = np.arange(512)[None, :]
    for o in range(4):
        kp = o * 128 + j
        mk[:, o, :] = np.where((kp // 64) <= (t // 64), 0.0, NEG)
        mk[:, 4 + o, :] = np.where(kp < t, 0.0, NEG)
    c['masks'] = mk.astype(bf)
    cd, sd = _rope_np(16)
    cm, sm = _rope_np(32)
    rt = np.zeros((6, 128, S), np.float32)
    rt[0] = 1.0
    rt[2] = 1.0
    rt[4] = 1.0
    for m in range(128):
        d = m % 64
        if d < 16:
            rt[0, m] = cd[:, d % 8]
            rt[1, m] = -sd[:, d % 8] if d < 8 else sd[:, d % 8]
    for m in range(64, 96):
        d = m - 64
        rt[2, m] = cm[:, d % 16]
        rt[3, m] = -sm[:, d % 16] if d < 16 else sm[:, d % 16]
    for m in range(32):
        rt[4, m] = cm[:, m % 16]
        rt[5, m] = -sm[:, m % 16] if m < 16 else sm[:, m % 16]
    c['rope'] = rt
    return c


GAIN_COLS = 64


def pack_gains(inp):
    g = np.zeros((128, GAIN_COLS), np.float32)

    def put(col, vec):
        n = vec.shape[0] // 128
        g[:, col:col + n] = vec.reshape(n, 128).T
    for l in range(2):
        put(0 + 8 * l, inp['ffn1_norm'][l])
        put(16 + 8 * l, inp['mix_norm'][l])
        put(32 + 8 * l, inp['ffn2_norm'][l])
        put(56 + l, inp['diff_subln'][l])
        put(58 + 2 * l, inp['mla_q_norm'][l])
        put(62 + l, inp['mla_kv_norm'][l])
    put(48, inp['final_norm'])
    return g


def pack_lam(inp):
    a = np.stack([np.stack([inp['diff_lambda_q1'][l], inp['diff_lambda_k1'][l],
                            inp['diff_lambda_q2'][l], inp['diff_lambda_k2'][l]]) for l in range(2)])
    return np.ascontiguousarray(np.broadcast_to(a[None], (128, 2, 4, 64))).astype(np.float32)


def build_program(phases=None, dump=None, dbg=None):
    dbg = dbg or {}
    nc = bass.Bass("TRN2", target_bir_lowering=False)
    _lp = nc.allow_low_precision("bf16 matmul operands, fp32 accumulation")
    _lp.__enter__()
    P = Prog(nc)
    dt = nc.dram_tensor

    def ein(name, shape, dtype=F32):
        return dt(name, list(shape), dtype, kind="ExternalInput").ap()

    x_in = ein("x", [S, D])
    w_gate = [ein("ffn1_w_gate", [2, D, DFF]), ein("ffn2_w_gate", [2, D, DFF])]
    w_up = [ein("ffn1_w_up", [2, D, DFF]), ein("ffn2_w_up", [2, D, DFF])]
    w_down = [ein("ffn1_w_down", [2, DFF, D]), ein("ffn2_w_down", [2, DFF, D])]
    w_in = ein("w_in", [2, D, INC])
    w_uq = ein("mla_w_uq", [2, 256, 384])
    w_ukv = ein("mla_w_ukv", [2, 128, 512])
    w_out = ein("w_out", [2, D, D])
    gains_d = ein("gains", [128, GAIN_COLS])
    lam_d = ein("lam", [128, 2, 4, 64])
    identf_d = ein("ident_f", [128, 128])
    cbf_d = ein("cbf", [128, 16, 128], BF16)
    masks_d = ein("masks", [128, 8, 512], BF16)
    rope_d = ein("rope", [6, 128, S])
    out_d = dt("out", [S, D], F32, kind="ExternalOutput").ap()

    xT_d = dt("xT_s", [8, 128, S], F32).ap()
    mg_d = dt("mg_s", [8, 128, S], BF16).ap()
    WGU = [[dt(f"wgu_{l}_{j}", [11, 128, 2, 8, 256], BF16).ap() for j in range(2)] for l in range(2)]
    WD = [[dt(f"wd_{l}_{j}", [11, 128, 2, D], BF16).ap() for j in range(2)] for l in range(2)]
    WIN = [dt(f"win_{l}", [128, 8, INC], BF16).ap() for l in range(2)]
    WUQ = [dt(f"wuq_{l}", [128, 2, 384], BF16).ap() for l in range(2)]
    WUKV = [dt(f"wukv_{l}", [128, 512], BF16).ap() for l in range(2)]
    WOUT = [dt(f"wout_{l}", [128, 8, D], BF16).ap() for l in range(2)]
    dbuf = {}

    def DB(name):
        if name not in dbuf:
            dbuf[name] = P.buf(None, name)
        return dbuf[name]

    if phases is None:
        phases = ['conv', 'F1_0', 'A_0', 'B_0', 'C_0', 'O_0', 'F2_0', 'F1_1', 'A_1', 'B_1', 'C_1', 'O_1', 'F2_1']

    gains_b, gains_t = P.sbuf([128, GAIN_COLS], F32, 'gains')
    identf_b, identf_t = P.sbuf([128, 128], F32, 'identf')
    cbf_b, cbf_t = P.sbuf([128, 16, 128], BF16, 'cbf')
    bias_b, bias_t = P.sbuf([128, 4], F32, 'bias')
    lamv_b, lamv_t = P.sbuf([128, 8], F32, 'lamv')
    P.sb_base = P.sb_off
    ps = []
    ps_all = nc.alloc_psum_tensor("ps_all", [128, 8, 512], F32)
    for i in range(8):
        ps.append((P.buf(None, f"ps{i}"), ps_all[:, i, :]))
        ps[-1][0].ex = True

    DMA(P, 'sp', gains_t[:], gains_d, 'c_g', [], [gains_b])
    DMA(P, 'sp', identf_t[:], identf_d, 'c_i', [], [identf_b])
    DMA(P, 'sp', cbf_t[:], cbf_d, 'c_c', [], [cbf_b])
    P.op('dve', lambda g: g.memset(bias_t[:, 0:1], EPS), [], [bias_b])
    P.op('dve', lambda g: g.memset(bias_t[:, 1:2], 1.0), [], [bias_b])
    eps_ap = bias_t[:, 0:1]
    one_ap = bias_t[:, 1:2]

    def cb(i, k=128, m=128):
        return cbf_t[0:k, i, 0:m]

    conv_items = []
    conv_pos = [0]
    conv_mark = {}

    def gen_conv_items():
        def item(src, dst, shape, dbuf_name):
            conv_items.append((src, dst, shape, dbuf_name))

        def conv_ffn(l, j):
            for g_ in range(11):
                for m, w in enumerate((w_gate[j], w_up[j])):
                    for kh in range(2):
                        src = w[l].rearrange("(kc p) f -> p kc f", p=128)[:, kh * 4:(kh + 1) * 4, g_ * 256:(g_ + 1) * 256]
                        item(src, WGU[l][j][g_, :, m, kh * 4:(kh + 1) * 4, :], [4, 256], f"wgu{l}{j}_{g_}")
            for g_ in range(11):
                src = w_down[j][l].rearrange("(fc p) d -> p fc d", p=128)[:, 2 * g_:2 * g_ + 2, :]
                item(src, WD[l][j][g_], [2, D], f"wd{l}{j}_{g_}")

        def conv_mixer(l):
            for kc in range(8):
                item(w_in[l][kc * 128:(kc + 1) * 128, :], WIN[l][:, kc, :], [INC], f"win{l}")
            item(w_uq[l].rearrange("(kc p) f -> p kc f", p=128), WUQ[l], [2, 384], f"wm{l}")
            item(w_ukv[l], WUKV[l], [512], f"wm{l}")
            for i in range(4):
                item(w_out[l].rearrange("(kc p) f -> p kc f", p=128)[:, 2 * i:2 * i + 2, :], WOUT[l][:, 2 * i:2 * i + 2, :],
                     [2, D], f"wo{l}")

        conv_ffn(0, 0)
        conv_mark['F1_0'] = len(conv_items)
        conv_mixer(0)
        conv_mark['A_0'] = conv_mark['B_0'] = conv_mark['C_0'] = conv_mark['O_0'] = len(conv_items)
        conv_ffn(0, 1)
        conv_mark['F2_0'] = len(conv_items)
        conv_ffn(1, 0)
        conv_mark['F1_1'] = len(conv_items)
        conv_mixer(1)
        conv_mark['A_1'] = conv_mark['B_1'] = conv_mark['C_1'] = conv_mark['O_1'] = len(conv_items)
        conv_ffn(1, 1)
        conv_mark['F2_1'] = len(conv_items)

    gen_conv_items()

    def conv_setup(nslots, engs, width=2816):
        st32 = [P.sbuf([128, width], F32, f"st32_{s}") for s in range(nslots)]
        st16 = [P.sbuf([128, width], BF16, f"st16_{s}") for s in range(nslots)]
        cn = [0]

        def pull(n):
            for _ in range(n):
                if conv_pos[0] >= len(conv_items):
                    return
                src, dst, shape, dbuf_name = conv_items[conv_pos[0]]
                ne_ = 1
                for d_ in shape:
                    ne_ *= d_
                if ne_ > width:
                    return
                conv_pos[0] += 1
                k = cn[0] % nslots
                e = engs[cn[0] % len(engs)]
                cn[0] += 1
                ne = 1
                for d_ in shape:
                    ne *= d_
                b32, t32 = st32[k]
                b16, t16 = st16[k]
                if len(shape) == 1:
                    v32 = t32[:, 0:ne]
                    v16 = t16[:, 0:ne]
                else:
                    v32 = t32[:, 0:ne].rearrange("p (a b) -> p a b", a=shape[0])
                    v16 = t16[:, 0:ne].rearrange("p (a b) -> p a b", a=shape[0])
                DMA(P, 'sp', v32, src, f"st32_{k}", [], [b32])
                CP(P, e, t16[:, 0:ne], t32[:, 0:ne], [b32], [b16])
                DMA(P, 'sp', dst, v16, f"st16_{k}", [b16], [DB(dbuf_name)])
        return pull

    def conv_ensure(upto):
        if conv_pos[0] >= upto:
            return
        P.fence()
        P.sb_off = P.sb_base
        pull = conv_setup(3, ('dve', 'act', 'dve', 'pool'))
        pull(upto - conv_pos[0])

    def tile_share(total, tt):
        a = total * (tt * (tt + 1) // 2) // 36
        b = total * ((tt + 1) * (tt + 2) // 2) // 36
        return b - a

    def xT_tile_ap(tt):
        return xT_d[:, :, tt * T:(tt + 1) * T].rearrange("c p t -> p c t")

    def alloc_chunks(n, dtype, name):
        t_ = P.sbt([128, n, T], dtype)
        return [P.buf(t_[:, c, :], f"{name}{c}") for c in range(n)], t_

    def norm_tile(src_aps, src_bufs, nch, gcol, out_aps, out_bufs, ones_idx, bank, sqs, rstd, eng='dve'):
        pb, pt = bank
        for c in range(nch):
            sb_, st_ = sqs[c % 2]
            ACT(P, st_[:], src_aps[c], AF.Square, [src_bufs[c]], [sb_])
            MM(P, pt[:], cb(ones_idx), st_[:], c == 0, c == nch - 1, [cbf_b, sb_], [pb])
        rb, rt_ = rstd
        ACT(P, rt_[:], pt[:], AF.Ln, [pb, bias_b], [rb], bias=eps_ap)
        ACT(P, rt_[:], rt_[:], AF.Exp, [rb], [rb], scale=-0.5)
        for c in range(nch):
            STT(P, eng, out_aps[c], src_aps[c], gains_t[:, gcol + c:gcol + c + 1], rt_[:], ALU.mult, ALU.mult,
                [src_bufs[c], gains_b, rb], [out_bufs[c]])

    def phase_ffn(l, j, first, last):
        P.fence()
        P.sb_off = P.sb_base
        gcol = (0 if j == 0 else 32) + 8 * l
        xts = [alloc_chunks(8, F32, f"xt{s}") for s in range(2)]
        hts = [alloc_chunks(8, BF16, f"ht{s}") for s in range(2)]
        HT, HT_t = alloc_chunks(NFC, BF16, "HT")
        sqs = [P.sbuf([128, T], BF16, f"sq{s}") for s in range(2)]
        rstds = [P.sbuf([128, T], F32, f"rstd{s}") for s in range(2)]
        sgs = [P.sbuf([128, T], F32, f"sg{s}") for s in range(2)]
        wgu = [P.sbuf([128, 2, 8, 256], BF16, f"wgu{s}") for s in range(3)]
        wd = [P.sbuf([128, 2, D], BF16, f"wd{s}") for s in range(3)]
        if first:
            xtok = [P.sbuf([128, D], F32, f"xtok{s}") for s in range(2)]
        if last:
            yT, yT_t = alloc_chunks(8, F32, "yT")
            otok = [P.sbuf([128, D], F32, f"otok{s}") for s in range(2)]
        if (l, j) == (0, 0):
            cpull, cper = conv_setup(2, ('pool',)), 2
            ctgt = conv_mark['A_0']
        else:
            cpull, cper, ctgt = None, 0, 0
        psG = [ps[0], ps[1]]
        psU = [ps[2], ps[3]]
        psY = [ps[4], ps[5]]
        psN = ps[6]
        psX = [ps[6], ps[7]]
        gu_n = [0]
        d_n = [0]

        def load_gu(n):
            if n >= NT * 11:
                return
            g_ = n % 11
            b_, t_ = wgu[n % 3]
            DMA(P, 'sp', t_[:], WGU[l][j][g_], f"wgu{n % 3}", [DB(f"wgu{l}{j}_{g_}")], [b_])

        def load_d(n):
            if n >= NT * 11:
                return
            g_ = n % 11
            b_, t_ = wd[n % 3]
            DMA(P, 'sp', t_[:], WD[l][j][g_], f"wd{n % 3}", [DB(f"wd{l}{j}_{g_}")], [b_])

        def load_x(tt):
            bufs, t_ = xts[tt % 2]
            if not first:
                for hf in range(2):
                    DMA(P, 'sp', t_[:, hf * 4:(hf + 1) * 4, :], xT_tile_ap(tt)[:, hf * 4:(hf + 1) * 4, :], f"xt{tt % 2}_{hf}",
                        [DB('xT')], bufs[hf * 4:(hf + 1) * 4])
            else:
                for s in range(4):
                    xb, xt_ = xtok[s % 2]
                    r0 = tt * T + s * 128
                    DMA(P, 'sp', xt_[:], x_in[r0:r0 + 128, :], f"xtok{s % 2}", [], [xb])
                    for hb in range(2):
                        pb, pt = psX[hb]
                        for c4 in range(4):
                            c = hb * 4 + c4
                            TR(P, pt[:, c4 * 128:(c4 + 1) * 128], xt_[:, c * 128:(c + 1) * 128], identf_t[:],
                               [xb, identf_b], [pb])
                        CP(P, 'dve', t_[:, hb * 4:(hb + 1) * 4, s * 128:(s + 1) * 128],
                           pt[:].rearrange("p (c t) -> p c t", c=4), [pb], bufs[hb * 4:(hb + 1) * 4])

        def do_norm(tt):
            bufs, t_ = xts[tt % 2]
            hb, ht_ = hts[tt % 2]
            norm_tile([t_[:, c, :] for c in range(8)], bufs, 8, gcol, [ht_[:, c, :] for c in range(8)], hb,
                      1, psN, sqs, rstds[tt % 2])

        def gateup(tt):
            hb, ht_ = hts[tt % 2]
            for g_ in range(11):
                n = tt * 11 + g_
                load_gu(n + 2)
                wb, wt = wgu[n % 3]
                for fi in range(2):
                    fc = 2 * g_ + fi
                    for m, bank in enumerate((psG[fc % 2], psU[fc % 2])):
                        pb, pt = bank
                        for kc in range(8):
                            MM(P, pt[:], wt[:, m, kc, fi * 128:(fi + 1) * 128], ht_[:, kc, :], kc == 0, kc == 7,
                               [wb, hb[kc]], [pb])
                    sb_, st_ = sgs[fc % 2]
                    ACT(P, st_[:], psG[fc % 2][1][:], AF.Silu, [psG[fc % 2][0]], [sb_])
                    TT(P, 'dve', HT_t[:, fc, :], st_[:], psU[fc % 2][1][:], ALU.mult, [sb_, psU[fc % 2][0]], [HT[fc]])

        def down(tt):
            bufs, t_ = xts[tt % 2]
            for g_ in range(11):
                n = tt * 11 + g_
                load_d(n + 2)
                wb, wt = wd[n % 3]
                for fi in range(2):
                    fc = 2 * g_ + fi
                    for dmc in range(8):
                        pb, pt = ps[dmc]
                        MM(P, pt[:], wt[:, fi, dmc * 128:(dmc + 1) * 128], HT_t[:, fc, :], fc == 0, fc == NFC - 1,
                           [wb, HT[fc]], [pb])
            for dmc in range(8):
                pb, pt = ps[dmc]
                STT(P, 'dve', t_[:, dmc, :], pt[:], 0.5, t_[:, dmc, :], ALU.mult, ALU.add, [pb, bufs[dmc]], [bufs[dmc]])

        def store(tt):
            bufs, t_ = xts[tt % 2]
            if not last:
                for hf in range(2):
                    DMA(P, 'sp', xT_tile_ap(tt)[:, hf * 4:(hf + 1) * 4, :], t_[:, hf * 4:(hf + 1) * 4, :], f"xt{tt % 2}_{hf}",
                        bufs[hf * 4:(hf + 1) * 4], [DB('xT')])
            else:
                norm_tile([t_[:, c, :] for c in range(8)], bufs, 8, 48, [yT_t[:, c, :] for c in range(8)], yT,
                          1, psN, sqs, rstds[tt % 2])
                for s in range(4):
                    ob, ot = otok[s % 2]
                    for hb_ in range(2):
                        pb, pt = psX[hb_]
                        for c4 in range(4):
                            c = hb_ * 4 + c4
                            TR(P, pt[:, c4 * 128:(c4 + 1) * 128], yT_t[:, c, s * 128:(s + 1) * 128], identf_t[:],
                               [yT[c], identf_b], [pb])
                        CP(P, 'dve', ot[:, hb_ * 512:(hb_ + 1) * 512], pt[:], [pb], [ob])
                    r0 = tt * T + s * 128
                    DMA(P, 'sp', out_d[r0:r0 + 128, :], ot[:], f"otok{s % 2}", [ob], [DB('out')])

        load_gu(0)
        load_gu(1)
        load_d(0)
        load_d(1)
        load_x(0)
        do_norm(0)
        for tt in range(NT):
            if tt + 1 < NT:
                load_x(tt + 1)
            if cpull is not None:
                cpull(max(0, min(cper, ctgt - conv_pos[0])))
            gateup(tt)
            if tt + 1 < NT:
                do_norm(tt + 1)
            down(tt)
            store(tt)

    def lam_setup(l):
        li = 0.8 - 0.6 * math.exp(-0.3 * l)
        lb, lt = P.sbuf([128, 4, 64], F32, 'laml')
        pr_b, pr_t = P.sbuf([128, 2, 64], F32, 'lamp')
        sm_b, sm_t = P.sbuf([128, 2], F32, 'lams')
        DMA(P, 'sp', lt[:], lam_d[:, l, :, :], 'laml', [], [lb])
        TT(P, 'dve', pr_t[:, 0, :], lt[:, 0, :], lt[:, 1, :], ALU.mult, [lb], [pr_b])
        TT(P, 'dve', pr_t[:, 1, :], lt[:, 2, :], lt[:, 3, :], ALU.mult, [lb], [pr_b])
        P.op('dve', lambda g: g.reduce_sum(out=sm_t[:], in_=pr_t[:], axis=mybir.AxisListType.X), [pr_b], [sm_b])
        ACT(P, sm_t[:], sm_t[:], AF.Exp, [sm_b], [sm_b])
        TT(P, 'dve', lamv_t[:, 2 * l:2 * l + 1], sm_t[:, 1:2], sm_t[:, 0:1], ALU.subtract, [sm_b], [lamv_b])
        TS(P, 'dve', lamv_t[:, 2 * l:2 * l + 1], lamv_t[:, 2 * l:2 * l + 1], -li, ALU.add, [lamv_b], [lamv_b])
        TS(P, 'dve', lamv_t[:, 4 + l:5 + l], gains_t[:, 56 + l:57 + l], 1.0 - li, ALU.mult, [gains_b, lamv_b], [lamv_b])

    def rope_chunk(src_bank, rows, pm_idx, ppbank, qb, cs_t, cs_b, t1, t2, out_ap, out_bufs):
        pb, pt = src_bank
        qbb, qbt = qb
        sub = dbg.get('sub', 9)
        CP(P, 'act', qbt[0:rows, :], pt[0:rows, :], [pb], [qbb])
        ppb, ppt = ppbank
        if sub < 2:
            return
        MM(P, ppt[0:rows, :], cb(pm_idx, rows, rows), qbt[0:rows, :], True, True, [cbf_b, qbb], [ppb])
        t1b, t1t = t1
        t2b, t2t = t2
        if sub < 3:
            return
        TT(P, 'dve', t1t[0:rows, :], pt[0:rows, :], cs_t[0:rows, 0, :], ALU.mult, [pb, cs_b], [t1b])
        TT(P, 'dve', t2t[0:rows, :], ppt[0:rows, :], cs_t[0:rows, 1, :], ALU.mult, [ppb, cs_b], [t2b])
        TT(P, dbg.get('addeng', 'dve'), out_ap, t1t[0:rows, :], t2t[0:rows, :], ALU.add, [t1b, t2b], out_bufs)

    def phase_A(l):
        P.fence()
        P.sb_off = P.sb_base
        lam_setup(l)
        xt, xt_t = alloc_chunks(8, F32, "xt")
        ht, ht_t = alloc_chunks(8, BF16, "ht")
        sqs = [P.sbuf([128, T], BF16, f"sq{s}") for s in range(2)]
        rstd = P.sbuf([128, T], F32, "rstd")
        _wb, win_t = P.sbuf([128, 8, 1536], BF16, "winA")
        win_p = [P.buf(None, f"winA{i}") for i in range(4)]
        KT = [P.sbuf([128, S], BF16, f"KT{h}") for h in range(4)]
        V_b, V_t = P.sbuf([128, 32, 512], BF16, "V")
        QT = [[P.sbuf([128, T], BF16, f"QT{s}{h}") for h in range(4)] for s in range(2)]
        qbs = [P.sbuf([128, T], BF16, f"qb{s}") for s in range(2)]
        cs = [P.sbuf([128, 2, T], F32, f"cs{s}") for s in range(2)]
        t1s = [P.sbuf([128, T], F32, f"t1{s}") for s in range(2)]
        t2s = [P.sbuf([128, T], F32, f"t2{s}") for s in range(2)]
        PT = [P.sbuf([128, 2, T], BF16, f"PT{s}") for s in range(3)]
        mk_b, mk_t = P.sbuf([128, 4, T], BF16, "mask")
        cpull = conv_setup(2, ('pool',), 2048)
        ctarget = conv_mark['F2_0'] if l == 0 else min(len(conv_items), conv_pos[0] + 30)
        ctotal = max(0, ctarget - conv_pos[0])
        o1s = [P.sbuf([128, T], F32, f"o1{s}") for s in range(2)]
        rsb = [P.sbuf([128, T], F32, f"rs{s}") for s in range(4)]
        orstd = P.sbuf([128, T], F32, "orstd")
        mgs = [P.sbuf([128, T], BF16, f"mg{s}") for s in range(2)]
        sacc = [P.sbuf([128, T], F32, f"sacc{s}") for s in range(2)]
        saccb = [P.sbuf([128, T], BF16, f"saccb{s}") for s in range(2)]
        print("phase A sbuf bytes", P.sb_off)
        for hf in range(4):
            DMA(P, 'sp', win_t[:, hf * 2:(hf + 1) * 2, :], WIN[l][:, hf * 2:(hf + 1) * 2, 0:1536], f"winA{hf}", [DB(f"win{l}")], [win_p[hf]])
        DMA(P, 'sp', mk_t[:], masks_d[:, 0:4, :], "mask", [], [mk_b])
        gcol = 16 + 8 * l
        scale = 0.125
        cnt = [0]
        lvl = dbg.get('lvl', 9)
        for tt in range(dbg.get('tiles', NT)):
            c0 = tt * T
            for hf in range(2):
                DMA(P, 'sp', xt_t[:, hf * 4:(hf + 1) * 4, :], xT_tile_ap(tt)[:, hf * 4:(hf + 1) * 4, :], f"xtA{hf}",
                    [DB('xT')], xt[hf * 4:(hf + 1) * 4])
            csb, cst = cs[tt % 2]
            DMA(P, 'sp', cst[:], rope_d[0:2, :, c0:c0 + T].rearrange("a p t -> p a t"), f"cs{tt % 2}", [], [csb])
            norm_tile([xt_t[:, c, :] for c in range(8)], xt, 8, gcol, [ht_t[:, c, :] for c in range(8)], ht,
                      1, ps[6], sqs, rstd)
            if lvl < 1:
                continue
            pend = []
            for kind in range(2):
                for h in range(4):
                    n = cnt[0]
                    cnt[0] += 1
                    bank = ps[6 + n % 2]
                    col = kind * 512 + h * 128
                    for kc in range(8):
                        MM(P, bank[1][:], win_t[:, kc, col:col + 128], ht_t[:, kc, :], kc == 0, kc == 7,
                           [win_p[kc // 2], ht[kc]], [bank[0]])
                    if kind == 0:
                        ob, ot = QT[tt % 2][h]
                        oap = ot[:]
                    else:
                        ob, ot = KT[h]
                        oap = ot[:, c0:c0 + T]
                    if pend:
                        rope_chunk(*pend.pop())
                    pend.append((bank, 128, 7, ps[n % 2], qbs[n % 2], cst, csb, t1s[n % 2], t2s[n % 2], oap, [ob]))
            rope_chunk(*pend.pop())
            if lvl < 2:
                continue
            for s in range(4):
                n = cnt[0]
                cnt[0] += 1
                bank = ps[6 + n % 2]
                for kc in range(8):
                    MM(P, bank[1][:], ht_t[:, kc, s * 128:(s + 1) * 128], win_t[:, kc, 1024:1536], kc == 0, kc == 7,
                       [win_p[kc // 2], ht[kc]], [bank[0]])
                CP(P, 'act' if s % 2 == 0 else 'dve', V_t[:, tt * 4 + s, :], bank[1][:], [bank[0]], [V_b])
            if lvl < 3:
                continue
            cpull(tile_share(ctotal, tt))
            nkb = 4 * (tt + 1)
            pend = [[], [], [], []]
            for h in range(4):
                qb_, qt_ = QT[tt % 2][h]
                kb_, kt_ = KT[h]
                Ob = [ps[4], ps[5]]
                Sm0 = ps[6]
                sab, sat = sacc[h % 2]

                def qk(kb):
                    slot = kb % 2
                    diag = kb >= 4 * tt
                    q0 = 128 * (kb - 4 * tt) if diag else 0
                    for v in range(2):
                        r0 = v * 64
                        bank = ps[2 * slot + v]
                        MM(P, bank[1][:, q0:], kt_[r0:r0 + 64, kb * 128:(kb + 1) * 128], qt_[r0:r0 + 64, q0:], True, not diag,
                           [kb_, qb_], [bank[0]])
                    if diag:
                        for v in range(2):
                            bank = ps[2 * slot + v]
                            MM(P, bank[1][:, q0:], cb(0), mk_t[:, kb - 4 * tt, q0:], False, True, [cbf_b, mk_b], [bank[0]])

                qk(0)
                if nkb > 1:
                    qk(1)
                for kb in range(nkb):
                    slot = kb % 2
                    q0 = 128 * (kb - 4 * tt) if kb >= 4 * tt else 0
                    pb_, pt_ = PT[kb % 3]
                    ACT(P, pt_[:, :, q0:], ps_all[:, 2 * slot:2 * slot + 2, q0:], AF.Exp, [ps[2 * slot][0], ps[2 * slot + 1][0]], [pb_],
                        scale=scale)
                    if kb + 2 < nkb:
                        qk(kb + 2)
                    for v in range(2):
                        MM(P, Ob[v][1][:, q0:], V_t[:, kb, h * 128:(h + 1) * 128], pt_[:, v, q0:], kb == 0, kb == nkb - 1,
                           [V_b, pb_], [Ob[v][0]])
                    MM(P, Sm0[1][:, q0:], cb(4), pt_[:, 0, q0:], kb == 0, kb == nkb - 1, [cbf_b, pb_], [Sm0[0]])
                    if kb == 0:
                        CP(P, 'dve', sat[:], pt_[:, 1, :], [pb_], [sab])
                    else:
                        TT(P, 'dve', sat[:, q0:], sat[:, q0:], pt_[:, 1, q0:], ALU.add, [sab, pb_], [sab])
                    for st_i, kq in enumerate((1, 2, 3, 4)):
                        if kb == min(kq, nkb - 1) and pend[st_i]:
                            pend[st_i].pop(0)()
                if lvl < 4:
                    continue
                o1b, o1t = o1s[h % 2]
                t2b, t2t = t2s[h % 2]
                r0b, r0t = rsb[h % 2]
                r1b, r1t = rsb[2 + h % 2]
                sbb, sbt = saccb[h % 2]
                ACT(P, r0t[:], Sm0[1][:], AF.Ln, [Sm0[0]], [r0b])
                ACT(P, r0t[:], r0t[:], AF.Exp, [r0b], [r0b], scale=-1.0)
                CP(P, 'act', t2t[:], Ob[1][1][:], [Ob[1][0]], [t2b])
                TT(P, 'dve', o1t[:], Ob[0][1][:], r0t[:], ALU.mult, [Ob[0][0], r0b], [o1b])
                CP(P, 'dve', sbt[:], sat[:], [sab], [sbb])

                def S1(h=h, sbb=sbb, sbt=sbt):
                    nb_ = ps[7]
                    MM(P, nb_[1][:], cb(4), sbt[:], True, True, [cbf_b, sbb], [nb_[0]])

                def S2(h=h, o1b=o1b, o1t=o1t, t2b=t2b, t2t=t2t, r1b=r1b, r1t=r1t):
                    nb_ = ps[7]
                    ACT(P, r1t[:], nb_[1][:], AF.Ln, [nb_[0]], [r1b])
                    ACT(P, r1t[:], r1t[:], AF.Exp, [r1b], [r1b], scale=-1.0)
                    TT(P, 'dve', t2t[:], t2t[:], r1t[:], ALU.mult, [t2b, r1b], [t2b])
                    STT(P, 'dve', o1t[:], t2t[:], lamv_t[:, 2 * l:2 * l + 1], o1t[:], ALU.mult, ALU.add,
                        [t2b, lamv_b, o1b], [o1b])

                def S3(h=h, o1b=o1b, o1t=o1t):
                    nb_ = ps[7]
                    sb_, st_ = sqs[h % 2]
                    ACT(P, st_[:], o1t[:], AF.Square, [o1b], [sb_])
                    MM(P, nb_[1][:], cb(3), st_[:], True, True, [cbf_b, sb_], [nb_[0]])

                def S4(h=h, o1b=o1b, o1t=o1t):
                    nb_ = ps[7]
                    mb, mt = mgs[h % 2]
                    orb, ort = orstd
                    ACT(P, ort[:], nb_[1][:], AF.Ln, [nb_[0], bias_b], [orb], bias=eps_ap)
                    ACT(P, ort[:], ort[:], AF.Exp, [orb], [orb], scale=-0.5)
                    STT(P, 'dve', mt[:], o1t[:], lamv_t[:, 4 + l:5 + l], ort[:], ALU.mult, ALU.mult,
                        [o1b, lamv_b, orb], [mb])
                    DMA(P, 'sp', mg_d[h, :, c0:c0 + T], mt[:], f"mg{h % 2}", [mb], [DB('mg')])
                for st_i, f in enumerate((S1, S2, S3, S4)):
                    pend[st_i].append(f)
            for st_i in range(4):
                while pend[st_i]:
                    pend[st_i].pop(0)()
        cpull(max(0, ctarget - conv_pos[0]))

    def phase_B(l):
        P.fence()
        P.sb_off = P.sb_base
        xt, xt_t = alloc_chunks(8, F32, "xt")
        ht, ht_t = alloc_chunks(8, BF16, "ht")
        sqs = [P.sbuf([128, T], BF16, f"sq{s}") for s in range(2)]
        rstd = P.sbuf([128, T], F32, "rstd")
        rstd2 = P.sbuf([128, T], F32, "rstd2")
        rstd3 = P.sbuf([128, T], F32, "rstd3")
        _wb, win_t = P.sbuf([128, 8, 416], BF16, "winB")
        win_p = [P.buf(None, f"winB{i}") for i in range(2)]
        wuq_b, wuq_t = P.sbuf([128, 2, 384], BF16, "wuq")
        wukv_b, wukv_t = P.sbuf([128, 512], BF16, "wukv")
        wkp_b, wkp_t = P.sbuf([128, 4, 96], BF16, "wkp")
        cqn, cqn_t = alloc_chunks(2, BF16, "cqn")
        ckvn_b, ckvn_t = P.sbuf([128, T], BF16, "ckvn")
        krr_b, krr_t = P.sbuf([128, T], BF16, "krr")
        KT = [P.sbuf([128, S], BF16, f"KT{h}") for h in range(4)]
        Vp_b, Vp_t = P.sbuf([128, 32, 4, 128], BF16, "Vp")
        QT = [[P.sbuf([128, T], BF16, f"QT{s}{h}") for h in range(4)] for s in range(2)]
        qbs = [P.sbuf([128, T], BF16, f"qb{s}") for s in range(2)]
        cs = [P.sbuf([128, 4, T], F32, f"cs{s}") for s in range(2)]
        t1s = [P.sbuf([128, T], F32, f"t1{s}") for s in range(2)]
        t2s = [P.sbuf([128, T], F32, f"t2{s}") for s in range(2)]
        PT = [P.sbuf([128, T], BF16, f"PT{s}") for s in range(4)]
        mk_b, mk_t = P.sbuf([128, 4, T], BF16, "mask")
        rsb = [P.sbuf([128, T], F32, f"rs{s}") for s in range(2)]
        mgs = [P.sbuf([128, T], BF16, f"mg{s}") for s in range(2)]
        for hf in range(2):
            DMA(P, 'sp', win_t[:, hf * 4:(hf + 1) * 4, :], WIN[l][:, hf * 4:(hf + 1) * 4, 1536:1952], f"winB{hf}", [DB(f"win{l}")], [win_p[hf]])
        DMA(P, 'sp', wuq_t[:], WUQ[l], "wuq", [DB(f"wm{l}")], [wuq_b])
        DMA(P, 'sp', wukv_t[:], WUKV[l], "wukv", [DB(f"wm{l}")], [wukv_b])
        DMA(P, 'sp', mk_t[:], masks_d[:, 0:4, :], "mask", [], [mk_b])
        P.op('pool', lambda g: g.memset(Vp_t[:], 0.0), [], [Vp_b])
        P.op('pool', lambda g: g.memset(wkp_t[:], 0.0), [], [wkp_b])
        cpull = conv_setup(2, ('pool',), 2048)
        ctotal = min(15 if l == 0 else 10, len(conv_items) - conv_pos[0])
        for h in range(4):
            CP(P, 'dve', wkp_t[:, h, 0:64], wukv_t[:, h * 128:h * 128 + 64], [wukv_b], [wkp_b])
        gcol = 16 + 8 * l
        scale = 96 ** -0.5
        cnt = [0]
        for tt in range(NT):
            c0 = tt * T
            for hf in range(2):
                DMA(P, 'sp', xt_t[:, hf * 4:(hf + 1) * 4, :], xT_tile_ap(tt)[:, hf * 4:(hf + 1) * 4, :], f"xtA{hf}",
                    [DB('xT')], xt[hf * 4:(hf + 1) * 4])
            csb, cst = cs[tt % 2]
            DMA(P, 'sp', cst[:], rope_d[2:6, :, c0:c0 + T].rearrange("a p t -> p a t"), f"cs{tt % 2}", [], [csb])
            norm_tile([xt_t[:, c, :] for c in range(8)], xt, 8, gcol, [ht_t[:, c, :] for c in range(8)], ht,
                      1, ps[6], sqs, rstd)
            for c in range(2):
                for kc in range(8):
                    MM(P, ps[4 + c][1][:], win_t[:, kc, c * 128:(c + 1) * 128], ht_t[:, kc, :], kc == 0, kc == 7,
                       [win_p[kc // 4], ht[kc]], [ps[4 + c][0]])
            norm_tile([ps[4][1][:], ps[5][1][:]], [ps[4][0], ps[5][0]], 2, 58 + 2 * l,
                      [cqn_t[:, 0, :], cqn_t[:, 1, :]], cqn, 2, ps[6], sqs, rstd2)
            for kc in range(8):
                MM(P, ps[7][1][:], win_t[:, kc, 256:384], ht_t[:, kc, :], kc == 0, kc == 7, [win_p[kc // 4], ht[kc]], [ps[7][0]])
            norm_tile([ps[7][1][:]], [ps[7][0]], 1, 62 + l, [ckvn_t[:]], [ckvn_b], 3, ps[6], sqs, rstd3)
            for kc in range(8):
                MM(P, ps[4][1][0:32, :], win_t[:, kc, 384:416], ht_t[:, kc, :], kc == 0, kc == 7, [win_p[kc // 4], ht[kc]], [ps[4][0]])
            cs_kr = cst[:, 2:4, :]
            rope_chunk(ps[4], 32, 9, ps[5], qbs[0], cs_kr, csb, t1s[0], t2s[0], krr_t[0:32, :], [krr_b])
            for h in range(4):
                n = cnt[0]
                cnt[0] += 1
                bank = ps[6 + n % 2]
                for kc in range(2):
                    MM(P, bank[1][0:96, :], wuq_t[:, kc, h * 96:(h + 1) * 96], cqn_t[:, kc, :], kc == 0, kc == 1,
                       [wuq_b, cqn[kc]], [bank[0]])
                qb_, qt_ = QT[tt % 2][h]
                rope_chunk(bank, 96, 8, ps[4 + n % 2], qbs[n % 2], cst[:, 0:2, :], csb, t1s[n % 2], t2s[n % 2],
                           qt_[0:96, :], [qb_])
                kbank = ps[n % 2]
                MM(P, kbank[1][0:96, :], wkp_t[:, h, :], ckvn_t[:], True, False, [wkp_b, ckvn_b], [kbank[0]])
                MM(P, kbank[1][0:96, :], cb(10, 32, 96), krr_t[0:32, :], False, True, [cbf_b, krr_b], [kbank[0]])
                kb_, kt_ = KT[h]
                CP(P, 'act', kt_[0:96, c0:c0 + T], kbank[1][0:96, :], [kbank[0]], [kb_])
            for s in range(4):
                n = cnt[0]
                cnt[0] += 1
                bank = ps[6 + n % 2]
                rhs = wukv_t[:].rearrange("p (h c) -> p h c", h=4)[:, :, 64:128]
                MM(P, bank[1][:, 0:256].rearrange("p (h c) -> p h c", h=4), ckvn_t[:, s * 128:(s + 1) * 128], rhs, True, True,
                   [wukv_b, ckvn_b], [bank[0]])
                for h in range(4):
                    CP(P, 'dve' if h % 2 == 0 else 'act', Vp_t[:, tt * 4 + s, h, (h % 2) * 64:(h % 2) * 64 + 64],
                       bank[1][:, h * 64:(h + 1) * 64], [bank[0]], [Vp_b])
            nkb = 4 * (tt + 1)
            cpull(tile_share(ctotal, tt))
            for pr in range(2):
                Ob, Ot = ps[4 + pr]
                Sb_, St_ = ps[6 + pr]
                steps = [(kb, hh) for kb in range(nkb) for hh in range(2)]

                def qk(i):
                    kb, hh = steps[i]
                    h = 2 * pr + hh
                    sbk = ps[i % 4]
                    diag = kb >= 4 * tt
                    MM(P, sbk[1][:], KT[h][1][0:96, kb * 128:(kb + 1) * 128], QT[tt % 2][h][1][0:96, :], True, not diag,
                       [KT[h][0], QT[tt % 2][h][0]], [sbk[0]])
                    if diag:
                        MM(P, sbk[1][:], cb(0), mk_t[:, kb - 4 * tt, :], False, True, [cbf_b, mk_b], [sbk[0]])

                qk(0)
                qk(1)
                for i in range(len(steps)):
                    if i + 2 < len(steps):
                        qk(i + 2)
                    kb, hh = steps[i]
                    h = 2 * pr + hh
                    sbk = ps[i % 4]
                    pb_, pt_ = PT[i % 4]
                    ACT(P, pt_[:], sbk[1][:], AF.Exp, [sbk[0]], [pb_], scale=scale)
                    MM(P, Ot[:], Vp_t[:, kb, h, :], pt_[:], i == 0, i == len(steps) - 1, [Vp_b, pb_], [Ob])
                    MM(P, St_[:], cb(11 + hh), pt_[:], i == 0, i == len(steps) - 1, [cbf_b, pb_], [Sb_])
                rb_, rt_ = rsb[pr]
                ACT(P, rt_[:], St_[:], AF.Ln, [Sb_], [rb_])
                ACT(P, rt_[:], rt_[:], AF.Exp, [rb_], [rb_], scale=-1.0)
                mb, mt = mgs[pr]
                TT(P, 'dve', mt[:], Ot[:], rt_[:], ALU.mult, [Ob, rb_], [mb])
                DMA(P, 'sp', mg_d[4 + pr, :, c0:c0 + T], mt[:], f"mg{pr}", [mb], [DB('mg')])

    def phase_C(l):
        P.fence()
        P.sb_off = P.sb_base
        xt, xt_t = alloc_chunks(8, F32, "xt")
        ht, ht_t = alloc_chunks(8, BF16, "ht")
        sqs = [P.sbuf([128, T], BF16, f"sq{s}") for s in range(2)]
        rstd = P.sbuf([128, T], F32, "rstd")
        _wb, win_t = P.sbuf([128, 8, 768], BF16, "winC")
        win_p = [P.buf(None, f"winC{i}") for i in range(2)]
        KT = [P.sbuf([128, S], BF16, f"KT{p}") for p in range(2)]
        Vp_b, Vp_t = P.sbuf([128, 32, 4, 128], BF16, "Vp")
        QT = [[P.sbuf([128, T], BF16, f"QT{s}{p}") for p in range(2)] for s in range(2)]
        mk_b, mk_t = P.sbuf([128, 4, T], BF16, "mask")
        e32 = [P.sbuf([128, 2, T], F32, f"e32{s}") for s in range(2)]
        spb = [P.sbuf([128, 2, T], BF16, f"sp{s}") for s in range(3)]
        raccs = [P.sbuf([128, 2, T], BF16, f"racc{s}") for s in range(2)]
        PT = [P.sbuf([128, 2, T], BF16, f"PT{s}") for s in range(3)]
        mgs = [P.sbuf([128, T], BF16, f"mg{s}") for s in range(2)]
        for hf in range(2):
            DMA(P, 'sp', win_t[:, hf * 4:(hf + 1) * 4, :], WIN[l][:, hf * 4:(hf + 1) * 4, 1952:2720], f"winC{hf}", [DB(f"win{l}")], [win_p[hf]])
        DMA(P, 'sp', mk_t[:], masks_d[:, 4:8, :], "mask", [], [mk_b])
        P.op('pool', lambda g: g.memset(Vp_t[:], 0.0), [], [Vp_b])
        cpull = conv_setup(2, ('pool',))
        ctarget = conv_mark['A_1'] if l == 0 else len(conv_items)
        ctotal = max(0, ctarget - conv_pos[0])
        gcol = 16 + 8 * l
        scale = 0.125
        cnt = [0]
        for tt in range(NT):
            c0 = tt * T
            for hf in range(2):
                DMA(P, 'sp', xt_t[:, hf * 4:(hf + 1) * 4, :], xT_tile_ap(tt)[:, hf * 4:(hf + 1) * 4, :], f"xtA{hf}",
                    [DB('xT')], xt[hf * 4:(hf + 1) * 4])
            norm_tile([xt_t[:, c, :] for c in range(8)], xt, 8, gcol, [ht_t[:, c, :] for c in range(8)], ht,
                      1, ps[6], sqs, rstd)
            for kind in range(2):
                for pr in range(2):
                    n = cnt[0]
                    cnt[0] += 1
                    bank = ps[6 + n % 2]
                    col = kind * 256 + pr * 128
                    for kc in range(8):
                        MM(P, bank[1][:], win_t[:, kc, col:col + 128], ht_t[:, kc, :], kc == 0, kc == 7,
                           [win_p[kc // 4], ht[kc]], [bank[0]])
                    if kind == 0:
                        CP(P, 'act', QT[tt % 2][pr][1][:], bank[1][:], [bank[0]], [QT[tt % 2][pr][0]])
                    else:
                        CP(P, 'dve', KT[pr][1][:, c0:c0 + T], bank[1][:], [bank[0]], [KT[pr][0]])
            for s in range(4):
                n = cnt[0]
                cnt[0] += 1
                bank = ps[6 + n % 2]
                for kc in range(8):
                    MM(P, bank[1][:, 0:256], ht_t[:, kc, s * 128:(s + 1) * 128], win_t[:, kc, 512:768], kc == 0, kc == 7,
                       [win_p[kc // 4], ht[kc]], [bank[0]])
                for h in range(4):
                    CP(P, 'dve' if h % 2 == 0 else 'act', Vp_t[:, tt * 4 + s, h, (h % 2) * 64:(h % 2) * 64 + 64],
                       bank[1][:, h * 64:(h + 1) * 64], [bank[0]], [Vp_b])
            nkb = 4 * (tt + 1)
            cpull(tile_share(ctotal, tt))
            for pr in range(2):
                Ob, Ot = ps[6 + pr]
                P.op('dve', lambda g, o=raccs[0][1][:]: g.memset(o, 0.0), [], [raccs[0][0]])
                kbs = list(range(nkb - 1, -1, -1))
                ns = len(kbs)

                def q0_of(i):
                    kb = kbs[i]
                    return 128 * (kb - 4 * tt) if (kb >= 4 * tt and i > 0) else 0

                def st1(i):
                    kb = kbs[i]
                    slot = i % 3
                    q0 = q0_of(i)
                    diag = kb >= 4 * tt
                    for hh in range(2):
                        r0 = hh * 64
                        zb = ps[2 * slot + hh]
                        MM(P, zb[1][:, q0:], KT[pr][1][r0:r0 + 64, kb * 128:(kb + 1) * 128], QT[tt % 2][pr][1][r0:r0 + 64, q0:],
                           True, not diag, [KT[pr][0], QT[tt % 2][pr][0]], [zb[0]])
                    if diag:
                        for hh in range(2):
                            zb = ps[2 * slot + hh]
                            MM(P, zb[1][:, q0:], cb(0), mk_t[:, kb - 4 * tt, q0:], False, True, [cbf_b, mk_b], [zb[0]])
                    eb, et = e32[i % 2]
                    zbufs = [ps[2 * slot][0], ps[2 * slot + 1][0]]
                    ACT(P, et[:, :, q0:], ps_all[:, 2 * slot:2 * slot + 2, q0:], AF.Exp, zbufs, [eb], scale=scale)
                    sb_, st_ = spb[i % 3]
                    ACT(P, st_[:, :, q0:], et[:, :, q0:], AF.Ln, [eb, bias_b], [sb_], bias=one_ap)

                def st2(i):
                    kb = kbs[i]
                    slot = i % 3
                    q0 = q0_of(i)
                    diag = kb >= 4 * tt
                    sb_, st_ = spb[i % 3]
                    for hh in range(2):
                        r0 = hh * 64
                        zb = ps[2 * slot + hh]
                        MM(P, zb[1][:, q0:], KT[pr][1][r0:r0 + 64, kb * 128:(kb + 1) * 128], QT[tt % 2][pr][1][r0:r0 + 64, q0:],
                           True, False, [KT[pr][0], QT[tt % 2][pr][0]], [zb[0]])
                    for hh in range(2):
                        zb = ps[2 * slot + hh]
                        if diag:
                            MM(P, zb[1][:, q0:], cb(0), mk_t[:, kb - 4 * tt, q0:], False, False, [cbf_b, mk_b], [zb[0]])
                        MM(P, zb[1][:, q0:], cb(5), st_[:, hh, q0:], False, False, [cbf_b, sb_], [zb[0]])
                        MM(P, zb[1][:, q0:], cb(6), raccs[i % 2][1][:, hh, q0:], False, True, [cbf_b, raccs[i % 2][0]], [zb[0]])
                    pb_, pt_ = PT[i % 3]
                    zbufs = [ps[2 * slot][0], ps[2 * slot + 1][0]]
                    ACT(P, pt_[:, :, q0:], ps_all[:, 2 * slot:2 * slot + 2, q0:], AF.Exp, zbufs, [pb_], scale=scale)
                    TT(P, 'dve', raccs[(i + 1) % 2][1][:, :, q0:], raccs[i % 2][1][:, :, q0:], st_[:, :, q0:], ALU.add,
                       [raccs[i % 2][0], sb_], [raccs[(i + 1) % 2][0]])

                def st3(i):
                    kb = kbs[i]
                    q0 = q0_of(i)
                    pb_, pt_ = PT[i % 3]
                    for hh in range(2):
                        h = 2 * pr + hh
                        MM(P, Ot[:, q0:], Vp_t[:, kb, h, :], pt_[:, hh, q0:], i == 0 and hh == 0, i == ns - 1 and hh == 1,
                           [Vp_b, pb_], [Ob])

                st1(0)
                for i in range(ns):
                    if i + 1 < ns:
                        st1(i + 1)
                    st2(i)
                    if i >= 1:
                        st3(i - 1)
                st3(ns - 1)
                mb, mt = mgs[pr]
                CP(P, 'dve', mt[:], Ot[:], [Ob], [mb])
                DMA(P, 'sp', mg_d[6 + pr, :, c0:c0 + T], mt[:], f"mg{pr}", [mb], [DB('mg')])
        cpull(max(0, ctarget - conv_pos[0]))

    def phase_O(l):
        P.fence()
        P.sb_off = P.sb_base
        xts = [alloc_chunks(8, F32, f"xt{s}") for s in range(2)]
        mgt = [alloc_chunks(8, BF16, f"mgt{s}") for s in range(2)]
        _wb, wo_t = P.sbuf([128, 8, D], BF16, "wout")
        wo_p = [P.buf(None, f"wout{i}") for i in range(2)]
        for hf in range(2):
            DMA(P, 'sp', wo_t[:, hf * 4:(hf + 1) * 4, :], WOUT[l][:, hf * 4:(hf + 1) * 4, :], f"wout{hf}", [DB(f"wo{l}")], [wo_p[hf]])
        for tt in range(NT):
            c0 = tt * T
            xb, xt_t = xts[tt % 2]
            mb, mt_t = mgt[tt % 2]
            for hf in range(2):
                DMA(P, 'sp', xt_t[:, hf * 4:(hf + 1) * 4, :], xT_tile_ap(tt)[:, hf * 4:(hf + 1) * 4, :], f"xt{tt % 2}_{hf}",
                    [DB('xT')], xb[hf * 4:(hf + 1) * 4])
                DMA(P, 'sp', mt_t[:, hf * 4:(hf + 1) * 4, :], mg_d[hf * 4:(hf + 1) * 4, :, c0:c0 + T].rearrange("c p t -> p c t"),
                    f"mgt{tt % 2}_{hf}", [DB('mg')], mb[hf * 4:(hf + 1) * 4])
            for dmc in range(8):
                pb, pt = ps[dmc % 4]
                for c in range(8):
                    MM(P, pt[:], wo_t[:, c, dmc * 128:(dmc + 1) * 128], mt_t[:, c, :], c == 0, c == 7, [wo_p[c // 4], mb[c]], [pb])
                TT(P, 'dve', xt_t[:, dmc, :], pt[:], xt_t[:, dmc, :], ALU.add, [pb, xb[dmc]], [xb[dmc]])
            for hf in range(2):
                DMA(P, 'sp', xT_tile_ap(tt)[:, hf * 4:(hf + 1) * 4, :], xt_t[:, hf * 4:(hf + 1) * 4, :], f"xt{tt % 2}_{hf}",
                    xb[hf * 4:(hf + 1) * 4], [DB('xT')])

    nphase = len([p for p in phases if p != 'conv'])
    seen_ffn = 0
    for ph in phases:
        if ph == 'conv':
            continue
        conv_ensure(conv_mark[ph])
        if ph.startswith('F'):
            j = int(ph[1]) - 1
            l = int(ph[3])
            phase_ffn(l, j, first=(ph == 'F1_0'), last=(ph == 'F2_1'))
        elif ph.startswith('A'):
            phase_A(int(ph[2]))
        elif ph.startswith('B'):
            phase_B(int(ph[2]))
        elif ph.startswith('C'):
            phase_C(int(ph[2]))
        elif ph.startswith('O'):
            phase_O(int(ph[2]))
    P.fence()
    if dump is not None:
        dd = dt("dump", [8, 128, S], F32 if dump == 'xT' else BF16, kind="ExternalOutput").ap()
        src = xT_d if dump == 'xT' else mg_d
        for c in range(8):
            DMA(P, 'sp', dd[c], src[c], "dump", [], [DB('dump')])
        P.fence()
    P.emit()
    return nc


_CONSTS = None


def make_in_maps(inputs):
    global _CONSTS
    if _CONSTS is None:
        _CONSTS = make_consts()
    c = _CONSTS
    inp = {k: np.asarray(v) for k, v in inputs.items()}
    shared = {
        "ffn1_w_gate": inp['ffn1_w_gate'], "ffn2_w_gate": inp['ffn2_w_gate'],
        "ffn1_w_up": inp['ffn1_w_up'], "ffn2_w_up": inp['ffn2_w_up'],
        "ffn1_w_down": inp['ffn1_w_down'], "ffn2_w_down": inp['ffn2_w_down'],
        "w_in": inp['w_in'], "mla_w_uq": inp['mla_w_uq'], "mla_w_ukv": inp['mla_w_ukv'], "w_out": inp['w_out'],
        "gains": pack_gains(inp), "lam": pack_lam(inp),
        "ident_f": c['ident_f'], "cbf": c['cbf'], "masks": c['masks'], "rope": c['rope'],
    }
    maps = []
    for b in range(8):
        m = dict(shared)
        m["x"] = np.ascontiguousarray(inp['x'][b])
        maps.append(m)
    return maps


def kernel(**inputs):
    nc = build_program()
    maps = make_in_maps(inputs)
    res = run_bass_kernel_spmd(nc, maps, core_ids=list(range(8)))
    out = np.stack([np.asarray(r["out"]) for r in res.results], axis=0)
    return out.astype(np.float32)
```

```python
import math
import numpy as np
import ml_dtypes
import concourse.bass as bass
import concourse.mybir as mybir
from concourse.bass_utils import run_bass_kernel_spmd

F32 = mybir.dt.float32
BF16 = mybir.dt.bfloat16
ALU = mybir.AluOpType
AF = mybir.ActivationFunctionType

S = 4096
D = 1024
T = 512
NT = S // T
DFF = 2816
NFC = DFF // 128
INC = 2720
EPS = 1e-6
NEG = -30000.0
ENG = ('pe', 'act', 'dve', 'pool', 'sp')


class Buf:
    __slots__ = ('ap', 'w', 'r', 'name', 'ex')

    def __init__(self, ap, name=''):
        self.ap = ap
        self.w = None
        self.r = {}
        self.name = name
        self.ex = False


class Prog:
    def __init__(self, nc):
        self.nc = nc
        self.q = {e: [] for e in ENG}
        self.seen = {e: {} for e in ENG}
        self.marked = {e: set() for e in ENG}
        self.dsem = {}
        self.sb_off = 16640
        self.sb_base = 16640
        self.uid = 0
        self.allbufs = []

    def sbt(self, shape, dt):
        per = 1
        for s_ in shape[1:]:
            per *= s_
        nbytes = per * (4 if dt == F32 else 2)
        nbytes = (nbytes + 63) // 64 * 64
        off = self.sb_off
        self.sb_off += nbytes
        assert self.sb_off <= 229376, f"SBUF overflow {self.sb_off}"
        self.uid += 1
        return self.nc.alloc_sbuf_tensor_at(f"sb{self.uid}", list(shape), dt, offset=off)

    def buf(self, ap, name=''):
        b = Buf(ap, name)
        self.allbufs.append(b)
        return b

    def sbuf(self, shape, dt, name=''):
        t = self.sbt(shape, dt)
        return self.buf(t[:], name), t

    def op(self, eng, fn, reads=(), writes=(), dma=None, extra=()):
        deps = set(extra)
        for b in reads:
            if b.w is not None:
                deps.add(b.w)
            if b.ex:
                for k, v in b.r.items():
                    if not (k[0] == 'e' and k[1] == eng):
                        deps.add((k[0], k[1], v))
        for b in writes:
            if b.w is not None:
                deps.add(b.w)
            for k, v in b.r.items():
                deps.add((k[0], k[1], v))
        waits = []
        seen = self.seen[eng]
        best = {}
        for (kind, key, v) in deps:
            if kind == 'e' and key == eng and eng in ('pe', 'sp'):
                continue
            if seen.get((kind, key), -1) >= v:
                continue
            if best.get((kind, key), -1) < v:
                best[(kind, key)] = v
        for (kind, key), v in best.items():
            seen[(kind, key)] = v
            waits.append((kind, key, v))
            if kind == 'e':
                self.marked[key].add(v)
        idx = len(self.q[eng])
        if dma is not None:
            cnt = self.dsem.get(dma, 0) + 16
            self.dsem[dma] = cnt
            tok = ('d', dma, cnt)
        else:
            tok = ('e', eng, idx)
        self.q[eng].append((fn, waits, dma))
        for b in reads:
            k = (tok[0], tok[1])
            if b.r.get(k, -1) < tok[2]:
                b.r[k] = tok[2]
        for b in writes:
            b.w = tok
            b.r = {}
        return tok

    def fence(self):
        toks = []
        for e in ('pe', 'act', 'dve', 'pool'):
            if self.q[e]:
                toks.append(('e', e, len(self.q[e]) - 1))
        for k, v in self.dsem.items():
            toks.append(('d', k, v))
        for e in ENG:
            self.op(e, lambda g: g.nop(), extra=toks)
        for b in self.allbufs:
            b.w = None
            b.r = {}

    def emit(self):
        nc = self.nc
        esem = {e: nc.alloc_semaphore(f"es_{e}") for e in ('pe', 'act', 'dve', 'pool')}
        dsem = {k: nc.alloc_semaphore(f"ds_{k}") for k in self.dsem}
        rank = {}
        for e in ('pe', 'act', 'dve', 'pool'):
            m = sorted(self.marked[e])
            rank[e] = {idx: i + 1 for i, idx in enumerate(m)}
        P = self

        def run(eng_name, g):
            for idx, (fn, waits, dma) in enumerate(P.q[eng_name]):
                for (kind, key, v) in waits:
                    if kind == 'e':
                        g.wait_ge(esem[key], rank[key][v])
                    else:
                        g.wait_ge(dsem[key], v)
                ins = fn(g)
                if dma is not None:
                    ins.then_inc(dsem[dma], 16)
                elif eng_name in rank and idx in rank[eng_name]:
                    ins.then_inc(esem[eng_name], 1)

        with nc.Block() as block:
            @block.tensor
            def _(g):
                run('pe', g)

            @block.scalar
            def _(g):
                run('act', g)

            @block.vector
            def _(g):
                run('dve', g)

            @block.gpsimd
            def _(g):
                run('pool', g)

            @block.sync
            def _(g):
                run('sp', g)


def MM(P, out, lhsT, rhs, start, stop, reads, writes):
    P.op('pe', lambda g, o=out, l=lhsT, r=rhs, s=start, t=stop: g.matmul(o, l, r, start=s, stop=t),
         reads, writes)


def TR(P, out, in_, ident, reads, writes):
    P.op('pe', lambda g, o=out, i=in_, d=ident: g.transpose(o, i, d), reads, writes)


def ACT(P, out, in_, func, reads, writes, bias=None, scale=1.0):
    if bias is None:
        P.op('act', lambda g, o=out, i=in_, f=func, s=scale: g.activation(out=o, in_=i, func=f, scale=s),
             reads, writes)
    else:
        P.op('act', lambda g, o=out, i=in_, f=func, s=scale, b=bias: g.activation(out=o, in_=i, func=f, bias=b, scale=s),
             reads, writes)


def TT(P, eng, out, in0, in1, op, reads, writes):
    P.op(eng, lambda g, o=out, a=in0, b=in1, p=op: g.tensor_tensor(out=o, in0=a, in1=b, op=p), reads, writes)


def STT(P, eng, out, in0, scalar, in1, op0, op1, reads, writes):
    P.op(eng, lambda g, o=out, a=in0, s=scalar, b=in1, p0=op0, p1=op1:
         g.scalar_tensor_tensor(out=o, in0=a, scalar=s, in1=b, op0=p0, op1=p1), reads, writes)


def TS(P, eng, out, in0, s1, op0, reads, writes, s2=None, op1=None):
    if op1 is None:
        P.op(eng, lambda g, o=out, a=in0, s=s1, p=op0: g.tensor_scalar(out=o, in0=a, scalar1=s, scalar2=None, op0=p),
             reads, writes)
    else:
        P.op(eng, lambda g, o=out, a=in0, s=s1, p=op0, t=s2, q=op1:
             g.tensor_scalar(out=o, in0=a, scalar1=s, scalar2=t, op0=p, op1=q), reads, writes)


def CP(P, eng, out, in_, reads, writes):
    if eng == 'act':
        P.op('act', lambda g, o=out, i=in_: g.copy(out=o, in_=i), reads, writes)
    else:
        P.op(eng, lambda g, o=out, i=in_: g.tensor_copy(out=o, in_=i), reads, writes)


def DMA(P, eng, out, in_, sem, reads, writes):
    P.op(eng, lambda g, o=out, i=in_: g.dma_start(out=o, in_=i), reads, writes, dma=sem)


def _rope_np(rot_dim):
    inv = (np.float32(500000.0) ** (-np.arange(0, rot_dim, 2, dtype=np.float32) / np.float32(rot_dim))).astype(np.float32)
    ang = (np.arange(S, dtype=np.float32)[:, None] * inv[None, :]).astype(np.float32)
    return np.cos(ang).astype(np.float32), np.sin(ang).astype(np.float32)


def make_consts():
    bf = ml_dtypes.bfloat16
    c = {}
    c['ident_f'] = np.eye(128, dtype=np.float32)
    cb = np.zeros((16, 128, 128), np.float32)
    cb[0] = np.eye(128)
    cb[1] = 1.0 / 1024
    cb[2] = 1.0 / 256
    cb[3] = 1.0 / 128
    cb[4] = 1.0
    jj = np.arange(128)
    cb[5] = np.where(jj[:, None] >= jj[None, :], -8.0, 0.0)
    cb[6] = -8.0
    pm = np.zeros((128, 128), np.float32)
    for m in range(128):
        d = m % 64
        if d < 8:
            pm[m + 8, m] = 1
        elif d < 16:
            pm[m - 8, m] = 1
    cb[7] = pm
    pm = np.zeros((128, 128), np.float32)
    for m in range(64, 96):
        d = m - 64
        pm[(m + 16) if d < 16 else (m - 16), m] = 1
    cb[8] = pm
    pm = np.zeros((128, 128), np.float32)
    for m in range(32):
        pm[(m + 16) if m < 16 else (m - 16), m] = 1
    cb[9] = pm
    es = np.zeros((128, 128), np.float32)
    for i in range(32):
        es[i, 64 + i] = 1
    cb[10] = es
    oh = np.zeros((128, 128), np.float32)
    oh[:, :64] = 1
    cb[11] = oh
    oh = np.zeros((128, 128), np.float32)
    oh[:, 64:] = 1
    cb[12] = oh
    c['cbf'] = np.ascontiguousarray(cb.transpose(1, 0, 2)).astype(bf)
    mk = np.zeros((128, 8, 512), np.float32)
    j = np.arange(128)[:, None]
    t = np.arange(512)[None, :]
    for o in range(4):
        kp = o * 128 + j
        mk[:, o, :] = np.where((kp // 64) <= (t // 64), 0.0, NEG)
        mk[:, 4 + o, :] = np.where(kp < t, 0.0, NEG)
    c['masks'] = mk.astype(bf)
    cd, sd = _rope_np(16)
    cm, sm = _rope_np(32)
    rt = np.zeros((6, 128, S), np.float32)
    rt[0] = 1.0
    rt[2] = 1.0
    rt[4] = 1.0
    for m in range(128):
        d = m % 64
        if d < 16:
            rt[0, m] = cd[:, d % 8]
            rt[1, m] = -sd[:, d % 8] if d < 8 else sd[:, d % 8]
    for m in range(64, 96):
        d = m - 64
        rt[2, m] = cm[:, d % 16]
        rt[3, m] = -sm[:, d % 16] if d < 16 else sm[:, d % 16]
    for m in range(32):
        rt[4, m] = cm[:, m % 16]
        rt[5, m] = -sm[:, m % 16] if m < 16 else sm[:, m % 16]
    c['rope'] = rt
    return c


GAIN_COLS = 64


def pack_gains(inp):
    g = np.zeros((128, GAIN_COLS), np.float32)

    def put(col, vec):
        n = vec.shape[0] // 128
        g[:, col:col + n] = vec.reshape(n, 128).T
    for l in range(2):
        put(0 + 8 * l, inp['ffn1_norm'][l])
        put(16 + 8 * l, inp['mix_norm'][l])
        put(32 + 8 * l, inp['ffn2_norm'][l])
        put(56 + l, inp['diff_subln'][l])
        put(58 + 2 * l, inp['mla_q_norm'][l])
        put(62 + l, inp['mla_kv_norm'][l])
    put(48, inp['final_norm'])
    return g


def pack_lam(inp):
    a = np.stack([np.stack([inp['diff_lambda_q1'][l], inp['diff_lambda_k1'][l],
                            inp['diff_lambda_q2'][l], inp['diff_lambda_k2'][l]]) for l in range(2)])
    return np.ascontiguousarray(np.broadcast_to(a[None], (128, 2, 4, 64))).astype(np.float32)


def build_program(phases=None, dump=None, dbg=None):
    dbg = dbg or {}
    nc = bass.Bass("TRN2", target_bir_lowering=False)
    _lp = nc.allow_low_precision("bf16 matmul operands, fp32 accumulation")
    _lp.__enter__()
    P = Prog(nc)
    dt = nc.dram_tensor

    def ein(name, shape, dtype=F32):
        return dt(name, list(shape), dtype, kind="ExternalInput").ap()

    x_in = ein("x", [S, D])
    w_gate = [ein("ffn1_w_gate", [2, D, DFF]), ein("ffn2_w_gate", [2, D, DFF])]
    w_up = [ein("ffn1_w_up", [2, D, DFF]), ein("ffn2_w_up", [2, D, DFF])]
    w_down = [ein("ffn1_w_down", [2, DFF, D]), ein("ffn2_w_down", [2, DFF, D])]
    w_in = ein("w_in", [2, D, INC])
    w_uq = ein("mla_w_uq", [2, 256, 384])
    w_ukv = ein("mla_w_ukv", [2, 128, 512])
    w_out = ein("w_out", [2, D, D])
    gains_d = ein("gains", [128, GAIN_COLS])
    lam_d = ein("lam", [128, 2, 4, 64])
    identf_d = ein("ident_f", [128, 128])
    cbf_d = ein("cbf", [128, 16, 128], BF16)
    masks_d = ein("masks", [128, 8, 512], BF16)
    rope_d = ein("rope", [6, 128, S])
    out_d = dt("out", [S, D], F32, kind="ExternalOutput").ap()

    xT_d = dt("xT_s", [8, 128, S], F32).ap()
    mg_d = dt("mg_s", [8, 128, S], BF16).ap()
    WGU = [[dt(f"wgu_{l}_{j}", [11, 128, 2, 8, 256], BF16).ap() for j in range(2)] for l in range(2)]
    WD = [[dt(f"wd_{l}_{j}", [11, 128, 2, D], BF16).ap() for j in range(2)] for l in range(2)]
    WIN = [dt(f"win_{l}", [128, 8, INC], BF16).ap() for l in range(2)]
    WUQ = [dt(f"wuq_{l}", [128, 2, 384], BF16).ap() for l in range(2)]
    WUKV = [dt(f"wukv_{l}", [128, 512], BF16).ap() for l in range(2)]
    WOUT = [dt(f"wout_{l}", [128, 8, D], BF16).ap() for l in range(2)]
    dbuf = {}

    def DB(name):
        if name not in dbuf:
            dbuf[name] = P.buf(None, name)
        return dbuf[name]

    if phases is None:
        phases = ['conv', 'F1_0', 'A_0', 'B_0', 'C_0', 'O_0', 'F2_0', 'F1_1', 'A_1', 'B_1', 'C_1', 'O_1', 'F2_1']

    gains_b, gains_t = P.sbuf([128, GAIN_COLS], F32, 'gains')
    identf_b, identf_t = P.sbuf([128, 128], F32, 'identf')
    cbf_b, cbf_t = P.sbuf([128, 16, 128], BF16, 'cbf')
    bias_b, bias_t = P.sbuf([128, 4], F32, 'bias')
    lamv_b, lamv_t = P.sbuf([128, 8], F32, 'lamv')
    P.sb_base = P.sb_off
    ps = []
    ps_all = nc.alloc_psum_tensor("ps_all", [128, 8, 512], F32)
    for i in range(8):
        ps.append((P.buf(None, f"ps{i}"), ps_all[:, i, :]))
        ps[-1][0].ex = True

    DMA(P, 'sp', gains_t[:], gains_d, 'c_g', [], [gains_b])
    DMA(P, 'sp', identf_t[:], identf_d, 'c_i', [], [identf_b])
    DMA(P, 'sp', cbf_t[:], cbf_d, 'c_c', [], [cbf_b])
    P.op('dve', lambda g: g.memset(bias_t[:, 0:1], EPS), [], [bias_b])
    P.op('dve', lambda g: g.memset(bias_t[:, 1:2], 1.0), [], [bias_b])
    eps_ap = bias_t[:, 0:1]
    one_ap = bias_t[:, 1:2]

    def cb(i, k=128, m=128):
        return cbf_t[0:k, i, 0:m]

    conv_items = []
    conv_pos = [0]
    conv_mark = {}

    def gen_conv_items():
        def item(src, dst, shape, dbuf_name):
            conv_items.append((src, dst, shape, dbuf_name))

        def conv_ffn(l, j):
            for g_ in range(11):
                for m, w in enumerate((w_gate[j], w_up[j])):
                    for kh in range(2):
                        src = w[l].rearrange("(kc p) f -> p kc f", p=128)[:, kh * 4:(kh + 1) * 4, g_ * 256:(g_ + 1) * 256]
                        item(src, WGU[l][j][g_, :, m, kh * 4:(kh + 1) * 4, :], [4, 256], f"wgu{l}{j}_{g_}")
            for g_ in range(11):
                src = w_down[j][l].rearrange("(fc p) d -> p fc d", p=128)[:, 2 * g_:2 * g_ + 2, :]
                item(src, WD[l][j][g_], [2, D], f"wd{l}{j}_{g_}")

        def conv_mixer(l):
            for kc in range(8):
                item(w_in[l][kc * 128:(kc + 1) * 128, :], WIN[l][:, kc, :], [INC], f"win{l}")
            item(w_uq[l].rearrange("(kc p) f -> p kc f", p=128), WUQ[l], [2, 384], f"wm{l}")
            item(w_ukv[l], WUKV[l], [512], f"wm{l}")
            for i in range(4):
                item(w_out[l].rearrange("(kc p) f -> p kc f", p=128)[:, 2 * i:2 * i + 2, :], WOUT[l][:, 2 * i:2 * i + 2, :],
                     [2, D], f"wo{l}")

        conv_ffn(0, 0)
        conv_mark['F1_0'] = len(conv_items)
        conv_mixer(0)
        conv_mark['A_0'] = conv_mark['B_0'] = conv_mark['C_0'] = conv_mark['O_0'] = len(conv_items)
        conv_ffn(0, 1)
        conv_mark['F2_0'] = len(conv_items)
        conv_ffn(1, 0)
        conv_mark['F1_1'] = len(conv_items)
        conv_mixer(1)
        conv_mark['A_1'] = conv_mark['B_1'] = conv_mark['C_1'] = conv_mark['O_1'] = len(conv_items)
        conv_ffn(1, 1)
        conv_mark['F2_1'] = len(conv_items)

    gen_conv_items()

    def conv_setup(nslots, engs, width=2816):
        st32 = [P.sbuf([128, width], F32, f"st32_{s}") for s in range(nslots)]
        st16 = [P.sbuf([128, width], BF16, f"st16_{s}") for s in range(nslots)]
        cn = [0]

        def pull(n):
            for _ in range(n):
                if conv_pos[0] >= len(conv_items):
                    return
                src, dst, shape, dbuf_name = conv_items[conv_pos[0]]
                ne_ = 1
                for d_ in shape:
                    ne_ *= d_
                if ne_ > width:
                    return
                conv_pos[0] += 1
                k = cn[0] % nslots
                e = engs[cn[0] % len(engs)]
                cn[0] += 1
                ne = 1
                for d_ in shape:
                    ne *= d_
                b32, t32 = st32[k]
                b16, t16 = st16[k]
                if len(shape) == 1:
                    v32 = t32[:, 0:ne]
                    v16 = t16[:, 0:ne]
                else:
                    v32 = t32[:, 0:ne].rearrange("p (a b) -> p a b", a=shape[0])
                    v16 = t16[:, 0:ne].rearrange("p (a b) -> p a b", a=shape[0])
                DMA(P, 'sp', v32, src, f"st32_{k}", [], [b32])
                CP(P, e, t16[:, 0:ne], t32[:, 0:ne], [b32], [b16])
                DMA(P, 'sp', dst, v16, f"st16_{k}", [b16], [DB(dbuf_name)])
        return pull

    def conv_ensure(upto):
        if conv_pos[0] >= upto:
            return
        P.fence()
        P.sb_off = P.sb_base
        pull = conv_setup(3, ('dve', 'act', 'dve', 'pool'))
        pull(upto - conv_pos[0])

    def tile_share(total, tt):
        a = total * (tt * (tt + 1) // 2) // 36
        b = total * ((tt + 1) * (tt + 2) // 2) // 36
        return b - a

    def xT_tile_ap(tt):
        return xT_d[:, :, tt * T:(tt + 1) * T].rearrange("c p t -> p c t")

    def alloc_chunks(n, dtype, name):
        t_ = P.sbt([128, n, T], dtype)
        return [P.buf(t_[:, c, :], f"{name}{c}") for c in range(n)], t_

    def norm_tile(src_aps, src_bufs, nch, gcol, out_aps, out_bufs, ones_idx, bank, sqs, rstd, eng='dve'):
        pb, pt = bank
        for c in range(nch):
            sb_, st_ = sqs[c % 2]
            ACT(P, st_[:], src_aps[c], AF.Square, [src_bufs[c]], [sb_])
            MM(P, pt[:], cb(ones_idx), st_[:], c == 0, c == nch - 1, [cbf_b, sb_], [pb])
        rb, rt_ = rstd
        ACT(P, rt_[:], pt[:], AF.Ln, [pb, bias_b], [rb], bias=eps_ap)
        ACT(P, rt_[:], rt_[:], AF.Exp, [rb], [rb], scale=-0.5)
        for c in range(nch):
            STT(P, eng, out_aps[c], src_aps[c], gains_t[:, gcol + c:gcol + c + 1], rt_[:], ALU.mult, ALU.mult,
                [src_bufs[c], gains_b, rb], [out_bufs[c]])

    def phase_ffn(l, j, first, last):
        P.fence()
        P.sb_off = P.sb_base
        gcol = (0 if j == 0 else 32) + 8 * l
        xts = [alloc_chunks(8, F32, f"xt{s}") for s in range(2)]
        hts = [alloc_chunks(8, BF16, f"ht{s}") for s in range(2)]
        HT, HT_t = alloc_chunks(NFC, BF16, "HT")
        sqs = [P.sbuf([128, T], BF16, f"sq{s}") for s in range(2)]
        rstds = [P.sbuf([128, T], F32, f"rstd{s}") for s in range(2)]
        sgs = [P.sbuf([128, T], F32, f"sg{s}") for s in range(2)]
        wgu = [P.sbuf([128, 2, 8, 256], BF16, f"wgu{s}") for s in range(3)]
        wd = [P.sbuf([128, 2, D], BF16, f"wd{s}") for s in range(3)]
        if first:
            xtok = [P.sbuf([128, D], F32, f"xtok{s}") for s in range(2)]
        if last:
            yT, yT_t = alloc_chunks(8, F32, "yT")
            otok = [P.sbuf([128, D], F32, f"otok{s}") for s in range(2)]
        if (l, j) == (0, 0):
            cpull, cper = conv_setup(2, ('pool',)), 2
            ctgt = conv_mark['A_0']
        else:
            cpull, cper, ctgt = None, 0, 0
        psG = [ps[0], ps[1]]
        psU = [ps[2], ps[3]]
        psY = [ps[4], ps[5]]
        psN = ps[6]
        psX = [ps[6], ps[7]]
        gu_n = [0]
        d_n = [0]

        def load_gu(n):
            if n >= NT * 11:
                return
            g_ = n % 11
            b_, t_ = wgu[n % 3]
            DMA(P, 'sp', t_[:], WGU[l][j][g_], f"wgu{n % 3}", [DB(f"wgu{l}{j}_{g_}")], [b_])

        def load_d(n):
            if n >= NT * 11:
                return
            g_ = n % 11
            b_, t_ = wd[n % 3]
            DMA(P, 'sp', t_[:], WD[l][j][g_], f"wd{n % 3}", [DB(f"wd{l}{j}_{g_}")], [b_])

        def load_x(tt):
            bufs, t_ = xts[tt % 2]
            if not first:
                for hf in range(2):
                    DMA(P, 'sp', t_[:, hf * 4:(hf + 1) * 4, :], xT_tile_ap(tt)[:, hf * 4:(hf + 1) * 4, :], f"xt{tt % 2}_{hf}",
                        [DB('xT')], bufs[hf * 4:(hf + 1) * 4])
            else:
                for s in range(4):
                    xb, xt_ = xtok[s % 2]
                    r0 = tt * T + s * 128
                    DMA(P, 'sp', xt_[:], x_in[r0:r0 + 128, :], f"xtok{s % 2}", [], [xb])
                    for hb in range(2):
                        pb, pt = psX[hb]
                        for c4 in range(4):
                            c = hb * 4 + c4
                            TR(P, pt[:, c4 * 128:(c4 + 1) * 128], xt_[:, c * 128:(c + 1) * 128], identf_t[:],
                               [xb, identf_b], [pb])
                        CP(P, 'dve', t_[:, hb * 4:(hb + 1) * 4, s * 128:(s + 1) * 128],
                           pt[:].rearrange("p (c t) -> p c t", c=4), [pb], bufs[hb * 4:(hb + 1) * 4])

        def do_norm(tt):
            bufs, t_ = xts[tt % 2]
            hb, ht_ = hts[tt % 2]
            norm_tile([t_[:, c, :] for c in range(8)], bufs, 8, gcol, [ht_[:, c, :] for c in range(8)], hb,
                      1, psN, sqs, rstds[tt % 2])

        def gateup(tt):
            hb, ht_ = hts[tt % 2]
            for g_ in range(11):
                n = tt * 11 + g_
                load_gu(n + 2)
                wb, wt = wgu[n % 3]
                for fi in range(2):
                    fc = 2 * g_ + fi
                    for m, bank in enumerate((psG[fc % 2], psU[fc % 2])):
                        pb, pt = bank
                        for kc in range(8):
                            MM(P, pt[:], wt[:, m, kc, fi * 128:(fi + 1) * 128], ht_[:, kc, :], kc == 0, kc == 7,
                               [wb, hb[kc]], [pb])
                    sb_, st_ = sgs[fc % 2]
                    ACT(P, st_[:], psG[fc % 2][1][:], AF.Silu, [psG[fc % 2][0]], [sb_])
                    TT(P, 'dve', HT_t[:, fc, :], st_[:], psU[fc % 2][1][:], ALU.mult, [sb_, psU[fc % 2][0]], [HT[fc]])

        def down(tt):
            bufs, t_ = xts[tt % 2]
            for g_ in range(11):
                n = tt * 11 + g_
                load_d(n + 2)
                wb, wt = wd[n % 3]
                for fi in range(2):
                    fc = 2 * g_ + fi
                    for dmc in range(8):
                        pb, pt = ps[dmc]
                        MM(P, pt[:], wt[:, fi, dmc * 128:(dmc + 1) * 128], HT_t[:, fc, :], fc == 0, fc == NFC - 1,
                           [wb, HT[fc]], [pb])
            for dmc in range(8):
                pb, pt = ps[dmc]
                STT(P, 'dve', t_[:, dmc, :], pt[:], 0.5, t_[:, dmc, :], ALU.mult, ALU.add, [pb, bufs[dmc]], [bufs[dmc]])

        def store(tt):
            bufs, t_ = xts[tt % 2]
            if not last:
                for hf in range(2):
                    DMA(P, 'sp', xT_tile_ap(tt)[:, hf * 4:(hf + 1) * 4, :], t_[:, hf * 4:(hf + 1) * 4, :], f"xt{tt % 2}_{hf}",
                        bufs[hf * 4:(hf + 1) * 4], [DB('xT')])
            else:
                norm_tile([t_[:, c, :] for c in range(8)], bufs, 8, 48, [yT_t[:, c, :] for c in range(8)], yT,
                          1, psN, sqs, rstds[tt % 2])
                for s in range(4):
                    ob, ot = otok[s % 2]
                    for hb_ in range(2):
                        pb, pt = psX[hb_]
                        for c4 in range(4):
                            c = hb_ * 4 + c4
                            TR(P, pt[:, c4 * 128:(c4 + 1) * 128], yT_t[:, c, s * 128:(s + 1) * 128], identf_t[:],
                               [yT[c], identf_b], [pb])
                        CP(P, 'dve', ot[:, hb_ * 512:(hb_ + 1) * 512], pt[:], [pb], [ob])
                    r0 = tt * T + s * 128
                    DMA(P, 'sp', out_d[r0:r0 + 128, :], ot[:], f"otok{s % 2}", [ob], [DB('out')])

        load_gu(0)
        load_gu(1)
        load_d(0)
        load_d(1)
        load_x(0)
        do_norm(0)
        for tt in range(NT):
            if tt + 1 < NT:
                load_x(tt + 1)
            if cpull is not None:
                cpull(max(0, min(cper, ctgt - conv_pos[0])))
            gateup(tt)
            if tt + 1 < NT:
                do_norm(tt + 1)
            down(tt)
            store(tt)

    def lam_setup(l):
        li = 0.8 - 0.6 * math.exp(-0.3 * l)
        lb, lt = P.sbuf([128, 4, 64], F32, 'laml')
        pr_b, pr_t = P.sbuf([128, 2, 64], F32, 'lamp')
        sm_b, sm_t = P.sbuf([128, 2], F32, 'lams')
        DMA(P, 'sp', lt[:], lam_d[:, l, :, :], 'laml', [], [lb])
        TT(P, 'dve', pr_t[:, 0, :], lt[:, 0, :], lt[:, 1, :], ALU.mult, [lb], [pr_b])
        TT(P, 'dve', pr_t[:, 1, :], lt[:, 2, :], lt[:, 3, :], ALU.mult, [lb], [pr_b])
        P.op('dve', lambda g: g.reduce_sum(out=sm_t[:], in_=pr_t[:], axis=mybir.AxisListType.X), [pr_b], [sm_b])
        ACT(P, sm_t[:], sm_t[:], AF.Exp, [sm_b], [sm_b])
        TT(P, 'dve', lamv_t[:, 2 * l:2 * l + 1], sm_t[:, 1:2], sm_t[:, 0:1], ALU.subtract, [sm_b], [lamv_b])
        TS(P, 'dve', lamv_t[:, 2 * l:2 * l + 1], lamv_t[:, 2 * l:2 * l + 1], -li, ALU.add, [lamv_b], [lamv_b])
        TS(P, 'dve', lamv_t[:, 4 + l:5 + l], gains_t[:, 56 + l:57 + l], 1.0 - li, ALU.mult, [gains_b, lamv_b], [lamv_b])

    def rope_chunk(src_bank, rows, pm_idx, ppbank, qb, cs_t, cs_b, t1, t2, out_ap, out_bufs):
        pb, pt = src_bank
        qbb, qbt = qb
        sub = dbg.get('sub', 9)
        CP(P, 'act', qbt[0:rows, :], pt[0:rows, :], [pb], [qbb])
        ppb, ppt = ppbank
        if sub < 2:
            return
        MM(P, ppt[0:rows, :], cb(pm_idx, rows, rows), qbt[0:rows, :], True, True, [cbf_b, qbb], [ppb])
        t1b, t1t = t1
        t2b, t2t = t2
        if sub < 3:
            return
        TT(P, 'dve', t1t[0:rows, :], pt[0:rows, :], cs_t[0:rows, 0, :], ALU.mult, [pb, cs_b], [t1b])
        TT(P, 'dve', t2t[0:rows, :], ppt[0:rows, :], cs_t[0:rows, 1, :], ALU.mult, [ppb, cs_b], [t2b])
        TT(P, dbg.get('addeng', 'pool'), out_ap, t1t[0:rows, :], t2t[0:rows, :], ALU.add, [t1b, t2b], out_bufs)

    def phase_A(l):
        P.fence()
        P.sb_off = P.sb_base
        lam_setup(l)
        xt, xt_t = alloc_chunks(8, F32, "xt")
        ht, ht_t = alloc_chunks(8, BF16, "ht")
        sqs = [P.sbuf([128, T], BF16, f"sq{s}") for s in range(2)]
        rstd = P.sbuf([128, T], F32, "rstd")
        _wb, win_t = P.sbuf([128, 8, 1536], BF16, "winA")
        win_p = [P.buf(None, f"winA{i}") for i in range(4)]
        KT = [P.sbuf([128, S], BF16, f"KT{h}") for h in range(4)]
        V_b, V_t = P.sbuf([128, 32, 512], BF16, "V")
        QT = [[P.sbuf([128, T], BF16, f"QT{s}{h}") for h in range(4)] for s in range(2)]
        qbs = [P.sbuf([128, T], BF16, f"qb{s}") for s in range(2)]
        cs = [P.sbuf([128, 2, T], F32, f"cs{s}") for s in range(2)]
        t1s = [P.sbuf([128, T], F32, f"t1{s}") for s in range(2)]
        t2s = [P.sbuf([128, T], F32, f"t2{s}") for s in range(2)]
        PT = [P.sbuf([128, 2, T], BF16, f"PT{s}") for s in range(3)]
        mk_b, mk_t = P.sbuf([128, 4, T], BF16, "mask")
        cpull = conv_setup(2, ('pool',), 2048)
        ctarget = conv_mark['F2_0'] if l == 0 else min(len(conv_items), conv_pos[0] + 30)
        ctotal = max(0, ctarget - conv_pos[0])
        o1s = [P.sbuf([128, T], F32, f"o1{s}") for s in range(2)]
        rsb = [P.sbuf([128, T], F32, f"rs{s}") for s in range(4)]
        orstd = P.sbuf([128, T], F32, "orstd")
        mgs = [P.sbuf([128, T], BF16, f"mg{s}") for s in range(2)]
        sacc = [P.sbuf([128, T], F32, f"sacc{s}") for s in range(2)]
        saccb = [P.sbuf([128, T], BF16, f"saccb{s}") for s in range(2)]
        print("phase A sbuf bytes", P.sb_off)
        for hf in range(4):
            DMA(P, 'sp', win_t[:, hf * 2:(hf + 1) * 2, :], WIN[l][:, hf * 2:(hf + 1) * 2, 0:1536], f"winA{hf}", [DB(f"win{l}")], [win_p[hf]])
        DMA(P, 'sp', mk_t[:], masks_d[:, 0:4, :], "mask", [], [mk_b])
        gcol = 16 + 8 * l
        scale = 0.125
        cnt = [0]
        lvl = dbg.get('lvl', 9)
        for tt in range(dbg.get('tiles', NT)):
            c0 = tt * T
            for hf in range(2):
                DMA(P, 'sp', xt_t[:, hf * 4:(hf + 1) * 4, :], xT_tile_ap(tt)[:, hf * 4:(hf + 1) * 4, :], f"xtA{hf}",
                    [DB('xT')], xt[hf * 4:(hf + 1) * 4])
            csb, cst = cs[tt % 2]
            DMA(P, 'sp', cst[:], rope_d[0:2, :, c0:c0 + T].rearrange("a p t -> p a t"), f"cs{tt % 2}", [], [csb])
            norm_tile([xt_t[:, c, :] for c in range(8)], xt, 8, gcol, [ht_t[:, c, :] for c in range(8)], ht,
                      1, ps[6], sqs, rstd)
            if lvl < 1:
                continue
            pend = []
            for kind in range(2):
                for h in range(4):
                    n = cnt[0]
                    cnt[0] += 1
                    bank = ps[6 + n % 2]
                    col = kind * 512 + h * 128
                    for kc in range(8):
                        MM(P, bank[1][:], win_t[:, kc, col:col + 128], ht_t[:, kc, :], kc == 0, kc == 7,
                           [win_p[kc // 2], ht[kc]], [bank[0]])
                    if kind == 0:
                        ob, ot = QT[tt % 2][h]
                        oap = ot[:]
                    else:
                        ob, ot = KT[h]
                        oap = ot[:, c0:c0 + T]
                    if pend:
                        rope_chunk(*pend.pop())
                    pend.append((bank, 128, 7, ps[n % 2], qbs[n % 2], cst, csb, t1s[n % 2], t2s[n % 2], oap, [ob]))
            rope_chunk(*pend.pop())
            if lvl < 2:
                continue
            for s in range(4):
                n = cnt[0]
                cnt[0] += 1
                bank = ps[6 + n % 2]
                for kc in range(8):
                    MM(P, bank[1][:], ht_t[:, kc, s * 128:(s + 1) * 128], win_t[:, kc, 1024:1536], kc == 0, kc == 7,
                       [win_p[kc // 2], ht[kc]], [bank[0]])
                CP(P, 'act' if s % 2 == 0 else 'dve', V_t[:, tt * 4 + s, :], bank[1][:], [bank[0]], [V_b])
            if lvl < 3:
                continue
            cpull(tile_share(ctotal, tt))
            nkb = 4 * (tt + 1)
            pend = [[], [], [], []]
            for h in range(4):
                qb_, qt_ = QT[tt % 2][h]
                kb_, kt_ = KT[h]
                Ob = [ps[4], ps[5]]
                Sm0 = ps[6]
                sab, sat = sacc[h % 2]

                def qk(kb):
                    slot = kb % 2
                    diag = kb >= 4 * tt
                    q0 = 128 * (kb - 4 * tt) if diag else 0
                    for v in range(2):
                        r0 = v * 64
                        bank = ps[2 * slot + v]
                        MM(P, bank[1][:, q0:], kt_[r0:r0 + 64, kb * 128:(kb + 1) * 128], qt_[r0:r0 + 64, q0:], True, not diag,
                           [kb_, qb_], [bank[0]])
                    if diag:
                        for v in range(2):
                            bank = ps[2 * slot + v]
                            MM(P, bank[1][:, q0:], cb(0), mk_t[:, kb - 4 * tt, q0:], False, True, [cbf_b, mk_b], [bank[0]])

                qk(0)
                if nkb > 1:
                    qk(1)
                for kb in range(nkb):
                    slot = kb % 2
                    q0 = 128 * (kb - 4 * tt) if kb >= 4 * tt else 0
                    pb_, pt_ = PT[kb % 3]
                    ACT(P, pt_[:, :, q0:], ps_all[:, 2 * slot:2 * slot + 2, q0:], AF.Exp, [ps[2 * slot][0], ps[2 * slot + 1][0]], [pb_],
                        scale=scale)
                    if kb + 2 < nkb:
                        qk(kb + 2)
                    for v in range(2):
                        MM(P, Ob[v][1][:, q0:], V_t[:, kb, h * 128:(h + 1) * 128], pt_[:, v, q0:], kb == 0, kb == nkb - 1,
                           [V_b, pb_], [Ob[v][0]])
                    MM(P, Sm0[1][:, q0:], cb(4), pt_[:, 0, q0:], kb == 0, kb == nkb - 1, [cbf_b, pb_], [Sm0[0]])
                    if kb == 0:
                        CP(P, 'dve', sat[:], pt_[:, 1, :], [pb_], [sab])
                    else:
                        TT(P, 'dve', sat[:, q0:], sat[:, q0:], pt_[:, 1, q0:], ALU.add, [sab, pb_], [sab])
                    for st_i, kq in enumerate((1, 2, 3, 4)):
                        if kb == min(kq, nkb - 1) and pend[st_i]:
                            pend[st_i].pop(0)()
                if lvl < 4:
                    continue
                o1b, o1t = o1s[h % 2]
                t2b, t2t = t2s[h % 2]
                r0b, r0t = rsb[h % 2]
                r1b, r1t = rsb[2 + h % 2]
                sbb, sbt = saccb[h % 2]
                ACT(P, r0t[:], Sm0[1][:], AF.Ln, [Sm0[0]], [r0b])
                ACT(P, r0t[:], r0t[:], AF.Exp, [r0b], [r0b], scale=-1.0)
                CP(P, 'act', t2t[:], Ob[1][1][:], [Ob[1][0]], [t2b])
                TT(P, 'dve', o1t[:], Ob[0][1][:], r0t[:], ALU.mult, [Ob[0][0], r0b], [o1b])
                CP(P, 'dve', sbt[:], sat[:], [sab], [sbb])

                def S1(h=h, sbb=sbb, sbt=sbt):
                    nb_ = ps[7]
                    MM(P, nb_[1][:], cb(4), sbt[:], True, True, [cbf_b, sbb], [nb_[0]])

                def S2(h=h, o1b=o1b, o1t=o1t, t2b=t2b, t2t=t2t, r1b=r1b, r1t=r1t):
                    nb_ = ps[7]
                    ACT(P, r1t[:], nb_[1][:], AF.Ln, [nb_[0]], [r1b])
                    ACT(P, r1t[:], r1t[:], AF.Exp, [r1b], [r1b], scale=-1.0)
                    TT(P, 'dve', t2t[:], t2t[:], r1t[:], ALU.mult, [t2b, r1b], [t2b])
                    STT(P, 'dve', o1t[:], t2t[:], lamv_t[:, 2 * l:2 * l + 1], o1t[:], ALU.mult, ALU.add,
                        [t2b, lamv_b, o1b], [o1b])

                def S3(h=h, o1b=o1b, o1t=o1t):
                    nb_ = ps[7]
                    sb_, st_ = sqs[h % 2]
                    ACT(P, st_[:], o1t[:], AF.Square, [o1b], [sb_])
                    MM(P, nb_[1][:], cb(3), st_[:], True, True, [cbf_b, sb_], [nb_[0]])

                def S4(h=h, o1b=o1b, o1t=o1t):
                    nb_ = ps[7]
                    mb, mt = mgs[h % 2]
                    orb, ort = orstd
                    ACT(P, ort[:], nb_[1][:], AF.Ln, [nb_[0], bias_b], [orb], bias=eps_ap)
                    ACT(P, ort[:], ort[:], AF.Exp, [orb], [orb], scale=-0.5)
                    STT(P, 'dve', mt[:], o1t[:], lamv_t[:, 4 + l:5 + l], ort[:], ALU.mult, ALU.mult,
                        [o1b, lamv_b, orb], [mb])
                    DMA(P, 'sp', mg_d[h, :, c0:c0 + T], mt[:], f"mg{h % 2}", [mb], [DB('mg')])
                for st_i, f in enumerate((S1, S2, S3, S4)):
                    pend[st_i].append(f)
            for st_i in range(4):
                while pend[st_i]:
                    pend[st_i].pop(0)()
        cpull(max(0, ctarget - conv_pos[0]))

    def phase_B(l):
        P.fence()
        P.sb_off = P.sb_base
        xt, xt_t = alloc_chunks(8, F32, "xt")
        ht, ht_t = alloc_chunks(8, BF16, "ht")
        sqs = [P.sbuf([128, T], BF16, f"sq{s}") for s in range(2)]
        rstd = P.sbuf([128, T], F32, "rstd")
        rstd2 = P.sbuf([128, T], F32, "rstd2")
        rstd3 = P.sbuf([128, T], F32, "rstd3")
        _wb, win_t = P.sbuf([128, 8, 416], BF16, "winB")
        win_p = [P.buf(None, f"winB{i}") for i in range(2)]
        wuq_b, wuq_t = P.sbuf([128, 2, 384], BF16, "wuq")
        wukv_b, wukv_t = P.sbuf([128, 512], BF16, "wukv")
        wkp_b, wkp_t = P.sbuf([128, 4, 96], BF16, "wkp")
        cqn, cqn_t = alloc_chunks(2, BF16, "cqn")
        ckvn_b, ckvn_t = P.sbuf([128, T], BF16, "ckvn")
        krr_b, krr_t = P.sbuf([128, T], BF16, "krr")
        KT = [P.sbuf([128, S], BF16, f"KT{h}") for h in range(4)]
        Vp_b, Vp_t = P.sbuf([128, 32, 4, 128], BF16, "Vp")
        QT = [[P.sbuf([128, T], BF16, f"QT{s}{h}") for h in range(4)] for s in range(2)]
        qbs = [P.sbuf([128, T], BF16, f"qb{s}") for s in range(2)]
        cs = [P.sbuf([128, 4, T], F32, f"cs{s}") for s in range(2)]
        t1s = [P.sbuf([128, T], F32, f"t1{s}") for s in range(2)]
        t2s = [P.sbuf([128, T], F32, f"t2{s}") for s in range(2)]
        PT = [P.sbuf([128, T], BF16, f"PT{s}") for s in range(4)]
        mk_b, mk_t = P.sbuf([128, 4, T], BF16, "mask")
        rsb = [P.sbuf([128, T], F32, f"rs{s}") for s in range(2)]
        mgs = [P.sbuf([128, T], BF16, f"mg{s}") for s in range(2)]
        for hf in range(2):
            DMA(P, 'sp', win_t[:, hf * 4:(hf + 1) * 4, :], WIN[l][:, hf * 4:(hf + 1) * 4, 1536:1952], f"winB{hf}", [DB(f"win{l}")], [win_p[hf]])
        DMA(P, 'sp', wuq_t[:], WUQ[l], "wuq", [DB(f"wm{l}")], [wuq_b])
        DMA(P, 'sp', wukv_t[:], WUKV[l], "wukv", [DB(f"wm{l}")], [wukv_b])
        DMA(P, 'sp', mk_t[:], masks_d[:, 0:4, :], "mask", [], [mk_b])
        P.op('pool', lambda g: g.memset(Vp_t[:], 0.0), [], [Vp_b])
        P.op('pool', lambda g: g.memset(wkp_t[:], 0.0), [], [wkp_b])
        cpull = conv_setup(2, ('pool',), 2048)
        ctotal = min(15 if l == 0 else 10, len(conv_items) - conv_pos[0])
        for h in range(4):
            CP(P, 'dve', wkp_t[:, h, 0:64], wukv_t[:, h * 128:h * 128 + 64], [wukv_b], [wkp_b])
        gcol = 16 + 8 * l
        scale = 96 ** -0.5
        cnt = [0]
        for tt in range(NT):
            c0 = tt * T
            for hf in range(2):
                DMA(P, 'sp', xt_t[:, hf * 4:(hf + 1) * 4, :], xT_tile_ap(tt)[:, hf * 4:(hf + 1) * 4, :], f"xtA{hf}",
                    [DB('xT')], xt[hf * 4:(hf + 1) * 4])
            csb, cst = cs[tt % 2]
            DMA(P, 'sp', cst[:], rope_d[2:6, :, c0:c0 + T].rearrange("a p t -> p a t"), f"cs{tt % 2}", [], [csb])
            norm_tile([xt_t[:, c, :] for c in range(8)], xt, 8, gcol, [ht_t[:, c, :] for c in range(8)], ht,
                      1, ps[6], sqs, rstd)
            for c in range(2):
                for kc in range(8):
                    MM(P, ps[4 + c][1][:], win_t[:, kc, c * 128:(c + 1) * 128], ht_t[:, kc, :], kc == 0, kc == 7,
                       [win_p[kc // 4], ht[kc]], [ps[4 + c][0]])
            norm_tile([ps[4][1][:], ps[5][1][:]], [ps[4][0], ps[5][0]], 2, 58 + 2 * l,
                      [cqn_t[:, 0, :], cqn_t[:, 1, :]], cqn, 2, ps[6], sqs, rstd2)
            for kc in range(8):
                MM(P, ps[7][1][:], win_t[:, kc, 256:384], ht_t[:, kc, :], kc == 0, kc == 7, [win_p[kc // 4], ht[kc]], [ps[7][0]])
            norm_tile([ps[7][1][:]], [ps[7][0]], 1, 62 + l, [ckvn_t[:]], [ckvn_b], 3, ps[6], sqs, rstd3)
            for kc in range(8):
                MM(P, ps[4][1][0:32, :], win_t[:, kc, 384:416], ht_t[:, kc, :], kc == 0, kc == 7, [win_p[kc // 4], ht[kc]], [ps[4][0]])
            cs_kr = cst[:, 2:4, :]
            rope_chunk(ps[4], 32, 9, ps[5], qbs[0], cs_kr, csb, t1s[0], t2s[0], krr_t[0:32, :], [krr_b])
            for h in range(4):
                n = cnt[0]
                cnt[0] += 1
                bank = ps[6 + n % 2]
                for kc in range(2):
                    MM(P, bank[1][0:96, :], wuq_t[:, kc, h * 96:(h + 1) * 96], cqn_t[:, kc, :], kc == 0, kc == 1,
                       [wuq_b, cqn[kc]], [bank[0]])
                qb_, qt_ = QT[tt % 2][h]
                rope_chunk(bank, 96, 8, ps[4 + n % 2], qbs[n % 2], cst[:, 0:2, :], csb, t1s[n % 2], t2s[n % 2],
                           qt_[0:96, :], [qb_])
                kbank = ps[n % 2]
                MM(P, kbank[1][0:96, :], wkp_t[:, h, :], ckvn_t[:], True, False, [wkp_b, ckvn_b], [kbank[0]])
                MM(P, kbank[1][0:96, :], cb(10, 32, 96), krr_t[0:32, :], False, True, [cbf_b, krr_b], [kbank[0]])
                kb_, kt_ = KT[h]
                CP(P, 'act', kt_[0:96, c0:c0 + T], kbank[1][0:96, :], [kbank[0]], [kb_])
            for s in range(4):
                n = cnt[0]
                cnt[0] += 1
                bank = ps[6 + n % 2]
                rhs = wukv_t[:].rearrange("p (h c) -> p h c", h=4)[:, :, 64:128]
                MM(P, bank[1][:, 0:256].rearrange("p (h c) -> p h c", h=4), ckvn_t[:, s * 128:(s + 1) * 128], rhs, True, True,
                   [wukv_b, ckvn_b], [bank[0]])
                for h in range(4):
                    CP(P, 'dve' if h % 2 == 0 else 'act', Vp_t[:, tt * 4 + s, h, (h % 2) * 64:(h % 2) * 64 + 64],
                       bank[1][:, h * 64:(h + 1) * 64], [bank[0]], [Vp_b])
            nkb = 4 * (tt + 1)
            cpull(tile_share(ctotal, tt))
            for pr in range(2):
                Ob, Ot = ps[4 + pr]
                Sb_, St_ = ps[6 + pr]
                steps = [(kb, hh) for kb in range(nkb) for hh in range(2)]

                def qk(i):
                    kb, hh = steps[i]
                    h = 2 * pr + hh
                    sbk = ps[i % 4]
                    diag = kb >= 4 * tt
                    MM(P, sbk[1][:], KT[h][1][0:96, kb * 128:(kb + 1) * 128], QT[tt % 2][h][1][0:96, :], True, not diag,
                       [KT[h][0], QT[tt % 2][h][0]], [sbk[0]])
                    if diag:
                        MM(P, sbk[1][:], cb(0), mk_t[:, kb - 4 * tt, :], False, True, [cbf_b, mk_b], [sbk[0]])

                qk(0)
                qk(1)
                for i in range(len(steps)):
                    if i + 2 < len(steps):
                        qk(i + 2)
                    kb, hh = steps[i]
                    h = 2 * pr + hh
                    sbk = ps[i % 4]
                    pb_, pt_ = PT[i % 4]
                    ACT(P, pt_[:], sbk[1][:], AF.Exp, [sbk[0]], [pb_], scale=scale)
                    MM(P, Ot[:], Vp_t[:, kb, h, :], pt_[:], i == 0, i == len(steps) - 1, [Vp_b, pb_], [Ob])
                    MM(P, St_[:], cb(11 + hh), pt_[:], i == 0, i == len(steps) - 1, [cbf_b, pb_], [Sb_])
                rb_, rt_ = rsb[pr]
                ACT(P, rt_[:], St_[:], AF.Ln, [Sb_], [rb_])
                ACT(P, rt_[:], rt_[:], AF.Exp, [rb_], [rb_], scale=-1.0)
                mb, mt = mgs[pr]
                TT(P, 'dve', mt[:], Ot[:], rt_[:], ALU.mult, [Ob, rb_], [mb])
                DMA(P, 'sp', mg_d[4 + pr, :, c0:c0 + T], mt[:], f"mg{pr}", [mb], [DB('mg')])

    def phase_C(l):
        P.fence()
        P.sb_off = P.sb_base
        xt, xt_t = alloc_chunks(8, F32, "xt")
        ht, ht_t = alloc_chunks(8, BF16, "ht")
        sqs = [P.sbuf([128, T], BF16, f"sq{s}") for s in range(2)]
        rstd = P.sbuf([128, T], F32, "rstd")
        _wb, win_t = P.sbuf([128, 8, 768], BF16, "winC")
        win_p = [P.buf(None, f"winC{i}") for i in range(2)]
        KT = [P.sbuf([128, S], BF16, f"KT{p}") for p in range(2)]
        Vp_b, Vp_t = P.sbuf([128, 32, 4, 128], BF16, "Vp")
        QT = [[P.sbuf([128, T], BF16, f"QT{s}{p}") for p in range(2)] for s in range(2)]
        mk_b, mk_t = P.sbuf([128, 4, T], BF16, "mask")
        e32 = [P.sbuf([128, 2, T], F32, f"e32{s}") for s in range(2)]
        spb = [P.sbuf([128, 2, T], BF16, f"sp{s}") for s in range(3)]
        raccs = [P.sbuf([128, 2, T], BF16, f"racc{s}") for s in range(2)]
        PT = [P.sbuf([128, 2, T], BF16, f"PT{s}") for s in range(3)]
        mgs = [P.sbuf([128, T], BF16, f"mg{s}") for s in range(2)]
        for hf in range(2):
            DMA(P, 'sp', win_t[:, hf * 4:(hf + 1) * 4, :], WIN[l][:, hf * 4:(hf + 1) * 4, 1952:2720], f"winC{hf}", [DB(f"win{l}")], [win_p[hf]])
        DMA(P, 'sp', mk_t[:], masks_d[:, 4:8, :], "mask", [], [mk_b])
        P.op('pool', lambda g: g.memset(Vp_t[:], 0.0), [], [Vp_b])
        cpull = conv_setup(2, ('pool',))
        ctarget = conv_mark['A_1'] if l == 0 else len(conv_items)
        ctotal = max(0, ctarget - conv_pos[0])
        gcol = 16 + 8 * l
        scale = 0.125
        cnt = [0]
        for tt in range(NT):
            c0 = tt * T
            for hf in range(2):
                DMA(P, 'sp', xt_t[:, hf * 4:(hf + 1) * 4, :], xT_tile_ap(tt)[:, hf * 4:(hf + 1) * 4, :], f"xtA{hf}",
                    [DB('xT')], xt[hf * 4:(hf + 1) * 4])
            norm_tile([xt_t[:, c, :] for c in range(8)], xt, 8, gcol, [ht_t[:, c, :] for c in range(8)], ht,
                      1, ps[6], sqs, rstd)
            for kind in range(2):
                for pr in range(2):
                    n = cnt[0]
                    cnt[0] += 1
                    bank = ps[6 + n % 2]
                    col = kind * 256 + pr * 128
                    for kc in range(8):
                        MM(P, bank[1][:], win_t[:, kc, col:col + 128], ht_t[:, kc, :], kc == 0, kc == 7,
                           [win_p[kc // 4], ht[kc]], [bank[0]])
                    if kind == 0:
                        CP(P, 'act', QT[tt % 2][pr][1][:], bank[1][:], [bank[0]], [QT[tt % 2][pr][0]])
                    else:
                        CP(P, 'dve', KT[pr][1][:, c0:c0 + T], bank[1][:], [bank[0]], [KT[pr][0]])
            for s in range(4):
                n = cnt[0]
                cnt[0] += 1
                bank = ps[6 + n % 2]
                for kc in range(8):
                    MM(P, bank[1][:, 0:256], ht_t[:, kc, s * 128:(s + 1) * 128], win_t[:, kc, 512:768], kc == 0, kc == 7,
                       [win_p[kc // 4], ht[kc]], [bank[0]])
                for h in range(4):
                    CP(P, 'dve' if h % 2 == 0 else 'act', Vp_t[:, tt * 4 + s, h, (h % 2) * 64:(h % 2) * 64 + 64],
                       bank[1][:, h * 64:(h + 1) * 64], [bank[0]], [Vp_b])
            nkb = 4 * (tt + 1)
            cpull(tile_share(ctotal, tt))
            for pr in range(2):
                Ob, Ot = ps[6 + pr]
                P.op('dve', lambda g, o=raccs[0][1][:]: g.memset(o, 0.0), [], [raccs[0][0]])
                kbs = list(range(nkb - 1, -1, -1))
                ns = len(kbs)

                def q0_of(i):
                    kb = kbs[i]
                    return 128 * (kb - 4 * tt) if (kb >= 4 * tt and i > 0) else 0

                def st1(i):
                    kb = kbs[i]
                    slot = i % 3
                    q0 = q0_of(i)
                    diag = kb >= 4 * tt
                    for hh in range(2):
                        r0 = hh * 64
                        zb = ps[2 * slot + hh]
                        MM(P, zb[1][:, q0:], KT[pr][1][r0:r0 + 64, kb * 128:(kb + 1) * 128], QT[tt % 2][pr][1][r0:r0 + 64, q0:],
                           True, not diag, [KT[pr][0], QT[tt % 2][pr][0]], [zb[0]])
                    if diag:
                        for hh in range(2):
                            zb = ps[2 * slot + hh]
                            MM(P, zb[1][:, q0:], cb(0), mk_t[:, kb - 4 * tt, q0:], False, True, [cbf_b, mk_b], [zb[0]])
                    eb, et = e32[i % 2]
                    zbufs = [ps[2 * slot][0], ps[2 * slot + 1][0]]
                    ACT(P, et[:, :, q0:], ps_all[:, 2 * slot:2 * slot + 2, q0:], AF.Exp, zbufs, [eb], scale=scale)
                    sb_, st_ = spb[i % 3]
                    ACT(P, st_[:, :, q0:], et[:, :, q0:], AF.Ln, [eb, bias_b], [sb_], bias=one_ap)

                def st2(i):
                    kb = kbs[i]
                    slot = i % 3
                    q0 = q0_of(i)
                    diag = kb >= 4 * tt
                    sb_, st_ = spb[i % 3]
                    for hh in range(2):
                        r0 = hh * 64
                        zb = ps[2 * slot + hh]
                        MM(P, zb[1][:, q0:], KT[pr][1][r0:r0 + 64, kb * 128:(kb + 1) * 128], QT[tt % 2][pr][1][r0:r0 + 64, q0:],
                           True, False, [KT[pr][0], QT[tt % 2][pr][0]], [zb[0]])
                    for hh in range(2):
                        zb = ps[2 * slot + hh]
                        if diag:
                            MM(P, zb[1][:, q0:], cb(0), mk_t[:, kb - 4 * tt, q0:], False, False, [cbf_b, mk_b], [zb[0]])
                        MM(P, zb[1][:, q0:], cb(5), st_[:, hh, q0:], False, False, [cbf_b, sb_], [zb[0]])
                        MM(P, zb[1][:, q0:], cb(6), raccs[i % 2][1][:, hh, q0:], False, True, [cbf_b, raccs[i % 2][0]], [zb[0]])
                    pb_, pt_ = PT[i % 3]
                    zbufs = [ps[2 * slot][0], ps[2 * slot + 1][0]]
                    ACT(P, pt_[:, :, q0:], ps_all[:, 2 * slot:2 * slot + 2, q0:], AF.Exp, zbufs, [pb_], scale=scale)
                    TT(P, 'dve', raccs[(i + 1) % 2][1][:, :, q0:], raccs[i % 2][1][:, :, q0:], st_[:, :, q0:], ALU.add,
                       [raccs[i % 2][0], sb_], [raccs[(i + 1) % 2][0]])

                def st3(i):
                    kb = kbs[i]
                    q0 = q0_of(i)
                    pb_, pt_ = PT[i % 3]
                    for hh in range(2):
                        h = 2 * pr + hh
                        MM(P, Ot[:, q0:], Vp_t[:, kb, h, :], pt_[:, hh, q0:], i == 0 and hh == 0, i == ns - 1 and hh == 1,
                           [Vp_b, pb_], [Ob])

                st1(0)
                for i in range(ns):
                    if i + 1 < ns:
                        st1(i + 1)
                    st2(i)
                    if i >= 1:
                        st3(i - 1)
                st3(ns - 1)
                mb, mt = mgs[pr]
                CP(P, 'dve', mt[:], Ot[:], [Ob], [mb])
                DMA(P, 'sp', mg_d[6 + pr, :, c0:c0 + T], mt[:], f"mg{pr}", [mb], [DB('mg')])
        cpull(max(0, ctarget - conv_pos[0]))

    def phase_O(l):
        P.fence()
        P.sb_off = P.sb_base
        xts = [alloc_chunks(8, F32, f"xt{s}") for s in range(2)]
        mgt = [alloc_chunks(8, BF16, f"mgt{s}") for s in range(2)]
        _wb, wo_t = P.sbuf([128, 8, D], BF16, "wout")
        wo_p = [P.buf(None, f"wout{i}") for i in range(2)]
        for hf in range(2):
            DMA(P, 'sp', wo_t[:, hf * 4:(hf + 1) * 4, :], WOUT[l][:, hf * 4:(hf + 1) * 4, :], f"wout{hf}", [DB(f"wo{l}")], [wo_p[hf]])
        for tt in range(NT):
            c0 = tt * T
            xb, xt_t = xts[tt % 2]
            mb, mt_t = mgt[tt % 2]
            for hf in range(2):
                DMA(P, 'sp', xt_t[:, hf * 4:(hf + 1) * 4, :], xT_tile_ap(tt)[:, hf * 4:(hf + 1) * 4, :], f"xt{tt % 2}_{hf}",
                    [DB('xT')], xb[hf * 4:(hf + 1) * 4])
                DMA(P, 'sp', mt_t[:, hf * 4:(hf + 1) * 4, :], mg_d[hf * 4:(hf + 1) * 4, :, c0:c0 + T].rearrange("c p t -> p c t"),
                    f"mgt{tt % 2}_{hf}", [DB('mg')], mb[hf * 4:(hf + 1) * 4])
            for dmc in range(8):
                pb, pt = ps[dmc % 4]
                for c in range(8):
                    MM(P, pt[:], wo_t[:, c, dmc * 128:(dmc + 1) * 128], mt_t[:, c, :], c == 0, c == 7, [wo_p[c // 4], mb[c]], [pb])
                TT(P, 'dve', xt_t[:, dmc, :], pt[:], xt_t[:, dmc, :], ALU.add, [pb, xb[dmc]], [xb[dmc]])
            for hf in range(2):
                DMA(P, 'sp', xT_tile_ap(tt)[:, hf * 4:(hf + 1) * 4, :], xt_t[:, hf * 4:(hf + 1) * 4, :], f"xt{tt % 2}_{hf}",
                    xb[hf * 4:(hf + 1) * 4], [DB('xT')])

    nphase = len([p for p in phases if p != 'conv'])
    seen_ffn = 0
    for ph in phases:
        if ph == 'conv':
            continue
        conv_ensure(conv_mark[ph])
        if ph.startswith('F'):
            j = int(ph[1]) - 1
            l = int(ph[3])
            phase_ffn(l, j, first=(ph == 'F1_0'), last=(ph == 'F2_1'))
        elif ph.startswith('A'):
            phase_A(int(ph[2]))
        elif ph.startswith('B'):
            phase_B(int(ph[2]))
        elif ph.startswith('C'):
            phase_C(int(ph[2]))
        elif ph.startswith('O'):
            phase_O(int(ph[2]))
    P.fence()
    if dump is not None:
        dd = dt("dump", [8, 128, S], F32 if dump == 'xT' else BF16, kind="ExternalOutput").ap()
        src = xT_d if dump == 'xT' else mg_d
        for c in range(8):
            DMA(P, 'sp', dd[c], src[c], "dump", [], [DB('dump')])
        P.fence()
    P.emit()
    return nc


_CONSTS = None


def make_in_maps(inputs):
    global _CONSTS
    if _CONSTS is None:
        _CONSTS = make_consts()
    c = _CONSTS
    inp = {k: np.asarray(v) for k, v in inputs.items()}
    shared = {
        "ffn1_w_gate": inp['ffn1_w_gate'], "ffn2_w_gate": inp['ffn2_w_gate'],
        "ffn1_w_up": inp['ffn1_w_up'], "ffn2_w_up": inp['ffn2_w_up'],
        "ffn1_w_down": inp['ffn1_w_down'], "ffn2_w_down": inp['ffn2_w_down'],
        "w_in": inp['w_in'], "mla_w_uq": inp['mla_w_uq'], "mla_w_ukv": inp['mla_w_ukv'], "w_out": inp['w_out'],
        "gains": pack_gains(inp), "lam": pack_lam(inp),
        "ident_f": c['ident_f'], "cbf": c['cbf'], "masks": c['masks'], "rope": c['rope'],
    }
    maps = []
    for b in range(8):
        m = dict(shared)
        m["x"] = np.ascontiguousarray(inp['x'][b])
        maps.append(m)
    return maps


def kernel(**inputs):
    nc = build_program()
    maps = make_in_maps(inputs)
    res = run_bass_kernel_spmd(nc, maps, core_ids=list(range(8)))
    out = np.stack([np.asarray(r["out"]) for r in res.results], axis=0)
    return out.astype(np.float32)
```
